# Optimizing a Trainium2 kernel written in Bass

```python
import jax, jax.numpy as jnp
from jax import lax
import numpy as np

D_MODEL = 2048
BATCH = 16
SEQ = 256
DEPTH = 1
DEC_BATCH = 8
DEC_SEQ = 2048
PAST_LEN = 256

GRID_W = 64
N_DIR = 2
M_WIDTH = D_MODEL // 2
M_HEADS = 4
M_DK = M_WIDTH // M_HEADS
R_WIDTH = D_MODEL - M_WIDTH
R_N = 64
R_HEADS = R_WIDTH // R_N
LORA = 64
CONV_K = 3
CHUNK = 64
EPS = 1e-6
LNX_EPS = 64e-5
M_GATE_COLS = N_DIR * 2 * M_HEADS
SHIFT_COLS = 3 * R_WIDTH + 2 * N_DIR * LORA
IN_COLS = 5 * M_WIDTH + M_GATE_COLS + R_WIDTH + SHIFT_COLS

kernel_name = "hybrid_mlstm_rwkv7_diffusion_step"


def _split(a, sizes):
    return jnp.split(a, np.cumsum(sizes)[:-1].tolist(), axis=-1)


def rmsnorm(x, g):
    xf = x.astype(jnp.float32)
    y = xf * lax.rsqrt(jnp.mean(xf * xf, axis=-1, keepdims=True) + EPS)
    return (y * g.astype(jnp.float32)).astype(x.dtype)


def centred_conv(p, w, b):
    pad = CONV_K // 2
    T = p.shape[1]
    pp = jnp.pad(p, ((0, 0), (pad, pad), (0, 0)))
    return sum(pp[:, j:j + T] * w[j] for j in range(CONV_K)) + b


def shift_seq(p):
    B, T, C = p.shape
    p4 = p.reshape(B, T, C // 4, 4)
    prev = jnp.pad(p4, ((0, 0), (1, 0), (0, 0), (0, 0)))[:, :T]
    nxt = jnp.pad(p4, ((0, 0), (0, 1), (0, 0), (0, 0)))[:, 1:]
    sel = (jnp.arange(4) % 2) == 0
    return jnp.where(sel, prev, nxt).reshape(B, T, C)


def shift_grid(p):
    B, T, C = p.shape
    rows = T // GRID_W
    g = p.reshape(B, rows, GRID_W, C // 4, 4)
    z2 = ((0, 0), (0, 0), (1, 0), (0, 0))
    left = jnp.pad(g[..., 0], z2)[:, :, :GRID_W]
    right = jnp.pad(g[..., 1], ((0, 0), (0, 0), (0, 1), (0, 0)))[:, :, 1:]
    up = jnp.pad(g[..., 2], ((0, 0), (1, 0), (0, 0), (0, 0)))[:, :rows]
    down = jnp.pad(g[..., 3], ((0, 0), (0, 1), (0, 0), (0, 0)))[:, 1:]
    return jnp.stack([left, right, up, down], axis=-1).reshape(B, T, C)


def mlstm_scan(q, k, v, log_i, log_f, C0, n0, m0):
    B, H, T, D = q.shape
    nc = T // CHUNK

    def to_chunks(a):
        return jnp.moveaxis(a.reshape(B, H, nc, CHUNK, *a.shape[3:]), 2, 0)

    tril = jnp.tril(jnp.ones((CHUNK, CHUNK), bool))

    def step(carry, xs):
        C, n, m = carry
        qc, kc, vc, ic, fc = xs
        b = jnp.cumsum(fc, axis=-1)
        dmat = jnp.where(tril, b[..., :, None] - b[..., None, :] + ic[..., None, :], -jnp.inf)
        inter = b + m[..., None]
        mt = jnp.maximum(inter, dmat.max(-1))
        A = jnp.exp(dmat - mt[..., None]) * jnp.einsum('bhtd,bhsd->bhts', qc, kc)
        s_in = jnp.exp(inter - mt)
        num = s_in[..., None] * jnp.einsum('bhtd,bhde->bhte', qc, C) + jnp.einsum('bhts,bhse->bhte', A, vc)
        den = s_in * jnp.einsum('bhtd,bhd->bht', qc, n) + A.sum(-1)
        h = num / jnp.maximum(jnp.abs(den), jnp.exp(-mt))[..., None]
        bL = b[..., -1]
        g = bL[..., None] - b + ic
        m_new = jnp.maximum(bL + m, g.max(-1))
        decay = jnp.exp(bL + m - m_new)
        wk = jnp.exp(g - m_new[..., None])[..., None] * kc
        C_new = decay[..., None, None] * C + jnp.einsum('bhsd,bhse->bhde', wk, vc)
        n_new = decay[..., None] * n + wk.sum(-2)
        return (C_new, n_new, m_new), h

    xs = tuple(to_chunks(a) for a in (q, k, v, log_i, log_f))
    (C, n, m), hs = lax.scan(step, (C0, n0, m0), xs)
    h = jnp.moveaxis(hs, 0, 2).reshape(B, H, T, D)
    return h, C, n, m


def rwkv_scan(r, d, k, v, kk, a, S0):
    def step(S, xs):
        rt, dt, kt, vt, kkt, at = xs
        sa = jnp.einsum('bhij,bhj->bhi', S, -kkt)
        S = S * dt[:, :, None, :] + sa[..., :, None] * (kkt * at)[..., None, :] + vt[..., :, None] * kt[..., None, :]
        return S, jnp.einsum('bhij,bhj->bhi', S, rt)

    xs = tuple(jnp.moveaxis(t, 1, 0) for t in (r, d, k, v, kk, a))
    S, ys = lax.scan(step, S0, xs)
    return jnp.moveaxis(ys, 0, 1), S


def mixer(h, st, lw, grid):
    (w_in, m_conv_w, m_conv_b, m_gate_b, m_ln_g, r_mu, r_w0, r_w2, r_a0, r_a2,
     r_k_k, r_k_a, r_r_k, r_ln_g, r_ln_b, w_out) = lw
    C0, n0, m0, S0 = st
    f32 = jnp.float32
    B, T, _ = h.shape
    p = jnp.einsum('btd,dc->btc', h, w_in)
    mq, mk, mv, mo, mz, mg, rz, rs = _split(p, [M_WIDTH] * 5 + [M_GATE_COLS, R_WIDTH, SHIFT_COLS])

    qk = jax.nn.silu(centred_conv(jnp.concatenate([mq, mk], -1), m_conv_w, m_conv_b))
    mq, mk = _split(qk, [M_WIDTH, M_WIDTH])
    heads = lambda a: a.reshape(B, T, M_HEADS, M_DK).transpose(0, 2, 1, 3).astype(f32)
    q = heads(mq)
    k = heads(mk) * (M_DK ** -0.5)
    v = heads(mv)
    gates = (mg.reshape(B, T, N_DIR, 2, M_HEADS).astype(f32) + m_gate_b).transpose(2, 3, 0, 4, 1)
    log_i = gates[:, 0]
    log_f = jax.nn.log_sigmoid(gates[:, 1])
    fl = lambda a: jnp.flip(a, axis=2)
    c32 = lambda a: a.astype(f32)
    h_f, Cf, nf, mf = mlstm_scan(q, k, v, log_i[0], log_f[0], c32(C0[:, 0]), c32(n0[:, 0]), c32(m0[:, 0]))
    h_b, Cb, nb, mb = mlstm_scan(fl(q), fl(k), fl(v), fl(log_i[1]), fl(log_f[1]),
                                 c32(C0[:, 1]), c32(n0[:, 1]), c32(m0[:, 1]))
    hm = jax.nn.sigmoid(heads(mo)) * (h_f + fl(h_b))
    mu = hm.mean(-1, keepdims=True)
    hm = (hm - mu) * lax.rsqrt(jnp.mean((hm - mu) ** 2, axis=-1, keepdims=True) + EPS)
    hm = hm.transpose(0, 2, 1, 3).reshape(B, T, M_WIDTH) * m_ln_g * jax.nn.silu(mz.astype(f32))

    shift = shift_grid if grid else shift_seq
    rs = rs + r_mu * (shift(rs) - rs)
    rr, rk, rv, rwd, rad = _split(rs, [R_WIDTH] * 3 + [N_DIR * LORA, N_DIR * LORA])
    rh = lambda a: a.reshape(B, T, R_HEADS, R_N).astype(f32)
    r = rh(rr)
    vv = rh(rv)
    kraw = rk.astype(f32)
    rwd = rwd.reshape(B, T, N_DIR, LORA).astype(f32)
    rad = rad.reshape(B, T, N_DIR, LORA).astype(f32)
    wlog = -jax.nn.softplus(-(r_w0 + jnp.einsum('btzl,zlc->btzc', jnp.tanh(rwd), r_w2))) - 0.5
    decay = jnp.exp(-jnp.exp(wlog))
    a = jax.nn.sigmoid(r_a0 + jnp.einsum('btzl,zlc->btzc', rad, r_a2))
    kk = rh(kraw * r_k_k)
    kk = kk / jnp.maximum(jnp.sqrt(jnp.sum(kk * kk, axis=-1, keepdims=True)), 1e-12)
    kz = kraw[:, :, None, :] * (1.0 + (a - 1.0) * r_k_a)
    ft = lambda t: jnp.flip(t, axis=1)
    y_f, Sf = rwkv_scan(r, rh(decay[:, :, 0]), rh(kz[:, :, 0]), vv, kk, rh(a[:, :, 0]), c32(S0[:, 0]))
    y_b, Sb = rwkv_scan(ft(r), ft(rh(decay[:, :, 1])), ft(rh(kz[:, :, 1])), ft(vv), ft(kk),
                        ft(rh(a[:, :, 1])), c32(S0[:, 1]))
    y = y_f + ft(y_b)
    ym = y.mean(-1, keepdims=True)
    y = (y - ym) * lax.rsqrt(jnp.mean((y - ym) ** 2, axis=-1, keepdims=True) + LNX_EPS)
    y = y.reshape(B, T, R_WIDTH) * r_ln_g + r_ln_b
    bonus = jnp.einsum('bthn,btzhn,hn->bth', r, kz.reshape(B, T, N_DIR, R_HEADS, R_N), r_r_k)
    y = y + (bonus[..., None] * vv).reshape(B, T, R_WIDTH)
    y = y * jax.nn.silu(rz.astype(f32))

    cat = jnp.concatenate([hm, y], axis=-1).astype(h.dtype)
    out = jnp.einsum('btc,cd->btd', cat, w_out)
    new_st = (jnp.stack([Cf, Cb], 1), jnp.stack([nf, nb], 1), jnp.stack([mf, mb], 1), jnp.stack([Sf, Sb], 1))
    return out, new_st


def block(x, mod, st, lw, norm_g, grid):
    shift, scale, gate = jnp.split(mod, 3, axis=-1)
    h = rmsnorm(x, norm_g) * (1.0 + scale) + shift
    out, new_st = mixer(h, st, lw, grid)
    return x + gate * out, new_st


def setup_inputs(seed: int = 0) -> dict:
    key = jax.random.key(seed)
    ks = jax.random.split(key, 40)
    f32 = jnp.float32
    nrm = lambda k, shape, s: s * jax.random.normal(k, shape, f32)
    L = DEPTH
    return {
        "x_prompt": nrm(ks[0], (BATCH, SEQ, D_MODEL), 1.0),
        "x_sample": nrm(ks[1], (DEC_BATCH, DEC_SEQ, D_MODEL), 1.0),
        "state_mlstm_C": nrm(ks[2], (DEC_BATCH, L, N_DIR, M_HEADS, M_DK, M_DK), 0.05),
        "state_mlstm_n": nrm(ks[3], (DEC_BATCH, L, N_DIR, M_HEADS, M_DK), 0.5),
        "state_mlstm_m": nrm(ks[4], (DEC_BATCH, L, N_DIR, M_HEADS), 1.0),
        "state_rwkv_S": nrm(ks[5], (DEC_BATCH, L, N_DIR, R_HEADS, R_N, R_N), 0.1),
        "c": nrm(ks[6], (DEC_BATCH, D_MODEL), 1.0),
        "c_ctx": nrm(ks[7], (D_MODEL,), 1.0),
        "norm_g": 1.0 + nrm(ks[8], (L, D_MODEL), 0.02),
        "w_ada": nrm(ks[9], (L, D_MODEL, 3 * D_MODEL), 0.5 * D_MODEL ** -0.5),
        "b_ada": nrm(ks[10], (L, 3 * D_MODEL), 0.02),
        "w_in": nrm(ks[11], (L, D_MODEL, IN_COLS), D_MODEL ** -0.5),
        "m_conv_w": nrm(ks[12], (L, CONV_K, 2 * M_WIDTH), CONV_K ** -0.5),
        "m_conv_b": nrm(ks[13], (L, 2 * M_WIDTH), 0.02),
        "m_gate_b": jnp.stack([nrm(ks[14], (L, N_DIR, M_HEADS), 0.1),
                               jnp.linspace(3.0, 6.0, M_HEADS, dtype=f32) + nrm(ks[15], (L, N_DIR, M_HEADS), 0.1)],
                              axis=2),
        "m_ln_g": 1.0 + nrm(ks[16], (L, M_WIDTH), 0.02),
        "r_mu": jax.random.uniform(ks[17], (L, SHIFT_COLS), f32),
        "r_w0": jax.random.uniform(ks[18], (L, N_DIR, R_WIDTH), f32, -6.0, 0.0),
        "r_w2": nrm(ks[19], (L, N_DIR, LORA, R_WIDTH), 0.1 * LORA ** -0.5),
        "r_a0": nrm(ks[20], (L, N_DIR, R_WIDTH), 0.1),
        "r_a2": nrm(ks[21], (L, N_DIR, LORA, R_WIDTH), 0.1 * LORA ** -0.5),
        "r_k_k": 0.85 + nrm(ks[22], (L, R_WIDTH), 0.02),
        "r_k_a": 1.0 + nrm(ks[23], (L, R_WIDTH), 0.02),
        "r_r_k": nrm(ks[24], (L, R_HEADS, R_N), 0.1),
        "r_ln_g": 1.0 + nrm(ks[25], (L, R_WIDTH), 0.02),
        "r_ln_b": nrm(ks[26], (L, R_WIDTH), 0.02),
        "w_out": nrm(ks[27], (L, D_MODEL, D_MODEL), D_MODEL ** -0.5),
        "final_g": 1.0 + nrm(ks[28], (D_MODEL,), 0.02),
    }


def reference(x_prompt, x_sample, state_mlstm_C, state_mlstm_n, state_mlstm_m, state_rwkv_S, c, c_ctx,
              norm_g, w_ada, b_ada, w_in, m_conv_w, m_conv_b, m_gate_b, m_ln_g, r_mu, r_w0, r_w2, r_a0,
              r_a2, r_k_k, r_k_a, r_r_k, r_ln_g, r_ln_b, w_out, final_g):
    f32 = jnp.float32
    bp = x_prompt.shape[0]
    ctx_state0 = (jnp.zeros((bp, N_DIR, M_HEADS, M_DK, M_DK), f32),
                  jnp.zeros((bp, N_DIR, M_HEADS, M_DK), f32),
                  jnp.full((bp, N_DIR, M_HEADS), -jnp.inf, f32),
                  jnp.zeros((bp, N_DIR, R_HEADS, R_N, R_N), f32))
    xp, xs = x_prompt, x_sample
    new_C, new_n, new_m, new_S = [], [], [], []
    for l in range(DEPTH):
        lw = (w_in[l], m_conv_w[l], m_conv_b[l], m_gate_b[l], m_ln_g[l], r_mu[l], r_w0[l], r_w2[l],
              r_a0[l], r_a2[l], r_k_k[l], r_k_a[l], r_r_k[l], r_ln_g[l], r_ln_b[l], w_out[l])
        mod_p = (jax.nn.silu(c_ctx) @ w_ada[l] + b_ada[l])[None, None, :]
        mod_s = (jax.nn.silu(c) @ w_ada[l] + b_ada[l])[:, None, :]
        xp, (Cp, np_, mp, Sp) = block(xp, mod_p, ctx_state0, lw, norm_g[l], False)
        st_s = (state_mlstm_C[:, l], state_mlstm_n[:, l], state_mlstm_m[:, l], state_rwkv_S[:, l])
        xs, _ = block(xs, mod_s, st_s, lw, norm_g[l], True)
        new_C.append(Cp)
        new_n.append(np_)
        new_m.append(mp)
        new_S.append(Sp)
    y_prompt = rmsnorm(xp, final_g)
    y_sample = rmsnorm(xs, final_g)
    return (y_prompt, y_sample, jnp.stack(new_C, 1), jnp.stack(new_n, 1), jnp.stack(new_m, 1), jnp.stack(new_S, 1))
```

```python
import contextlib
import numpy as np
import concourse.bass as bass
import concourse.mybir as mybir
from concourse.bass_utils import run_bass_kernel_spmd

F32 = mybir.dt.float32
BF16 = mybir.dt.bfloat16
AF = mybir.ActivationFunctionType
ALU = mybir.AluOpType
AX = mybir.AxisListType

N_DMA_SEMS = 8


class Tok:
    __slots__ = ("name", "w", "r")

    def __init__(self, name=""):
        self.name = name
        self.w = None
        self.r = {}


class Eng:
    def __init__(self, name, kind, issuer=None):
        self.name = name
        self.kind = kind
        self.issuer = issuer
        self.ops = []
        self.count = 0
        self.ninst = 0
        self.waited = {}
        self.sems = None


class EW:
    def __init__(self, prog, name):
        self._p = prog
        self._n = name

    def __getattr__(self, meth):
        def rec(*args, r=(), w=(), **kw):
            import sys as _s
            ln = _s._getframe(1).f_lineno

            def fn(e):
                inst = getattr(e, meth)(*args, **kw)
                if DEBUG.get("trace_inst"):
                    try:
                        DEBUG.setdefault("inst_lines", {})[inst.ins.name] = (meth, ln)
                    except Exception:
                        pass
                return inst
            fn._ln = ln
            self._p.op(self._n, fn, reads=r, writes=w, dur=est_dur(self._n, meth, args, kw))
        return rec


def _fsize(ap):
    try:
        n = 1
        for d in ap.shape[1:]:
            n *= int(d)
        return n
    except Exception:
        return 256


def est_dur(engname, meth, args, kw):
    out = kw.get("out", args[0] if args else None)
    if engname == "pe":
        if meth == "transpose":
            return 0.12
        rhs = kw.get("rhs")
        n = _fsize(rhs) if rhs is not None else 128
        return DEBUG.get("pe_fix", 0.005) + max(n, 64) / 2300.0
    if engname.startswith("q_"):
        n = _fsize(out) if out is not None else 1024
        try:
            npart = int(out.shape[0])
        except Exception:
            npart = 128
        return 2.0 + npart * n * 2.0 / 150e3
    n = _fsize(out) if out is not None else 256
    if engname == "pool":
        return 0.3 + n / 500.0
    return DEBUG.get("ev_fix", 0.22) + n / 960.0


class Prog:
    def __init__(self):
        self.engs = {}
        for n in ("pe", "dve", "act", "pool", "sp"):
            self.engs[n] = Eng(n, "c")
        for n, iss in (("q_sp", "sp"), ("q_act", "act"), ("q_pool", "pool")):
            self.engs[n] = Eng(n, "d", iss)
        self.n_ops = 0
        self.sched = bool(DEBUG.get("sched", True))
        self.seg = []
        self.pending = {n: [] for n in ("pe", "dve", "act", "pool", "sp")}
        self.pe = EW(self, "pe")
        self.dve = EW(self, "dve")
        self.act = EW(self, "act")
        self.pool = EW(self, "pool")
        self.q_sp = EW(self, "q_sp")
        self.q_act = EW(self, "q_act")
        self.q_pool = EW(self, "q_pool")

    def _deps(self, eng, reads, writes):
        deps = {}

        def add(e, s, same_ok, ni=None):
            if e is eng and eng.kind == "c":
                if not same_ok:
                    return
                if ni is not None and eng.ninst - ni >= 3:
                    return
            k = (e.name, (s - 1) % N_DMA_SEMS if e.kind == "d" else 0)
            if deps.get(k, 0) < s:
                deps[k] = s

        for t in reads:
            if t.w is not None:
                add(t.w[0], t.w[1], True, t.w[2])
        for t in writes:
            if t.w is not None:
                add(t.w[0], t.w[1], False)
            for e, s in t.r.items():
                add(e, s, False)
        return deps

    def op(self, engname, fn, reads=(), writes=(), dur=0.5):
        if self.sched:
            self.seg.append((engname, fn, tuple(reads), tuple(writes), dur))
            return
        self._op(engname, fn, reads, writes)

    def flush(self):
        seg = self.seg
        self.seg = []
        n = len(seg)
        if n == 0:
            return
        HOP = DEBUG.get("hop_big", DEBUG.get("hop", 1.0)) if n > 40000 else DEBUG.get("hop", 1.0)
        preds = [None] * n
        lastw = {}
        readers = {}
        for i, (en, fn, rd, wr, du) in enumerate(seg):
            ps_ = set()
            for t in rd:
                j = lastw.get(id(t))
                if j is not None:
                    ps_.add(j)
            for t in wr:
                j = lastw.get(id(t))
                if j is not None:
                    ps_.add(j)
                for j in readers.get(id(t), ()):
                    ps_.add(j)
            ps_.discard(i)
            preds[i] = ps_
            for t in wr:
                lastw[id(t)] = i
                readers[id(t)] = []
            for t in rd:
                readers.setdefault(id(t), []).append(i)
        succs = [[] for _ in range(n)]
        for i in range(n):
            for j in preds[i]:
                succs[j].append(i)
        issuer_ = {en: (e.issuer if e.kind == "d" else en) for en, e in self.engs.items()}
        mark = [False] * n
        last_on = {}
        for i, (en, fn, rd, wr, du) in enumerate(seg):
            if self.engs[en].kind == "d":
                mark[i] = True
                continue
            last_on[en] = i
            wset = set(id(t) for t in wr)
            for j in succs[i]:
                enj = seg[j][0]
                if self.engs[enj].kind == "d" or enj != en:
                    mark[i] = True
                    break
                if any(id(t) in wset for t in seg[j][2]):
                    mark[i] = True
                    break
        for en, i in last_on.items():
            mark[i] = True
        prio = [0.0] * n
        for i in range(n - 1, -1, -1):
            m = 0.0
            for k in succs[i]:
                if prio[k] > m:
                    m = prio[k]
            prio[i] = seg[i][4] + 0.4 + m
        import heapq
        issuer = {}
        for en, e in self.engs.items():
            issuer[en] = e.issuer if e.kind == "d" else en
        npred = [len(p) for p in preds]
        ready_t = [0.0] * n
        ready = {k: [] for k in ("pe", "dve", "act", "pool", "sp")}
        free_t = {k: 0.0 for k in ready}
        for i in range(n):
            if npred[i] == 0:
                heapq.heappush(ready[issuer[seg[i][0]]], (-prio[i], i))
        order = []
        done = 0
        WINDOW = 3000
        while done < n:
            best = None
            for k, hp_ in ready.items():
                if not hp_:
                    continue
                cand = None
                tmpl = []
                cnt = 0
                while hp_ and cnt < DEBUG.get("cand", 8):
                    pr, i = heapq.heappop(hp_)
                    tmpl.append((pr, i))
                    cnt += 1
                    st_ = max(free_t[k], ready_t[i])
                    if cand is None or st_ < cand[0] - 1e-9:
                        cand = (st_, pr, i)
                for it in tmpl:
                    heapq.heappush(hp_, it)
                if cand is not None and (best is None or cand[0] < best[0]):
                    best = (cand[0], k, cand[2])
            st_, k, i = best
            hp_ = ready[k]
            hp_.remove((-prio[i], i))
            heapq.heapify(hp_)
            en, fn, rd, wr, du = seg[i]
            if self.engs[en].kind == "d":
                free_t[k] = st_ + 0.08
                fin = st_ + du
            else:
                free_t[k] = st_ + du
                fin = st_ + du
            order.append((st_, i))
            if DEBUG.get("why") is not None:
                fin_t = DEBUG.setdefault("_fin", {})
                fin_t[i] = fin
                idle = st_ - prev_free.get(k, 0.0) if (prev_free := DEBUG.setdefault("_pf", {})) is not None else 0.0
                if k == DEBUG["why"] and preds[i] and ready_t[i] > prev_free.get(k, 0.0) + 1e-6:
                    j = max(preds[i], key=lambda q: fin_t.get(q, 0.0))
                    key = (getattr(seg[j][1], "_ln", 0), seg[j][0], getattr(fn, "_ln", 0))
                    DEBUG.setdefault("_blame", {})[key] = DEBUG.setdefault("_blame", {}).get(key, 0.0) + (st_ - max(prev_free.get(k, 0.0), 0.0))
                prev_free[k] = free_t[k]
            done += 1
            for j in succs[i]:
                npred[j] -= 1
                rt = fin + (HOP if issuer[seg[j][0]] != k or self.engs[en].kind == "d" else DEBUG.get("same", 0.0))
                if rt > ready_t[j]:
                    ready_t[j] = rt
                if npred[j] == 0:
                    heapq.heappush(ready[issuer[seg[j][0]]], (-prio[j], j))
        order.sort()
        self.sim_time = getattr(self, "sim_time", 0.0) + max(free_t.values())
        last_emit = {}
        for st_, i in order:
            if self.engs[seg[i][0]].kind == "c":
                last_emit[seg[i][0]] = i
        for i in last_emit.values():
            mark[i] = True
        for st_, i in order:
            en, fn, rd, wr, du = seg[i]
            self._op(en, fn, rd, wr, marked=mark[i])

    def _op(self, engname, fn, reads=(), writes=(), marked=True):
        eng = self.engs[engname]
        if eng.kind == "d":
            return self._dma(eng, fn, reads, writes)
        deps = self._deps(eng, reads, writes)
        waits = []
        for k, s in deps.items():
            en = k[0]
            if en != eng.name and eng.waited.get(k, 0) >= s:
                continue
            eng.waited[k] = max(eng.waited.get(k, 0), s)
            waits.append((en, s))
        if self.pending[eng.name]:
            waits = self.pending[eng.name] + waits
            self.pending[eng.name] = []
        if marked:
            eng.count += 1
            seq = eng.count
        else:
            seq = eng.count + 1
        eng.ninst += 1
        eng.ops.append((waits, fn, None, marked))
        for t in writes:
            t.w = (eng, seq, eng.ninst)
            t.r = {}
        for t in reads:
            if t.r.get(eng, 0) < seq:
                t.r[eng] = seq
        self.n_ops += 1
        return seq

    def _dma(self, q, fn, reads, writes):
        iss = self.engs[q.issuer]
        deps = self._deps(q, reads, writes)
        waits = []
        q.count += 1
        seq = q.count
        if seq > N_DMA_SEMS:
            k = (q.name, (seq - 1) % N_DMA_SEMS)
            deps[k] = max(deps.get(k, 0), seq - N_DMA_SEMS)
        for k, s in deps.items():
            en = k[0]
            if en == iss.name:
                waits.append((en, s))
                continue
            if iss.waited.get(k, 0) >= s:
                continue
            iss.waited[k] = s
            waits.append((en, s))
        if self.pending[iss.name]:
            waits = self.pending[iss.name] + waits
            self.pending[iss.name] = []
        iss.ninst += 1
        iss.ops.append((waits, fn, (q.name, seq), True))
        for t in writes:
            t.w = (q, seq, 0)
            t.r = {}
        for t in reads:
            t.r[q] = seq
        self.n_ops += 1
        return seq

    def barrier(self):
        self.flush()
        ws = []
        for e in self.engs.values():
            if e.count == 0:
                continue
            if e.kind == "c":
                ws.append((e.name, e.count))
            else:
                for k in range(max(1, e.count - N_DMA_SEMS + 1), e.count + 1):
                    ws.append((e.name, k))
        for n in self.pending:
            self.pending[n] = [w for w in ws if w[0] != n]
            eng = self.engs[n]
            for (en, sq) in ws:
                e = self.engs[en]
                k = (en, (sq - 1) % N_DMA_SEMS if e.kind == "d" else 0)
                eng.waited[k] = max(eng.waited.get(k, 0), sq)

    def sem_ref(self, en, s):
        e = self.engs[en]
        if e.kind == "c":
            return e.sems[0], s
        slot = (s - 1) % N_DMA_SEMS
        return e.sems[slot], 16 * ((s - 1) // N_DMA_SEMS + 1)

    def emit(self, nc, st, final_waits=()):
        self.flush()
        for e in self.engs.values():
            n = 1 if e.kind == "c" else N_DMA_SEMS
            e.sems = [st.enter_context(nc.semaphore(f"s_{e.name}_{i}")) for i in range(n)]
        fin = []
        seen = set()
        for t in final_waits:
            if t.w is None:
                continue
            e, s = t.w[0], t.w[1]
            if e.kind == "d":
                for k in range(max(1, e.count - N_DMA_SEMS + 1), e.count + 1):
                    if (e.name, k) not in seen:
                        seen.add((e.name, k))
                        fin.append((e.name, k))
            elif (e.name, s) not in seen:
                seen.add((e.name, s))
                fin.append((e.name, s))
        block = st.enter_context(nc.Block())
        prog = self

        def replay(engname):
            def body(engobj):
                e = prog.engs[engname]
                own_sem = e.sems[0]
                for waits, fn, dma, marked in e.ops:
                    for en, s in waits:
                        sem, val = prog.sem_ref(en, s)
                        engobj.wait_ge(sem, val)
                    inst = fn(engobj)
                    if dma is None:
                        if marked:
                            inst.then_inc(own_sem, 1)
                    else:
                        sem, val = prog.sem_ref(dma[0], dma[1])
                        inst.then_inc(sem, 16)
                if engname == "sp":
                    for en, s in fin:
                        sem, val = prog.sem_ref(en, s)
                        engobj.wait_ge(sem, val)
            return body

        block.tensor(replay("pe"))
        block.vector(replay("dve"))
        block.scalar(replay("act"))
        block.gpsimd(replay("pool"))
        block.sync(replay("sp"))


class TK:
    def __init__(self):
        self.d = {}

    def __getitem__(self, k):
        t = self.d.get(k)
        if t is None:
            t = self.d[k] = Tok(str(k))
        return t


D = 2048
KC = 16
T = 2560
NT = 20
NG = 5
SEQS = ((0, 2048, True), (2048, 256, False), (2304, 256, False))
EPS = 1e-6
IN_COLS = 9488
C_MQ, C_MK, C_MV, C_MO, C_MZ, C_MG, C_RZ, C_RS = 0, 1024, 2048, 3072, 4096, 5120, 5136, 6160

DEBUG = {}


class NS:
    pass


NEG = -1.0e30
ROWS = (0, 1, 2, 3, 32, 33, 34, 35)


def phase_mlstm(nc, P, tk, E, heads=(0, 1, 2, 3)):
    hT, idf, idb, pb, ptb = E.hT, E.idf, E.idb, E.pb, E.ptb
    w_in_v = E.w_in.rearrange("(c p) n -> p c n", p=128)
    hT_all = [tk["hT", g] for g in range(NG)]
    with contextlib.ExitStack() as sm:
        def sb(name, shape, dt=F32):
            return sm.enter_context(nc.sbuf_tensor("m_" + name, list(shape), dt))
        rowsM = sb("rowsM", [36, T])
        colE = sb("colE", [128, NT, 8])
        colS = sb("colS", [128, NT, 8])
        colL = sb("colL", [128, NT, 8])
        colW = sb("colW", [128, NT, 8])
        decbc = sb("decbc", [128, 8, NT])
        sel = sb("sel", [36, 8, 128])
        maskbig = sb("maskbig", [128, 2, 128], BF16)
        lng_bc = sb("lng_bc", [128, 256], BF16)
        cw = sb("cw", [128, KC, 3])
        cb = sb("cb", [128, KC])
        P.q_sp.dma_start(out=sel[:], in_=E.sel_d[:, :, :], w=[tk["sel"]])
        P.q_pool.dma_start(out=maskbig[:], in_=E.maskbig_d[:, :, :], w=[tk["maskbig"]])
        P.q_sp.dma_start(out=cw[:], in_=E.m_conv_w[:, :, :], w=[tk["cw"]])
        P.q_sp.dma_start(out=cb[:], in_=E.m_conv_b[:, :], w=[tk["cb"]])

        qT = sb("qT", [128, 2, T], BF16)
        kT = sb("kT", [128, 2, T], BF16)
        big = sb("big", [128, 2 * T])
        raw = big[:, 0:T]
        acc = big[:, T:2 * T]
        woz = big[:].bitcast(BF16)[:, 0:KC * 512].rearrange("p (k n) -> p k n", n=512)
        wqk1 = sb("wqk", [128, KC, 256], BF16)
        wqk = [wqk1, wqk1]
        def qk_block(hh):
            for qi, (c0, dst) in enumerate(((C_MQ, qT), (C_MK, kT))):
                wq = wqk[qi]
                P.q_pool.dma_start(out=wq[:], in_=w_in_v[:, :, c0 + hh * 256:c0 + (hh + 1) * 256], w=[tk["wqk", 0]])
                for blk in range(2):
                    fblk = (c0 // 128) + hh * 2 + blk
                    for g in range(NG):
                        bank = pb[g % 2]
                        for kc in range(KC):
                            P.pe.matmul(bank[:, :], lhsT=wq[:, kc, blk * 128:(blk + 1) * 128], rhs=hT[:, kc, g * 512:(g + 1) * 512],
                                        start=(kc == 0), stop=(kc == KC - 1), r=hT_all + [tk["wqk", 0]], w=[tk["pb", g % 2]])
                        P.act.copy(out=raw[:, g * 512:(g + 1) * 512], in_=bank[:, :], r=[tk["pb", g % 2]], w=[tk["raw"], tk["woz"]])
                    P.dve.tensor_scalar(out=acc[:], in0=raw[:], scalar1=cw[:, fblk, 1:2], scalar2=cb[:, fblk:fblk + 1],
                                        op0=ALU.mult, op1=ALU.add, r=[tk["raw"], tk["cw"], tk["cb"]], w=[tk["acc"], tk["woz"]])
                    for (s0, ln, _) in SEQS:
                        e0 = s0 + ln
                        P.dve.scalar_tensor_tensor(out=acc[:, s0 + 1:e0], in0=raw[:, s0:e0 - 1], scalar=cw[:, fblk, 0:1],
                                                   in1=acc[:, s0 + 1:e0], op0=ALU.mult, op1=ALU.add,
                                                   r=[tk["raw"], tk["acc"]], w=[tk["acc"]])
                        P.dve.scalar_tensor_tensor(out=acc[:, s0:e0 - 1], in0=raw[:, s0 + 1:e0], scalar=cw[:, fblk, 2:3],
                                                   in1=acc[:, s0:e0 - 1], op0=ALU.mult, op1=ALU.add,
                                                   r=[tk["raw"], tk["acc"]], w=[tk["acc"]])
                    if qi == 0:
                        P.act.activation(out=dst[:, blk, :], in_=acc[:], func=AF.Silu, r=[tk["acc"]], w=[tk["qkT", qi]])
                    else:
                        P.act.activation(out=acc[:], in_=acc[:], func=AF.Silu, r=[tk["acc"]], w=[tk["acc"]])
                        P.dve.tensor_scalar_mul(out=dst[:, blk, :], in0=acc[:], scalar1=0.0625, r=[tk["acc"]], w=[tk["qkT", qi]])

        with contextlib.ExitStack() as sg:
            def sbg(name, shape, dt=F32):
                return sg.enter_context(nc.sbuf_tensor("mg_" + name, list(shape), dt))
            wg = sbg("wg", [128, KC, 16], BF16)
            gb_bc = sbg("gb_bc", [128, 16])
            g_tok = sbg("g_tok", [128, NT, 16])
            tmpf = sbg("tmpf", [128, NT, 2, 4])
            g36 = sbg("g36", [128, NT, 2, 36])
            rowsI = sbg("rowsI", [36, T])
            rowsF = sbg("rowsF", [36, T])
            rowsB = sbg("rowsB", [36, T])
            rowsT = sbg("rowsT", [36, T])
            ones36 = sbg("ones36", [36, 2048])
            m0c = sbg("m0c", [36, 1])
            Mp = sbg("Mp", [36, NT])
            Mn = sbg("Mn", [36, NT])
            nMn = sbg("nMn", [36, NT])
            dec = sbg("dec", [36, NT])
            P.q_pool.dma_start(out=wg[:], in_=w_in_v[:, :, C_MG:C_MG + 16], w=[tk["wg"]])
            P.q_act.dma_start(out=gb_bc[:], in_=E.m_gate_b.partition_broadcast(128), w=[tk["gb"]])
            P.q_sp.dma_start(out=m0c[:], in_=E.m0[:, :], w=[tk["m0c"]])
            P.dve.memset(ones36[:], 1.0, w=[tk["ones36"]])
            P.dve.memset(g36[:], 0.0, w=[tk["g36"]])
            for tt in range(NT):
                bank = pb[tt % 2]
                for kc in range(KC):
                    P.pe.matmul(bank[:, 0:16], lhsT=hT[:, kc, tt * 128:(tt + 1) * 128], rhs=wg[:, kc, :],
                                start=(kc == 0), stop=(kc == KC - 1), r=hT_all + [tk["wg"]], w=[tk["pb", tt % 2]])
                P.dve.tensor_tensor(out=g_tok[:, tt, :], in0=bank[:, 0:16], in1=gb_bc[:], op=ALU.add,
                                    r=[tk["pb", tt % 2], tk["gb"]], w=[tk["g_tok"]])
            gv = g_tok[:].rearrange("p t (d g h) -> p t d g h", d=2, g=2)
            P.act.activation(out=tmpf[:], in_=gv[:, :, :, 1, :], func=AF.Exp, scale=-1.0, r=[tk["g_tok"]], w=[tk["tmpf"]])
            P.act.activation(out=tmpf[:], in_=tmpf[:], func=AF.Ln, bias=1.0, r=[tk["tmpf"]], w=[tk["tmpf"]])
            for d in range(2):
                po = 0 if d == 0 else 32
                P.dve.tensor_copy(out=g36[:, :, 0, po:po + 4], in_=gv[:, :, d, 0, :], r=[tk["g_tok"]], w=[tk["g36"]])
                P.dve.tensor_scalar_mul(out=g36[:, :, 1, po:po + 4], in0=tmpf[:, :, d, :], scalar1=-1.0,
                                        r=[tk["tmpf"]], w=[tk["g36"]])
            for gate, rows in ((0, rowsI), (1, rowsF)):
                for g in range(NG):
                    bank = pb[2 + (g % 2)]
                    for t4 in range(4):
                        tt = g * 4 + t4
                        P.pe.matmul(bank[0:36, t4 * 128:(t4 + 1) * 128], lhsT=g36[:, tt, gate, :], rhs=idf[:, :],
                                    start=True, stop=True, r=[tk["g36"], tk["idf"]], w=[tk["pb", 2 + (g % 2)]])
                    P.act.copy(out=rows[0:36, g * 512:(g + 1) * 512], in_=bank[0:36, :],
                               r=[tk["pb", 2 + (g % 2)]], w=[tk["rows", gate]])
            tI, tF, tB, tM, tT = tk["rows", 0], tk["rows", 1], tk["rowsB"], tk["rowsM"], tk["rowsT"]
            for si, (s0, ln, _) in enumerate(SEQS):
                for d in range(2):
                    po = 0 if d == 0 else 32

                    def V(tns, ln_=ln, s0_=s0, po_=po, d_=d):
                        v = tns[po_:po_ + 4, s0_:s0_ + ln_]
                        return v[:, ::-1] if d_ == 1 else v
                    o36 = ones36[po:po + 4, 0:ln]
                    P.dve.tensor_tensor_scan(out=V(rowsB), data0=o36, data1=V(rowsF), initial=0.0,
                                             op0=ALU.mult, op1=ALU.add, r=[tF, tk["ones36"]], w=[tB])
                    P.dve.tensor_tensor(out=V(rowsI), in0=V(rowsI), in1=V(rowsB), op=ALU.subtract, r=[tI, tB], w=[tI])
                    init = m0c[po:po + 4, 0:1] if si == 0 else NEG
                    P.dve.tensor_tensor_scan(out=V(rowsM), data0=V(rowsI), data1=V(rowsI), initial=init,
                                             op0=ALU.max, op1=ALU.max, r=[tI, tk["m0c"]], w=[tM])
            Mv = rowsM[:].rearrange("p (t i) -> p t i", i=128)
            P.dve.memset(Mp[:], 0.0, w=[tk["Mp"]])
            P.dve.memset(Mn[:], 0.0, w=[tk["Mn"]])
            P.dve.tensor_copy(out=Mn[0:4, :], in_=Mv[0:4, :, 127], r=[tM], w=[tk["Mn"]])
            P.dve.tensor_copy(out=Mn[32:36, :], in_=Mv[32:36, :, 0], r=[tM], w=[tk["Mn"]])
            for si, (s0, ln, _) in enumerate(SEQS):
                ts, te = s0 // 128, (s0 + ln) // 128
                if te - ts > 1:
                    P.dve.tensor_copy(out=Mp[0:4, ts + 1:te], in_=Mn[0:4, ts:te - 1], r=[tk["Mn"]], w=[tk["Mp"]])
                    P.dve.tensor_copy(out=Mp[32:36, ts:te - 1], in_=Mn[32:36, ts + 1:te], r=[tk["Mn"]], w=[tk["Mp"]])
                for po, tpos in ((0, ts), (32, te - 1)):
                    if si == 0:
                        P.dve.tensor_copy(out=Mp[po:po + 4, tpos:tpos + 1], in_=m0c[po:po + 4, 0:1], r=[tk["m0c"]], w=[tk["Mp"]])
                    else:
                        P.dve.memset(Mp[po:po + 4, tpos:tpos + 1], NEG, w=[tk["Mp"]])
            P.dve.tensor_scalar_mul(out=nMn[:], in0=Mn[:], scalar1=-1.0, r=[tk["Mn"]], w=[tk["nMn"]])
            P.dve.tensor_tensor(out=dec[:], in0=Mp[:], in1=Mn[:], op=ALU.subtract, r=[tk["Mp"], tk["Mn"]], w=[tk["dec"]])
            P.act.activation(out=dec[:], in_=dec[:], func=AF.Exp, r=[tk["dec"]], w=[tk["dec"]])
            for ri in range(8):
                P.pe.matmul(pb[4][:, ri * NT:(ri + 1) * NT], lhsT=sel[0:36, ri, :], rhs=dec[0:36, :], start=True, stop=True,
                            r=[tk["sel"], tk["dec"]], w=[tk["pb", 4]])
            P.act.copy(out=decbc[:].rearrange("p r t -> p (r t)"), in_=pb[4][:, 0:8 * NT], r=[tk["pb", 4]], w=[tk["decbc"]])

            def to_cols(rows, rtok, col, ctok):
                for g8 in range(0, NT, 8):
                    n = min(8, NT - g8)
                    bank = pb[2 + ((g8 // 8) % 2)]
                    bt = tk["pb", 2 + ((g8 // 8) % 2)]
                    for i in range(n):
                        tt = g8 + i
                        P.pe.matmul(bank[:, i * 36:(i + 1) * 36], lhsT=rows[0:36, tt * 128:(tt + 1) * 128],
                                    rhs=idf[0:36, 0:36], start=True, stop=True, r=[rtok, tk["idf"]], w=[bt])
                    bv = bank[:, 0:n * 36].rearrange("p (t c) -> p t c", c=36)
                    P.act.copy(out=col[:, g8:g8 + n, 0:4], in_=bv[:, :, 0:4], r=[bt], w=[ctok])
                    P.act.copy(out=col[:, g8:g8 + n, 4:8], in_=bv[:, :, 32:36], r=[bt], w=[ctok])
            to_cols(rowsI, tI, colE, tk["colE"])
            for tt in range(NT):
                for po in (0, 32):
                    P.act.activation(out=rowsT[po:po + 4, tt * 128:(tt + 1) * 128], in_=rowsM[po:po + 4, tt * 128:(tt + 1) * 128],
                                     func=AF.Exp, scale=-1.0, bias=Mp[po:po + 4, tt:tt + 1], r=[tM, tk["Mp"]], w=[tT])
            to_cols(rowsT, tT, colS, tk["colS"])
            for tt in range(NT):
                for po in (0, 32):
                    P.act.activation(out=rowsT[po:po + 4, tt * 128:(tt + 1) * 128], in_=rowsI[po:po + 4, tt * 128:(tt + 1) * 128],
                                     func=AF.Exp, scale=1.0, bias=nMn[po:po + 4, tt:tt + 1], r=[tI, tk["nMn"]], w=[tT])
            to_cols(rowsT, tT, colW, tk["colW"])
            P.dve.tensor_tensor(out=rowsB[:], in0=rowsB[:], in1=rowsM[:], op=ALU.add, r=[tB, tM], w=[tB])
            P.q_sp.dma_start(out=E.om[:, :], in_=rowsB[0:36, 2048:2560], r=[tB], w=[tk["out_om"]])
            P.act.activation(out=rowsT[:], in_=rowsB[:], func=AF.Exp, scale=-1.0, r=[tB], w=[tT])
            to_cols(rowsT, tT, colL, tk["colL"])
        if len(heads):
            qk_block(heads[0])
        P.barrier()

        v_ext = sb("v_ext", [128, NT, 257], BF16)
        k_tok = sb("k_tok", [128, NT, 256], BF16)
        hf_ = sb("hf_", [128, NT, 256], BF16)
        hb_ = sb("hb_", [128, NT, 256], BF16)
        Cst2 = [sb(f"Cst{d}", [128, 2, 257]) for d in range(2)]
        Cb2 = [sb(f"Cb{d}", [128, 2, 257], BF16) for d in range(2)]
        dexp2 = [sb(f"dexp{d}", [128, 128]) for d in range(2)]
        AT2 = [sb(f"AT{d}", [128, 128], BF16) for d in range(2)]
        av2 = [sb(f"av_sb{d}", [128, 257]) for d in range(2)]
        num2 = [sb(f"num{d}", [128, 257]) for d in range(2)]
        dd2 = [sb(f"dd{d}", [128, 2]) for d in range(2)]
        wk2 = [sb(f"wk{d}", [128, 256], BF16) for d in range(2)]
        pst = [ptb[d][:].bitcast(F32) for d in range(2)]
        hsL = [sb(f"hs{i}", [128, 256]) for i in range(2)]
        soL = [sb(f"so{i}", [128, 256], BF16) for i in range(2)]
        szL = [sb(f"sz{i}", [128, 256], BF16) for i in range(2)]
        st1L = [sb(f"st1{i}", [128, 4]) for i in range(2)]
        cat_tokL = [sb(f"cat_tok{i}", [128, 256], BF16) for i in range(2)]
        catT_sbL = [sb(f"catT_sb{i}", [128, 2, 128], BF16) for i in range(2)]
        P.dve.memset(v_ext[:, :, 256:257], 1.0, w=[tk["v_ext"]])

        for hh in heads:
            P.q_pool.dma_start(out=lng_bc[:], in_=E.m_ln_g[:, hh * 256:(hh + 1) * 256].partition_broadcast(128), w=[tk["lng"]])
            if hh != heads[0]:
                qk_block(hh)
            wv_ = wqk[0]
            P.q_pool.dma_start(out=wv_[:], in_=w_in_v[:, :, C_MV + hh * 256:C_MV + (hh + 1) * 256], w=[tk["wqk", 0]])
            for tt in range(NT):
                bank = pb[tt % 2]
                for kc in range(KC):
                    P.pe.matmul(bank[:, 0:256], lhsT=hT[:, kc, tt * 128:(tt + 1) * 128], rhs=wv_[:, kc, :],
                                start=(kc == 0), stop=(kc == KC - 1), r=hT_all + [tk["wqk", 0]], w=[tk["pb", tt % 2]])
                P.act.copy(out=v_ext[:, tt, 0:256], in_=bank[:, 0:256], r=[tk["pb", tt % 2]], w=[tk["v_ext"]])
                pt = ptb[tt % 2]
                for blk in range(2):
                    P.pe.transpose(out=pt[:, blk * 128:(blk + 1) * 128], in_=kT[:, blk, tt * 128:(tt + 1) * 128], identity=idb[:],
                                   r=[tk["qkT", 1], tk["idb"]], w=[tk["ptb", tt % 2]])
                P.dve.tensor_copy(out=k_tok[:, tt, :], in_=pt[:, 0:256], r=[tk["ptb", tt % 2]], w=[tk["k_tok"]])
            for si, (s0, ln, _) in enumerate(SEQS):
                ts, te = s0 // 128, (s0 + ln) // 128
                for d in range(2):
                    tS = tk["Cst", d]
                    if si == 0:
                        P.q_sp.dma_start(out=Cst2[d][:, :, 0:256], in_=E.mC0[d, hh].rearrange("(b p) e -> p b e", p=128), w=[tS])
                        P.q_sp.dma_start(out=Cst2[d][:, :, 256:257], in_=E.mn0[d, hh].rearrange("(b p o) -> p b o", p=128, o=1),
                                         allow_slow_non_contiguous=True, w=[tS])
                    else:
                        P.dve.memset(Cst2[d][:], 0.0, w=[tS])
                    P.act.copy(out=Cb2[d][:], in_=Cst2[d][:], r=[tS], w=[tk["Cb", d]])
                nch = te - ts
                for i in range(nch):
                    for d in range(2):
                        tt = ts + i if d == 0 else te - 1 - i
                        last = (i == nch - 1)
                        ci = d * 4 + hh
                        ri = d * 4 + hh
                        tS = tk["Cst", d]
                        Cst, Cb = Cst2[d], Cb2[d]
                        dexp, AT, av_sb, num, dd, wk = dexp2[d], AT2[d], av2[d], num2[d], dd2[d], wk2[d]
                        bX, bQ, bA, bS = pb[3 * d], pb[3 * d + 1], pb[3 * d + 2], pst[d]
                        tX, tQ, tA, tSt = tk["pb", 3 * d], tk["pb", 3 * d + 1], tk["pb", 3 * d + 2], tk["ptb", d]
                        tsl = slice(tt * 128, (tt + 1) * 128)
                        for blk in range(2):
                            P.pe.matmul(bX[:, 0:128], lhsT=kT[:, blk, tsl], rhs=qT[:, blk, tsl], start=(blk == 0), stop=(blk == 1),
                                        r=[tk["qkT", 0], tk["qkT", 1]], w=[tX])
                        P.pe.matmul(bX[:, 128:256], lhsT=sel[0:36, ri, :], rhs=rowsM[0:36, tsl], start=True, stop=False,
                                    r=[tk["sel"], tk["rowsM"]], w=[tX])
                        P.pe.matmul(bX[:, 128:256], lhsT=idb[:, :], rhs=maskbig[:, d, :], start=False, stop=True,
                                    r=[tk["idb"], tk["maskbig"]], w=[tX])
                        for blk in range(2):
                            P.pe.matmul(bQ[:, 0:257], lhsT=qT[:, blk, tsl], rhs=Cb[:, blk, :], start=(blk == 0), stop=(blk == 1),
                                        r=[tk["qkT", 0], tk["Cb", d]], w=[tQ])
                        if not (si == 0 and last):
                            P.dve.tensor_scalar_mul(out=wk[:], in0=k_tok[:, tt, :], scalar1=colW[:, tt, ci:ci + 1],
                                                    r=[tk["k_tok"], tk["colW"]], w=[tk["wk", d]])
                            for blk in range(2):
                                P.pe.matmul(bS[:, 0:257], lhsT=wk[:, blk * 128:(blk + 1) * 128], rhs=v_ext[:, tt, :], start=True, stop=True,
                                            r=[tk["wk", d], tk["v_ext"]], w=[tSt])
                                P.dve.scalar_tensor_tensor(out=Cst[:, blk, :], in0=Cst[:, blk, :], scalar=decbc[:, ri, tt:tt + 1], in1=bS[:, 0:257],
                                                           op0=ALU.mult, op1=ALU.add, r=[tS, tk["decbc"], tSt], w=[tS])
                            P.act.copy(out=Cb[:], in_=Cst[:], r=[tS], w=[tk["Cb", d]])
                        P.act.activation(out=dexp[:], in_=bX[:, 128:256], func=AF.Exp, scale=-1.0, bias=colE[:, tt, ci:ci + 1],
                                         r=[tX, tk["colE"]], w=[tk["dexp", d]])
                        P.dve.tensor_tensor(out=AT[:], in0=bX[:, 0:128], in1=dexp[:], op=ALU.mult,
                                            r=[tX, tk["dexp", d]], w=[tk["AT", d]])
                        P.pe.matmul(bA[:, 0:257], lhsT=AT[:], rhs=v_ext[:, tt, :], start=True, stop=True,
                                    r=[tk["AT", d], tk["v_ext"]], w=[tA])
                        P.act.copy(out=av_sb[:], in_=bA[:, 0:257], r=[tA], w=[tk["av_sb", d]])
                        P.dve.scalar_tensor_tensor(out=num[:], in0=bQ[:, 0:257], scalar=colS[:, tt, ci:ci + 1], in1=av_sb[:],
                                                   op0=ALU.mult, op1=ALU.add, r=[tQ, tk["colS"], tk["av_sb", d]], w=[tk["num", d]])
                        P.dve.scalar_tensor_tensor(out=dd[:, 0:1], in0=num[:, 256:257], scalar=-1.0, in1=num[:, 256:257],
                                                   op0=ALU.mult, op1=ALU.max, r=[tk["num", d]], w=[tk["dd", d]])
                        P.dve.tensor_tensor(out=dd[:, 0:1], in0=dd[:, 0:1], in1=colL[:, tt, ci:ci + 1], op=ALU.max,
                                            r=[tk["dd", d], tk["colL"]], w=[tk["dd", d]])
                        P.dve.reciprocal(out=dd[:, 1:2], in_=dd[:, 0:1], r=[tk["dd", d]], w=[tk["dd", d]])
                        if d == 0:
                            P.act.activation(out=hf_[:, tt, :], in_=num[:, 0:256], func=AF.Copy, scale=dd[:, 1:2],
                                             r=[tk["num", d], tk["dd", d]], w=[tk["hf", tt]])
                        else:
                            P.act.activation(out=hb_[:, tt, :], in_=num[:, 0:256], func=AF.Copy, scale=dd[:, 1:2],
                                             r=[tk["num", d], tk["dd", d]], w=[tk["hb", tt]])
                for d in range(2):
                    if si > 0:
                        pi = si - 1
                        tS = tk["Cst", d]
                        P.q_sp.dma_start(out=E.oC[pi, d, hh].rearrange("(b p) e -> p b e", p=128), in_=Cst2[d][:, :, 0:256], r=[tS], w=[tk["out_oC"]])
                        P.q_sp.dma_start(out=E.on[pi, d, hh].rearrange("(b p o) -> p b o", p=128, o=1), in_=Cst2[d][:, :, 256:257],
                                         allow_slow_non_contiguous=True, r=[tS], w=[tk["out_oC"]])
            P.q_pool.dma_start(out=woz[:, :, 0:256], in_=w_in_v[:, :, C_MO + hh * 256:C_MO + (hh + 1) * 256], w=[tk["woz"], tk["raw"], tk["acc"]])
            P.q_pool.dma_start(out=woz[:, :, 256:512], in_=w_in_v[:, :, C_MZ + hh * 256:C_MZ + (hh + 1) * 256], w=[tk["woz"], tk["raw"], tk["acc"]])
            for tt in range(NT):
                tsl = slice(tt * 128, (tt + 1) * 128)
                fi = tt % 2
                hs, so, sz, st1, cat_tok, catT_sb = hsL[fi], soL[fi], szL[fi], st1L[fi], cat_tokL[fi], catT_sbL[fi]
                pbf, tpbf = pb[4 + fi], tk["pb", 4 + fi]
                for kc in range(KC):
                    P.pe.matmul(pbf[:, :], lhsT=hT[:, kc, tsl], rhs=woz[:, kc, :], start=(kc == 0), stop=(kc == KC - 1),
                                r=hT_all + [tk["woz"]], w=[tpbf])
                P.act.activation(out=so[:], in_=pbf[:, 0:256], func=AF.Sigmoid, r=[tpbf], w=[tk["so", fi]])
                P.act.activation(out=sz[:], in_=pbf[:, 256:512], func=AF.Silu, r=[tpbf], w=[tk["sz", fi]])
                P.dve.tensor_tensor(out=hs[:], in0=hf_[:, tt, :], in1=hb_[:, tt, :], op=ALU.add, r=[tk["hf", tt], tk["hb", tt]], w=[tk["hs", fi]])
                P.dve.tensor_tensor(out=hs[:], in0=hs[:], in1=so[:], op=ALU.mult, r=[tk["hs", fi], tk["so", fi]], w=[tk["hs", fi]])
                P.dve.tensor_reduce(out=st1[:, 0:1], in_=hs[:], axis=AX.X, op=ALU.add, r=[tk["hs", fi]], w=[tk["st1", fi]])
                P.dve.tensor_scalar_mul(out=st1[:, 1:2], in0=st1[:, 0:1], scalar1=-1.0 / 256, r=[tk["st1", fi]], w=[tk["st1", fi]])
                P.dve.tensor_scalar_add(out=hs[:], in0=hs[:], scalar1=st1[:, 1:2], r=[tk["hs", fi], tk["st1", fi]], w=[tk["hs", fi]])
                P.act.activation(out=so[:], in_=hs[:], func=AF.Square, accum_out=st1[:, 2:3], r=[tk["hs", fi]], w=[tk["so", fi], tk["st1", fi]])
                P.dve.tensor_scalar(out=st1[:, 2:3], in0=st1[:, 2:3], scalar1=1.0 / 256, scalar2=EPS, op0=ALU.mult, op1=ALU.add,
                                    r=[tk["st1", fi]], w=[tk["st1", fi]])
                P.act.sqrt(out=st1[:, 2:3], in_=st1[:, 2:3], r=[tk["st1", fi]], w=[tk["st1", fi]])
                P.dve.reciprocal(out=st1[:, 3:4], in_=st1[:, 2:3], r=[tk["st1", fi]], w=[tk["st1", fi]])
                P.dve.scalar_tensor_tensor(out=hs[:], in0=hs[:], scalar=st1[:, 3:4], in1=lng_bc[:, :],
                                           op0=ALU.mult, op1=ALU.mult, r=[tk["hs", fi], tk["st1", fi], tk["lng"]], w=[tk["hs", fi]])
                P.dve.tensor_tensor(out=cat_tok[:], in0=hs[:], in1=sz[:], op=ALU.mult, r=[tk["hs", fi], tk["sz", fi]], w=[tk["cat_tok", fi]])
                pt = ptb[tt % 2]
                for blk in range(2):
                    P.pe.transpose(out=pt[:, blk * 128:(blk + 1) * 128], in_=cat_tok[:, blk * 128:(blk + 1) * 128], identity=idb[:],
                                   r=[tk["cat_tok", fi], tk["idb"]], w=[tk["ptb", tt % 2]])
                P.act.copy(out=catT_sb[:].rearrange("p b t -> p (b t)"), in_=pt[:, 0:256], r=[tk["ptb", tt % 2]], w=[tk["catT_sb", fi]])
                g, t4 = tt // 4, tt % 4
                P.q_act.dma_start(out=E.catT[g, :, hh * 2:hh * 2 + 2, t4 * 128:(t4 + 1) * 128], in_=catT_sb[:],
                                  r=[tk["catT_sb", fi]], w=[tk["catT", g]])
    P.barrier()


KAP = 0.6065306597126334
LNX_EPS = 64e-5


def phase_rwkv(nc, P, tk, E, pairs=range(8)):
    hT, idf, idb, pb, ptb = E.hT, E.idf, E.idb, E.pb, E.ptb
    w_in_v = E.w_in.rearrange("(c p) n -> p c n", p=128)
    hT_all = [tk["hT", g] for g in range(NG)]
    with contextlib.ExitStack() as sr:
        def sb(name, shape, dt=F32):
            return sr.enter_context(nc.sbuf_tensor("rk_" + name, list(shape), dt))
        bonesf = sb("bonesf", [128, 128])
        bonesb = sb("bonesb", [128, 128], BF16)
        mkf = sb("mkf", [128, 2, 5, 64], BF16)
        rm = sb("rm", [128, 513])
        gmask = sb("gmask", [128, 4])
        idpf = sb("idpf", [128, 8, 64], BF16)
        mu_sb = sb("mu", [128, 26])
        w0_sb = sb("w0", [128, 2, 8])
        a0_sb = sb("a0", [128, 2, 8])
        kk_sb = sb("kkp", [128, 8])
        ka_sb = sb("kap", [128, 8])
        rk_sb = sb("rkp", [128, 8])
        lg_sb = sb("lgp", [128, 8])
        lb_sb = sb("lbp", [128, 8])
        w2hh = sb("w2hh", [128, 1, 2, 128], BF16)
        a2hh = sb("a2hh", [128, 1, 2, 128], BF16)
        coef = sb("coef", [128, 8])
        coef2 = sb("coef2", [128, 8])
        for dst, src, nm in ((bonesf, E.bones_d, "bonesf"), (rm, E.rm_d, "rm"), (gmask, E.gmask_d, "gmask"),
                             (mu_sb, E.r_mu, "mu"), (w0_sb, E.r_w0, "w0"), (a0_sb, E.r_a0, "a0"),
                             (kk_sb, E.r_k_k, "kkp"), (ka_sb, E.r_k_a, "kap"), (rk_sb, E.r_r_k, "rkp"), (lg_sb, E.r_ln_g, "lgp"),
                             (lb_sb, E.r_ln_b, "lbp")):
            P.q_sp.dma_start(out=dst[:], in_=src, w=[tk[nm]])
        P.dve.tensor_copy(out=bonesb[:], in_=bonesf[:], r=[tk["bonesf"]], w=[tk["bonesb"]])
        P.q_pool.dma_start(out=idpf[:], in_=E.idp_d, w=[tk["idpf"]])
        P.q_pool.dma_start(out=mkf[:], in_=E.mk_d, w=[tk["mkf"]])
        P.dve.memset(w2hh[:], 0.0, w=[tk["w2h", 0]])
        P.dve.memset(a2hh[:], 0.0, w=[tk["a2h", 0]])

        rwdT = sb("rwdT", [128, T], BF16)
        radT = sb("radT", [128, T], BF16)
        big = sb("big", [128, 2 * T])
        raw = big[:, 0:T]
        acc = big[:, T:2 * T]
        wr = sb("wr", [128, 2, KC, 128], BF16)
        wr_n = [0]

        def project_cm(col0, widx, evac):
            widx = wr_n[0] % 2
            wr_n[0] += 1
            P.q_pool.dma_start(out=wr[:, widx, :, :], in_=w_in_v[:, :, col0:col0 + 128], w=[tk["wr", widx]])
            for g in range(NG):
                bank = pb[g % 2]
                for kc in range(KC):
                    P.pe.matmul(bank[:, :], lhsT=wr[:, widx, kc, :], rhs=hT[:, kc, g * 512:(g + 1) * 512],
                                start=(kc == 0), stop=(kc == KC - 1), r=hT_all + [tk["wr", widx]], w=[tk["pb", g % 2]])
                evac(g, bank, tk["pb", g % 2])

        RA = [(raw, acc, 0), (raw, acc, 0)]
        cur_ra = [RA[0]]
        coefs = [coef, coef2]

        def evac_raw(g, bank, bt):
            raw_, acc_, pp = cur_ra[0]
            P.act.copy(out=raw_[:, g * 512:(g + 1) * 512], in_=bank[:, :], r=[bt], w=[tk["raw", pp, g]])
            P.act.activation(out=acc_[:, g * 512:(g + 1) * 512], in_=bank[:, :], func=AF.Copy, scale=coefs[pp][:, 0:1],
                             r=[bt, tk["coef", pp]], w=[tk["acc", pp, g]])

        def prep_coef(mublk):
            m = mu_sb[:, mublk:mublk + 1]
            pp = cur_ra[0][2]
            coef = coefs[pp]
            tc_ = tk["coef", pp]
            P.dve.tensor_scalar(out=coef[:, 0:1], in0=m, scalar1=-1.0, scalar2=1.0, op0=ALU.mult, op1=ALU.add, r=[tk["mu"]], w=[tc_])
            P.dve.tensor_scalar_mul(out=coef[:, 1:5], in0=gmask[:], scalar1=m, r=[tk["gmask"], tk["mu"]], w=[tc_])
            P.dve.tensor_tensor(out=coef[:, 5:6], in0=coef[:, 1:2], in1=coef[:, 3:4], op=ALU.add, r=[tc_], w=[tc_])
            P.dve.tensor_tensor(out=coef[:, 6:7], in0=coef[:, 2:3], in1=coef[:, 4:5], op=ALU.add, r=[tc_], w=[tc_])

        def mix(mublk):
            raw, acc, pp = cur_ra[0]
            coef = coefs[pp]
            tc_ = tk["coef", pp]
            traw = [tk["raw", pp, g] for g in range(NG)]
            tacc = [tk["acc", pp, g] for g in range(NG)]
            rv_ = raw[:, 0:2048].rearrange("p (r c) -> p r c", c=64)
            av_ = acc[:, 0:2048].rearrange("p (r c) -> p r c", c=64)

            def stt(o, i, cidx, tr, ta):
                P.dve.scalar_tensor_tensor(out=o, in0=i, scalar=coef[:, cidx:cidx + 1], in1=o, op0=ALU.mult, op1=ALU.add,
                                           r=tr + ta + [tc_], w=ta)
            stt(av_[:, :, 1:64], rv_[:, :, 0:63], 1, traw[0:4], tacc[0:4])
            stt(av_[:, :, 0:63], rv_[:, :, 1:64], 2, traw[0:4], tacc[0:4])
            stt(av_[:, 1:32, :], rv_[:, 0:31, :], 3, traw[0:4], tacc[0:4])
            stt(av_[:, 0:31, :], rv_[:, 1:32, :], 4, traw[0:4], tacc[0:4])
            rp_ = raw[:, 2048:2560].rearrange("p (s c) -> p s c", c=256)
            ap_ = acc[:, 2048:2560].rearrange("p (s c) -> p s c", c=256)
            stt(ap_[:, :, 1:256], rp_[:, :, 0:255], 5, traw[4:5], tacc[4:5])
            stt(ap_[:, :, 0:255], rp_[:, :, 1:256], 6, traw[4:5], tacc[4:5])

        accall = [tk["acc", 0, g] for g in range(NG)]
        prep_coef(24)
        project_cm(C_RS + 3072, 0, evac_raw)
        mix(24)
        P.act.activation(out=rwdT[:], in_=acc[:], func=AF.Tanh, r=accall, w=[tk["rwdT"]])
        prep_coef(25)
        project_cm(C_RS + 3200, 0, evac_raw)
        mix(25)
        P.act.copy(out=radT[:], in_=acc[:], r=accall, w=[tk["radT"]])

        stgA = sb("stgA", [128, 4, T], BF16)
        stg = [stgA[:, j, :] for j in range(4)]
        raw2 = stgA[:, 0:2, :].rearrange("p a t -> p (a t)").bitcast(F32)
        acc2 = stgA[:, 2:4, :].rearrange("p a t -> p (a t)").bitcast(F32)
        RA[0] = (raw, acc, 0)
        RA[1] = (raw2, acc2, 1)
        nblk = [0]
        for hp in pairs:
            for j, c0 in enumerate((C_RS, C_RS + 1024, C_RS + 2048)):
                cur_ra[0] = RA[nblk[0] % 2]
                nblk[0] += 1
                prep_coef(j * 8 + hp)
                project_cm(c0 + hp * 128, j, evac_raw)
                mix(j * 8 + hp)
                pp = cur_ra[0][2]
                P.q_pool.dma_start(out=E.rscr[hp, j], in_=cur_ra[0][1], r=[tk["acc", pp, g] for g in range(NG)], w=[tk["rscr", hp]])
            cur_ra[0] = RA[nblk[0] % 2]
            nblk[0] += 1
            pp = cur_ra[0][2]

            def evac_sz(g, bank, bt, pp=pp, accb=cur_ra[0][1]):
                P.act.activation(out=accb[:, g * 512:(g + 1) * 512], in_=bank[:, :], func=AF.Silu, r=[bt], w=[tk["acc", pp, g]])
            project_cm(C_RZ + hp * 128, 3, evac_sz)
            P.q_pool.dma_start(out=E.rscr[hp, 3], in_=cur_ra[0][1], r=[tk["acc", pp, g] for g in range(NG)], w=[tk["rscr", hp]])
        P.barrier()

        hTf = hT[:].rearrange("p k t -> p (k t)")
        carve_off = [0]

        def carve(shape, dt):
            n = 1
            for d_ in shape[1:]:
                n *= d_
            ne = n if dt == BF16 else 2 * n
            v = hTf[:, carve_off[0]:carve_off[0] + ne]
            carve_off[0] += ne
            if dt != BF16:
                v = v.bitcast(F32)
            if len(shape) == 2:
                return v
            names = "abcd"[:len(shape) - 1]
            kws = {names[i]: shape[1 + i] for i in range(1, len(shape) - 1)}
            return v.rearrange("p (" + " ".join(names) + ") -> p " + " ".join(names), **kws)

        class BS:
            pass

        def mkset(z):
            B = BS()
            if z == 0:
                al_ = lambda name, shape, dt=F32: sb(name, shape, dt)[:]
                B.tmp = [big[:, i * 512:(i + 1) * 512] for i in range(10)]
            else:
                al_ = lambda name, shape, dt=F32: carve(shape, dt)
                bg = carve([128, 2 * T], F32)
                B.tmp = [bg[:, i * 512:(i + 1) * 512] for i in range(10)]
            B.H = []
            for par in range(2):
                h = BS()
                if par == 1 and z == 0:
                    wrf = wr[:, 0, :, :].rearrange("p k n -> p (k n)")
                    h.AR = wrf[:, 0:1024].rearrange("p (a b) -> p a b", b=512)
                    h.BK = wrf[:, 1024:2048].rearrange("p (a b) -> p a b", b=512)
                else:
                    h.AR = al_(f"AR{par}", [128, 2, 512], BF16)
                    h.BK = al_(f"BK{par}", [128, 2, 512], BF16)
                stk = (lambda name, shape, dt=F32: sb(name + "z1", shape, dt)[:]) if (par == 1 and z == 1) else al_
                h.tokm = stk(f"tokm{par}", [128, 8, 4, 64], BF16)
                h.MS = al_(f"MS{par}", [128, 8, 5, 64], BF16)
                h.Qa = stk(f"Qa{par}", [128, 8, 64], BF16)
                h.Qb = stk(f"Qb{par}", [128, 8, 64], BF16)
                h.egL = al_(f"egL{par}", [128, 8])
                B.H.append(h)
            B.PW = al_("PW", [128, 2, 8, 2, 64], BF16)
            B.W1b = al_("W1b", [128, 8, 64], BF16)
            B.W2b = al_("W2b", [128, 8, 64], BF16)
            B.IPhiT = al_("IPhiT", [128, 8, 64])
            B.PsiE = al_("PsiE", [128, 8, 64])
            B.RG = al_("RG", [128, 8, 64], BF16)
            B.MV = B.RG
            B.Tst2 = al_("Tst2", [128, 64])
            B.cur = [0]
            B.Tbs = al_("Tbs", [128, 9, 64], BF16)
            B.Tst = al_("Tst", [128, 64])
            B.S0p = al_("S0p", [128, 64])
            B.So = al_("So", [128, 64])
            B.sqb = al_("sqb", [128, 512], BF16)
            return B
        sets = [mkset(0), mkset(1)]
        inp2 = [stg, [carve([128, T], BF16) for _ in range(4)]]
        yT = sb("yT", [128, T])
        bonT = sb("bonT", [128, T], BF16)

        def group_stage(hp, z, g, groups, B, par, rT, kT, vT, tin, w2h, a2h):
            h = B.H[par]
            AR, BK, tokm, MS, PW, Qa, Qb, MV, W1b, W2b, IPhiT, PsiE, RG, egL, Tbs, Tst, Tst2, S0p, So, sqb = (
                h.AR, h.BK, h.tokm, h.MS, B.PW, h.Qa, h.Qb, B.MV, B.W1b, B.W2b, B.IPhiT, B.PsiE, B.RG, h.egL, B.Tbs, B.Tst, B.Tst2, B.S0p, B.So, B.sqb)
            B_cur = B.cur
            sw, al, cs, eg, ieg, egp, kkf, kkn, t1, t2 = B.tmp

            class TKZ:
                def __getitem__(self_, k):
                    if k in ("AR", "BK", "egL") or (isinstance(k, tuple) and k[0] in ("tokm", "MS", "Q")):
                        return tk[("z", z, par, k)]
                    if k == "MV":
                        k = "RG"
                    if k == "w2h":
                        return w2h[1]
                    if k == "a2h":
                        return a2h[1]
                    if k in ("rm", "rwdT", "radT", "w0", "a0", "kkp", "kap", "rkp", "bonesb", "bonesf", "mkf", "idp", "idpf", "idb", "idf",
                             "yT", "bonT", "out_oS") or (isinstance(k, tuple) and k[0] in ("pb", "ptb", "yT", "bonT")):
                        return tk[k]
                    if isinstance(k, tuple) and k[0] == "rkv":
                        return tin
                    return tk[("z", z, k)]
            tkz = TKZ()
            gs = slice(g * 512, (g + 1) * 512)
            _group_body(hp, z, g, groups, gs, tkz, AR, BK, tokm, MS, PW, Qa, Qb, MV, W1b, W2b, IPhiT, PsiE, RG, egL, Tbs, Tst, Tst2, B_cur, S0p, So, sqb,
                        sw, al, cs, eg, ieg, egp, kkf, kkn, t1, t2, rT, kT, vT, w2h[0], a2h[0])

        def _group_body(hp, z, g, groups, gs, tk, AR, BK, tokm, MS, PW, Qa, Qb, MV, W1b, W2b, IPhiT, PsiE, RG, egL, Tbs, Tst, Tst2, B_cur, S0p, So, sqb,
                        sw, al, cs, eg, ieg, egp, kkf, kkn, t1, t2, rT, kT, vT, w2h, a2h):
            bA, bB, bC, bT = pb[3 * z], pb[3 * z + 1], pb[3 * z + 2], ptb[z]
            tA, tB, tC, tT_ = tk["pb", 3 * z], tk["pb", 3 * z + 1], tk["pb", 3 * z + 2], tk["ptb", z]
            P.pe.matmul(bB[:, :], lhsT=w2h[:, z, :], rhs=rwdT[:, gs], start=True, stop=True,
                        r=[tk["w2h"], tk["rwdT"]], w=[tB])
            P.act.activation(out=sw, in_=bB[:, :], func=AF.Sigmoid, bias=w0_sb[:, z, hp:hp + 1], r=[tB, tk["w0"]], w=[tk["sw"]])
            P.pe.matmul(bC[:, :], lhsT=a2h[:, z, :], rhs=radT[:, gs], start=True, stop=True,
                        r=[tk["a2h"], tk["radT"]], w=[tC])
            P.act.activation(out=al, in_=bC[:, :], func=AF.Sigmoid, bias=a0_sb[:, z, hp:hp + 1], r=[tC, tk["a0"]], w=[tk["al"]])
            if z == 0:
                P.dve.tensor_tensor_scan(out=cs, data0=rm[:, 0:512], data1=sw, initial=0.0, op0=ALU.mult, op1=ALU.add,
                                         r=[tk["rm"], tk["sw"]], w=[tk["cs"]])
            else:
                P.dve.tensor_tensor_scan(out=cs[:, ::-1], data0=rm[:, 1:513][:, ::-1], data1=sw[:, ::-1], initial=0.0, op0=ALU.mult, op1=ALU.add,
                                         r=[tk["rm"], tk["sw"]], w=[tk["cs"]])
            P.act.activation(out=eg, in_=cs, func=AF.Exp, scale=-KAP, r=[tk["cs"]], w=[tk["eg"]])
            P.act.activation(out=ieg, in_=cs, func=AF.Exp, scale=KAP, r=[tk["cs"]], w=[tk["ieg"]])
            P.dve.tensor_tensor(out=t1, in0=cs, in1=sw, op=ALU.subtract, r=[tk["cs"], tk["sw"]], w=[tk["t1"]])
            P.act.activation(out=egp, in_=t1, func=AF.Exp, scale=-KAP, r=[tk["t1"]], w=[tk["egp"]])
            egv = eg.rearrange("p (c i) -> p c i", i=64)
            P.dve.tensor_copy(out=egL[:], in_=egv[:, :, 63 if z == 0 else 0], r=[tk["eg"]], w=[tk["egL"]])
            P.dve.tensor_scalar_mul(out=kkf, in0=kT[:, gs], scalar1=kk_sb[:, hp:hp + 1], r=[tk["rkv", 1], tk["kkp"]], w=[tk["kkf"]])
            P.dve.tensor_tensor(out=sqb[:], in0=kkf, in1=kkf, op=ALU.mult, r=[tk["kkf"]], w=[tk["sqb"]])
            P.pe.matmul(bB[:, :], lhsT=bonesb[:], rhs=sqb[:], start=True, stop=True, r=[tk["bonesb"], tk["sqb"]], w=[tB])
            P.dve.tensor_scalar_max(out=t1, in0=bB[:, :], scalar1=1e-24, r=[tB], w=[tk["t1"]])
            P.act.activation(out=t1, in_=t1, func=AF.Ln, r=[tk["t1"]], w=[tk["t1"]])
            P.act.activation(out=t1, in_=t1, func=AF.Exp, scale=-0.5, r=[tk["t1"]], w=[tk["t1"]])
            P.dve.tensor_tensor(out=kkn, in0=kkf, in1=t1, op=ALU.mult, r=[tk["kkf"], tk["t1"]], w=[tk["kkn"]])
            P.dve.tensor_scalar(out=t2, in0=al, scalar1=-1.0, scalar2=ka_sb[:, hp:hp + 1], op0=ALU.add, op1=ALU.mult,
                                r=[tk["al"], tk["kap"]], w=[tk["t2"]])
            P.dve.scalar_tensor_tensor(out=t2, in0=t2, scalar=1.0, in1=kT[:, gs], op0=ALU.add, op1=ALU.mult,
                                       r=[tk["t2"], tk["rkv", 1]], w=[tk["t2"]])
            P.dve.scalar_tensor_tensor(out=AR[:, 0, :], in0=kkn, scalar=-1.0, in1=egp, op0=ALU.mult, op1=ALU.mult,
                                       r=[tk["kkn"], tk["egp"]], w=[tk["AR"]])
            P.dve.tensor_tensor(out=AR[:, 1, :], in0=rT[:, gs], in1=eg, op=ALU.mult, r=[tk["rkv", 0], tk["eg"]], w=[tk["AR"]])
            P.dve.tensor_tensor(out=t1, in0=kkn, in1=al, op=ALU.mult, r=[tk["kkn"], tk["al"]], w=[tk["t1"]])
            P.dve.tensor_tensor(out=BK[:, 0, :], in0=t1, in1=ieg, op=ALU.mult, r=[tk["t1"], tk["ieg"]], w=[tk["BK"]])
            P.dve.tensor_tensor(out=BK[:, 1, :], in0=t2, in1=ieg, op=ALU.mult, r=[tk["t2"], tk["ieg"]], w=[tk["BK"]])
            P.dve.scalar_tensor_tensor(out=sqb[:], in0=t2, scalar=rk_sb[:, hp:hp + 1], in1=rT[:, gs], op0=ALU.mult, op1=ALU.mult,
                                       r=[tk["t2"], tk["rkp"], tk["rkv", 0]], w=[tk["sqb"]])
            P.pe.matmul(bC[:, :], lhsT=bonesb[:], rhs=sqb[:], start=True, stop=True, r=[tk["bonesb"], tk["sqb"]], w=[tC])
            P.dve.tensor_tensor(out=bonT[:, gs], in0=bonT[:, gs], in1=bC[:, :], op=ALU.add, r=[tk["bonT", g], tC], w=[tk["bonT", g]])

            tp = (None, (64, 64))

            def hs_(e):
                return slice(64 * e, 64 * e + 64)
            srcs = (AR[:, 0, :], BK[:, 0, :], BK[:, 1, :], vT[:, gs])
            srct = (tk["AR"], tk["BK"], tk["BK"], tk["rkv", 2])
            bTf = bT[:].bitcast(F32)
            for half in range(2):
                pt = bT
                for c4 in range(4):
                    c = half * 4 + c4
                    for wi in range(4):
                        for e in range(2):
                            kw = {} if e == 0 else {"tile_position": (64, 64)}
                            P.pe.transpose(out=pt[hs_(e), (c4 * 4 + wi) * 64:(c4 * 4 + wi + 1) * 64],
                                           in_=srcs[wi][hs_(e), c * 64:(c + 1) * 64], identity=idb[hs_(e), hs_(e)],
                                           r=[srct[wi], tk["idb"]], w=[tT_], **kw)
                P.act.copy(out=tokm[:, half * 4:(half + 1) * 4, :, :].rearrange("p c w i -> p (c w i)"), in_=pt[:, :],
                           r=[tT_], w=[tk["tokm", half]])
            for c in range(8):
                bank = (bC, bB)[c % 2]
                bt = (tC, tB)[c % 2]
                csl = slice(c * 64, (c + 1) * 64)
                for e in range(2):
                    kw = {} if e == 0 else {"tile_position": (64, 64)}
                    P.pe.matmul(bank[hs_(e), 0:128], lhsT=BK[hs_(e), 0, csl], rhs=AR[hs_(e), :, csl], start=True, stop=True,
                                r=[tk["AR"], tk["BK"]], w=[bt], **kw)
                    P.pe.matmul(bank[hs_(e), 128:256], lhsT=BK[hs_(e), 1, csl], rhs=AR[hs_(e), :, csl], start=True, stop=True,
                                r=[tk["AR"], tk["BK"]], w=[bt], **kw)
                    P.pe.matmul(bank[hs_(e), 256:320], lhsT=AR[hs_(e), 0, csl], rhs=BK[hs_(e), 0, csl], start=True, stop=True,
                                r=[tk["AR"], tk["BK"]], w=[bt], **kw)
                P.dve.tensor_tensor(out=MS[:, c, :, :].rearrange("p m i -> p (m i)"), in0=bank[:, 0:320],
                                    in1=mkf[:, z, :, :].rearrange("p m i -> p (m i)"), op=ALU.mult,
                                    r=[bt, tk["mkf"]], w=[tk["MS", c]])
            Qs = (Qa, Qb)
            for half in range(2):
                hsl = slice(half * 4, (half + 1) * 4)
                P.dve.tensor_tensor(out=Qa[:, hsl, :], in0=MS[:, hsl, 0, :], in1=idpf[:, hsl, :], op=ALU.add,
                                    r=[tk["MS", c] for c in range(half * 4, half * 4 + 4)] + [tk["idpf"]], w=[tk["Q", 0, half]])
            for lvl in range(5):
                for half in range(2):
                    hsl = slice(half * 4, (half + 1) * 4)
                    bank = (bB, bC)[half]
                    bt = (tB, tC)[half]
                    for c4 in range(4):
                        c = half * 4 + c4
                        if lvl == 0:
                            Z, ZT = MS[:, c, 4, :], MS[:, c, 0, :]
                        else:
                            Z, ZT = PW[:, (lvl - 1) % 2, c, 0, :], PW[:, (lvl - 1) % 2, c, 1, :]
                        for e in range(2):
                            kw = {} if e == 0 else {"tile_position": (64, 64)}
                            rt = [tk["MS", c]] if lvl == 0 else [tk["PW", (lvl - 1) % 2, half]]
                            P.pe.matmul(bank[hs_(e), (c4 * 2) * 64:(c4 * 2 + 1) * 64], lhsT=ZT[hs_(e), :], rhs=Z[hs_(e), :],
                                        start=True, stop=True, r=rt, w=[bt], **kw)
                            if lvl < 4:
                                P.pe.matmul(bank[hs_(e), (c4 * 2 + 1) * 64:(c4 * 2 + 2) * 64], lhsT=Z[hs_(e), :], rhs=ZT[hs_(e), :],
                                            start=True, stop=True, r=rt, w=[bt], **kw)
                    P.act.copy(out=PW[:, lvl % 2, hsl, :, :].rearrange("p c w i -> p (c w i)"), in_=bank[:, :],
                               r=[bt], w=[tk["PW", lvl % 2, half]])
                    qi, qo = Qs[lvl % 2], Qs[(lvl + 1) % 2]
                    qbank, qbt = bTf[:, half * 256:(half + 1) * 256], tT_
                    for c4 in range(4):
                        c = half * 4 + c4
                        for e in range(2):
                            kw = {} if e == 0 else {"tile_position": (64, 64)}
                            P.pe.matmul(qbank[hs_(e), c4 * 64:(c4 + 1) * 64], lhsT=PW[hs_(e), lvl % 2, c, 0, :], rhs=qi[hs_(e), c, :],
                                        start=True, stop=True, r=[tk["PW", lvl % 2, half], tk["Q", lvl % 2, half]], w=[qbt], **kw)
                    P.dve.tensor_tensor(out=qo[:, hsl, :].rearrange("p c i -> p (c i)"), in0=qi[:, hsl, :].rearrange("p c i -> p (c i)"),
                                        in1=qbank[:, 0:256], op=ALU.add, r=[tk["Q", lvl % 2, half], qbt], w=[tk["Q", (lvl + 1) % 2, half]])
            Q = Qs[1]
            for c in range(8):
                for e in range(2):
                    kw = {} if e == 0 else {"tile_position": (64, 64)}
                    P.pe.matmul(bA[hs_(e), c * 64:(c + 1) * 64], lhsT=MS[hs_(e), c, 2, :], rhs=tokm[hs_(e), c, 3, :],
                                start=True, stop=True, r=[tk["MS", c], tk["tokm", c // 4]], w=[tA], **kw)
            P.act.copy(out=MV[:].rearrange("p c i -> p (c i)"), in_=bA[:, :], r=[tA], w=[tk["MV"]])

            def tail_mm(fn, evac):
                for c in range(8):
                    for e in range(2):
                        kw = {} if e == 0 else {"tile_position": (64, 64)}
                        fn(c, e, slice(c * 64, (c + 1) * 64), kw)
                evac()
            tail_mm(lambda c, e, csl, kw: P.pe.matmul(bA[hs_(e), csl], lhsT=Q[hs_(e), c, :], rhs=tokm[hs_(e), c, 0, :], start=True, stop=True,
                                                      r=[tk["tokm", c // 4], tk["Q", 1, c // 4]], w=[tA], **kw),
                    lambda: P.act.copy(out=W1b[:].rearrange("p c i -> p (c i)"), in_=bA[:, :], r=[tA], w=[tk["W1b"]]))
            tail_mm(lambda c, e, csl, kw: P.pe.matmul(bA[hs_(e), csl], lhsT=Q[hs_(e), c, :], rhs=MV[hs_(e), c, :], start=True, stop=True,
                                                      r=[tk["Q", 1, c // 4], tk["MV"]], w=[tA], **kw),
                    lambda: P.act.copy(out=W2b[:].rearrange("p c i -> p (c i)"), in_=bA[:, :], r=[tA], w=[tk["W2b"]]))
            tail_mm(lambda c, e, csl, kw: P.pe.matmul(bA[hs_(e), csl], lhsT=W1b[hs_(e), c, :], rhs=tokm[hs_(e), c, 1, :], start=True, stop=True,
                                                      r=[tk["W1b"], tk["tokm", c // 4]], w=[tA], **kw),
                    lambda: P.dve.tensor_tensor(out=IPhiT[:].rearrange("p c i -> p (c i)"), in0=bA[:, :], in1=idpf[:].rearrange("p c i -> p (c i)"),
                                                op=ALU.add, r=[tA, tk["idpf"]], w=[tk["IPhiT"]]))

            def psi_mm(c, e, csl, kw):
                P.pe.matmul(bA[hs_(e), csl], lhsT=tokm[hs_(e), c, 2, :], rhs=tokm[hs_(e), c, 3, :], start=True, stop=False,
                            r=[tk["tokm", c // 4]], w=[tA], **kw)
                P.pe.matmul(bA[hs_(e), csl], lhsT=tokm[hs_(e), c, 1, :], rhs=W2b[hs_(e), c, :], start=False, stop=True,
                            r=[tk["tokm", c // 4], tk["W2b"]], w=[tA], **kw)

            def psi_ev():
                for c in range(8):
                    P.act.activation(out=PsiE[:, c, :], in_=bA[:, c * 64:(c + 1) * 64], func=AF.Copy, scale=egL[:, c:c + 1],
                                     r=[tA, tk["egL"]], w=[tk["PsiE"]])
            tail_mm(psi_mm, psi_ev)
            tail_mm(lambda c, e, csl, kw: P.pe.matmul(bA[hs_(e), csl], lhsT=W1b[hs_(e), c, :], rhs=MS[hs_(e), c, 1, :], start=True, stop=True,
                                                      r=[tk["W1b"], tk["MS", c]], w=[tA], **kw),
                    lambda: P.dve.tensor_tensor(out=RG[:].rearrange("p c i -> p (c i)"), in0=bA[:, :], in1=AR[:, 1, :], op=ALU.add,
                                                r=[tA, tk["AR"]], w=[tk["RG"]]))
            if g < 4:
                seqs_c = [list(range(8)) if z == 0 else list(range(7, -1, -1))]
                first_of_seq = (g == groups[0])
            else:
                seqs_c = [[0, 1, 2, 3], [4, 5, 6, 7]] if z == 0 else [[3, 2, 1, 0], [7, 6, 5, 4]]
                first_of_seq = True
            TT = (Tst, Tst2)
            for si_, corder in enumerate(seqs_c):
                if first_of_seq:
                    cur = 0
                    if g < 4:
                        P.q_sp.dma_start(out=S0p[:], in_=E.rS0[z, 2 * hp:2 * hp + 2].rearrange("e v k -> (e v) k"), w=[tk["S0p"]])
                        for e in range(2):
                            kw = {} if e == 0 else {"tile_position": (64, 64)}
                            P.pe.matmul(bA[hs_(e), 0:64], lhsT=S0p[hs_(e), :], rhs=idf[hs_(e), hs_(e)], start=True, stop=True,
                                        r=[tk["S0p"], tk["idf"]], w=[tA], **kw)
                        P.dve.tensor_copy(out=TT[0][:], in_=bA[:, 0:64], r=[tA], w=[tk["Tst", 0]])
                    else:
                        P.dve.memset(TT[0][:], 0.0, w=[tk["Tst", 0]])
                else:
                    cur = B_cur[0]
                for c in corder:
                    Tc, Tn = TT[cur], TT[1 - cur]
                    P.act.copy(out=Tbs[:, c, :], in_=Tc[:], r=[tk["Tst", cur]], w=[tk["Tbs", c]])
                    for e in range(2):
                        kw = {} if e == 0 else {"tile_position": (64, 64)}
                        P.pe.matmul(bA[hs_(e), 0:64], lhsT=IPhiT[hs_(e), c, :], rhs=Tc[hs_(e), :], start=True, stop=True,
                                    r=[tk["IPhiT"], tk["Tst", cur]], w=[tA], **kw)
                    P.dve.scalar_tensor_tensor(out=Tn[:], in0=bA[:, 0:64], scalar=egL[:, c:c + 1], in1=PsiE[:, c, :], op0=ALU.mult, op1=ALU.add,
                                               r=[tA, tk["egL"], tk["PsiE"]], w=[tk["Tst", 1 - cur]])
                    cur = 1 - cur
                B_cur[0] = cur
                if g == 4:
                    for e in range(2):
                        kw = {} if e == 0 else {"tile_position": (64, 64)}
                        P.pe.matmul(bA[hs_(e), 64:128], lhsT=TT[cur][hs_(e), :], rhs=idf[hs_(e), hs_(e)], start=True, stop=True,
                                    r=[tk["Tst", cur], tk["idf"]], w=[tA], **kw)
                    P.act.copy(out=So[:], in_=bA[:, 64:128], r=[tA], w=[tk["So"]])
                    P.q_sp.dma_start(out=E.oS[si_, z, 2 * hp:2 * hp + 2].rearrange("e v k -> (e v) k"), in_=So[:], r=[tk["So"]], w=[tk["out_oS"]])
            for c in range(8):
                csl = slice(c * 64, (c + 1) * 64)
                for e in range(2):
                    kw = {} if e == 0 else {"tile_position": (64, 64)}
                    P.pe.matmul(bA[hs_(e), csl], lhsT=Tbs[hs_(e), c, :], rhs=RG[hs_(e), c, :], start=True, stop=False,
                                r=[tk["Tbs", c], tk["RG"]], w=[tA], **kw)
                    P.pe.matmul(bA[hs_(e), csl], lhsT=W2b[hs_(e), c, :], rhs=MS[hs_(e), c, 1, :], start=False, stop=False,
                                r=[tk["W2b"], tk["MS", c]], w=[tA], **kw)
                    P.pe.matmul(bA[hs_(e), csl], lhsT=tokm[hs_(e), c, 3, :], rhs=MS[hs_(e), c, 3, :], start=False, stop=True,
                                r=[tk["tokm", c // 4], tk["MS", c]], w=[tA], **kw)
            P.dve.tensor_tensor(out=yT[:, gs], in0=yT[:, gs], in1=bA[:, :], op=ALU.add, r=[tk["yT", g], tA], w=[tk["yT", g]])

        for ih, hp in enumerate(pairs):
            ipar = (len(pairs) - 1 - ih) % 2
            rT, kT, vT, szT = inp2[ipar]
            tin = tk["inp2", ipar]
            for j in range(4):
                P.q_act.dma_start(out=inp2[ipar][j][:], in_=E.rscr[hp, j], r=[tk["rscr", hp]], w=[tin])
            for z in range(2):
                P.q_pool.dma_start(out=w2hh[z * 64:(z + 1) * 64, 0, z, :], in_=E.r_w2[z][:, hp * 128:(hp + 1) * 128], w=[tk["w2h", 0]])
                P.q_pool.dma_start(out=a2hh[z * 64:(z + 1) * 64, 0, z, :], in_=E.r_a2[z][:, hp * 128:(hp + 1) * 128], w=[tk["a2h", 0]])
            for g in range(NG):
                P.dve.memset(yT[:, g * 512:(g + 1) * 512], 0.0, w=[tk["yT", g]])
                P.dve.memset(bonT[:, g * 512:(g + 1) * 512], 0.0, w=[tk["bonT", g]])
            for z in range(2):
                groups = (0, 1, 2, 3, 4) if z == 0 else (3, 2, 1, 0, 4)
                for gi, g in enumerate(groups):
                    group_stage(hp, z, g, groups, sets[z], gi % 2, rT, kT, vT, tin, (w2hh[:, 0], tk["w2h", 0]), (a2hh[:, 0], tk["a2h", 0]))
            t1, t2 = sets[0].tmp[8], sets[0].tmp[9]
            sqb = sets[0].sqb
            if ih == len(pairs) - 1 and len(pairs) == 8:
                P.barrier()
                E.load_wout()
            for g in range(NG):
                gs = slice(g * 512, (g + 1) * 512)
                P.pe.matmul(pb[0][:, :], lhsT=bonesf[:], rhs=yT[:, gs], start=True, stop=True, r=[tk["bonesf"], tk["yT", g]], w=[tk["pb", 0]])
                P.dve.scalar_tensor_tensor(out=t1, in0=pb[0][:, :], scalar=-1.0 / 64, in1=yT[:, gs], op0=ALU.mult, op1=ALU.add,
                                           r=[tk["pb", 0], tk["yT", g]], w=[tk["z", 0, "t1"]])
                P.act.activation(out=t2, in_=t1, func=AF.Square, r=[tk["z", 0, "t1"]], w=[tk["z", 0, "t2"]])
                P.pe.matmul(pb[1][:, :], lhsT=bonesf[:], rhs=t2, start=True, stop=True, r=[tk["bonesf"], tk["z", 0, "t2"]], w=[tk["pb", 1]])
                P.dve.tensor_scalar(out=t2, in0=pb[1][:, :], scalar1=1.0 / 64, scalar2=LNX_EPS, op0=ALU.mult, op1=ALU.add,
                                    r=[tk["pb", 1]], w=[tk["z", 0, "t2"]])
                P.act.activation(out=t2, in_=t2, func=AF.Ln, r=[tk["z", 0, "t2"]], w=[tk["z", 0, "t2"]])
                P.act.activation(out=t2, in_=t2, func=AF.Exp, scale=-0.5, r=[tk["z", 0, "t2"]], w=[tk["z", 0, "t2"]])
                P.dve.tensor_tensor(out=t1, in0=t1, in1=t2, op=ALU.mult, r=[tk["z", 0, "t1"], tk["z", 0, "t2"]], w=[tk["z", 0, "t1"]])
                P.dve.tensor_scalar(out=t1, in0=t1, scalar1=lg_sb[:, hp:hp + 1], scalar2=lb_sb[:, hp:hp + 1], op0=ALU.mult, op1=ALU.add,
                                    r=[tk["z", 0, "t1"], tk["lgp"], tk["lbp"]], w=[tk["z", 0, "t1"]])
                P.dve.tensor_tensor(out=t2, in0=bonT[:, gs], in1=vT[:, gs], op=ALU.mult, r=[tk["bonT", g], tin], w=[tk["z", 0, "t2"]])
                P.dve.tensor_tensor(out=t1, in0=t1, in1=t2, op=ALU.add, r=[tk["z", 0, "t1"], tk["z", 0, "t2"]], w=[tk["z", 0, "t1"]])
                P.dve.tensor_tensor(out=sqb[:], in0=t1, in1=szT[:, gs], op=ALU.mult, r=[tk["z", 0, "t1"], tin], w=[tk["z", 0, "sqb"]])
                P.q_act.dma_start(out=E.catT[g, :, 8 + hp, :], in_=sqb[:], r=[tk["z", 0, "sqb"]], w=[tk["catT", g]])
    P.barrier()


def build(stage=99):
    nc = bass.Bass("TRN2", target_bir_lowering=False)
    P = Prog()
    tk = TK()

    def din(name, shape, dt=F32):
        return nc.dram_tensor(name, list(shape), dt, kind="ExternalInput").ap()

    def dout(name, shape, dt=F32):
        return nc.dram_tensor(name, list(shape), dt, kind="ExternalOutput").ap()

    x = din("x", [T, D])
    cv = din("cv", [128, KC, 2])
    w_ada = din("w_ada", [D, 3 * D])
    b_ada = din("b_ada", [128, 48])
    norm_g = din("norm_g", [128, KC])
    final_g = din("final_g", [1, D])
    w_in = din("w_in", [D, IN_COLS])
    w_out = din("w_out", [D, D])
    ident = din("ident", [128, 128])
    E = NS()
    E.w_in = w_in
    E.m_gate_b = din("m_gate_b", [1, 16])
    E.m_conv_w = din("m_conv_w", [128, KC, 3])
    E.m_conv_b = din("m_conv_b", [128, KC])
    E.m_ln_g = din("m_ln_g", [1, 1024])
    E.m0 = din("m0", [36, 1])
    E.mC0 = din("mC0", [2, 4, 256, 256])
    E.mn0 = din("mn0", [2, 4, 256])
    E.sel_d = din("sel", [36, 8, 128])
    E.maskbig_d = din("maskbig", [128, 2, 128])
    E.bones_d = din("bones", [128, 128])
    E.mk_d = din("mk", [128, 2, 5, 64])
    E.rm_d = din("rm", [128, 513])
    E.gmask_d = din("gmask", [128, 4])
    E.idp_d = din("idp", [128, 8, 64])
    E.r_mu = din("r_mu", [128, 26])
    E.r_w0 = din("r_w0", [128, 2, 8])
    E.r_a0 = din("r_a0", [128, 2, 8])
    E.r_k_k = din("r_k_k", [128, 8])
    E.r_k_a = din("r_k_a", [128, 8])
    E.r_r_k = din("r_r_k", [128, 8])
    E.r_ln_g = din("r_ln_g", [128, 8])
    E.r_ln_b = din("r_ln_b", [128, 8])
    E.r_w2 = din("r_w2", [2, 64, 1024])
    E.r_a2 = din("r_a2", [2, 64, 1024])
    E.rS0 = din("rS0", [2, 16, 64, 64])
    y = dout("y", [T, D])
    E.oS = dout("oS", [2, 2, 16, 64, 64])
    E.oC = dout("oC", [2, 2, 4, 256, 256])
    E.on = dout("on", [2, 2, 4, 256])
    E.om = dout("om", [36, 512])
    if "catT" in DEBUG:
        catT = dout("catT", [NG, 128, KC, 512], BF16)
    else:
        catT = nc.dram_tensor("catT", [NG, 128, KC, 512], BF16).ap()
    E.catT = catT
    E.rscr = nc.dram_tensor("rscr", [8, 4, 128, T], BF16).ap()

    st = contextlib.ExitStack()
    with st:
        def sb(name, shape, dt=F32):
            return st.enter_context(nc.sbuf_tensor(name, list(shape), dt))

        def ps(name, shape, dt=F32):
            return st.enter_context(nc.psum_tensor(name, list(shape), dt))

        hT = sb("hT", [128, KC, T], BF16)
        idf = sb("idf", [128, 128])
        idb = sb("idb", [128, 128], BF16)
        modv = sb("modv", [128, 48, 2])
        Amod = sb("Amod", [128, KC, 2])
        ng_sb = sb("ng_sb", [128, KC])
        bada_sb = sb("bada_sb", [128, 48])
        cv_sb = sb("cv_sb", [128, KC, 2])
        sT = sb("sT", [128, KC, 2], BF16)
        pb = [ps(f"pb{i}", [128, 512]) for i in range(6)]
        ptb = [ps(f"ptb{i}", [128, 1024], BF16) for i in range(2)]

        P.q_sp.dma_start(out=idf[:], in_=ident[:, :], w=[tk["idf"]])
        P.dve.tensor_copy(out=idb[:], in_=idf[:], r=[tk["idf"]], w=[tk["idb"]])
        P.q_sp.dma_start(out=cv_sb[:], in_=cv[:, :, :], w=[tk["cv"]])
        P.q_sp.dma_start(out=ng_sb[:], in_=norm_g[:, :], w=[tk["ng"]])
        P.q_sp.dma_start(out=bada_sb[:], in_=b_ada[:, :], w=[tk["bada"]])
        P.act.activation(out=sT[:], in_=cv_sb[:], func=AF.Silu, r=[tk["cv"]], w=[tk["sT"]])

        st0 = contextlib.ExitStack()
        if True:
            wab = [st0.enter_context(nc.sbuf_tensor(f"wab{i}", [128, 3 * D], BF16)) for i in range(2)]
            wv = w_ada.rearrange("(c p) n -> p c n", p=128)
            pmod = pb[0][:, 0:96].rearrange("p (f j) -> p f j", j=2)
            for kc in range(KC):
                b = wab[kc % 2]
                P.q_pool.dma_start(out=b[:], in_=wv[:, kc, :], w=[tk["wab", kc % 2]])
                for fb in range(48):
                    P.pe.matmul(pmod[:, fb, :], lhsT=b[:, fb * 128:(fb + 1) * 128], rhs=sT[:, kc, :],
                                start=(kc == 0 and fb == 0), stop=(kc == KC - 1 and fb == 47), skip_group_check=True,
                                r=[tk["wab", kc % 2], tk["sT"]], w=[tk["pb", 0]])
            for j in range(2):
                P.dve.tensor_tensor(out=modv[:, :, j], in0=pmod[:, :, j], in1=bada_sb[:], op=ALU.add,
                                    r=[tk["pb", 0], tk["bada"]], w=[tk["modv"]])
            for j in range(2):
                P.dve.scalar_tensor_tensor(out=Amod[:, :, j], in0=modv[:, 16:32, j], scalar=1.0, in1=ng_sb[:],
                                           op0=ALU.add, op1=ALU.mult, r=[tk["modv"], tk["ng"]], w=[tk["Amod"]])
        st1 = st0
        if True:
            xt = [st1.enter_context(nc.sbuf_tensor(f"xt{i}", [128, D], F32)) for i in range(4)]
            xn = [st1.enter_context(nc.sbuf_tensor(f"xn{i}", [128, D], BF16)) for i in range(12)]
            junk = st1.enter_context(nc.sbuf_tensor("junk", [128, D], BF16))
            ssq = st1.enter_context(nc.sbuf_tensor("ssq", [128, NT], F32))
            rstd = st1.enter_context(nc.sbuf_tensor("rstd", [128, NT], F32))
            ev = 0
            for g in range(NG):
                j = 0 if g < 4 else 1
                for t4 in range(4):
                    tt = g * 4 + t4
                    xb_ = xt[tt % 4]
                    (P.q_sp if tt % 2 == 0 else P.q_act).dma_start(out=xb_[:], in_=x[tt * 128:(tt + 1) * 128, :],
                                                                   w=[tk["xt", tt % 4]])
                    P.act.activation(out=junk[:], in_=xb_[:], func=AF.Square, accum_out=ssq[:, tt:tt + 1],
                                     r=[tk["xt", tt % 4]], w=[tk["junk"], tk["ssq", tt]])
                    P.dve.tensor_scalar(out=rstd[:, tt:tt + 1], in0=ssq[:, tt:tt + 1], scalar1=1.0 / D, scalar2=EPS,
                                        op0=ALU.mult, op1=ALU.add, r=[tk["ssq", tt]], w=[tk["rstd", tt]])
                    P.act.sqrt(out=rstd[:, tt:tt + 1], in_=rstd[:, tt:tt + 1], r=[tk["rstd", tt]], w=[tk["rstd", tt]])
                    P.dve.reciprocal(out=rstd[:, tt:tt + 1], in_=rstd[:, tt:tt + 1], r=[tk["rstd", tt]], w=[tk["rstd", tt]])
                    P.act.activation(out=xn[tt % 12][:], in_=xb_[:], func=AF.Copy, scale=rstd[:, tt:tt + 1],
                                     r=[tk["xt", tt % 4], tk["rstd", tt]], w=[tk["xn", tt % 12]])
                for fc in range(KC):
                    pt = ptb[fc % 2]
                    for t4 in range(4):
                        P.pe.transpose(out=pt[:, t4 * 128:(t4 + 1) * 128], in_=xn[(g * 4 + t4) % 12][:, fc * 128:(fc + 1) * 128],
                                       identity=idb[:], r=[tk["xn", (g * 4 + t4) % 12], tk["idb"]], w=[tk["ptb", fc % 2]])
                    dst = hT[:, fc, g * 512:(g + 1) * 512]
                    if ev % 2 == 0:
                        P.act.activation(out=dst, in_=pt[:, 0:512], func=AF.Identity, scale=Amod[:, fc, j:j + 1],
                                         bias=modv[:, fc, j:j + 1], r=[tk["ptb", fc % 2], tk["Amod"], tk["modv"]],
                                         w=[tk["hT", g]])
                    else:
                        P.dve.tensor_scalar(out=dst, in0=pt[:, 0:512], scalar1=Amod[:, fc, j:j + 1],
                                            scalar2=modv[:, fc, j:j + 1], op0=ALU.mult, op1=ALU.add,
                                            r=[tk["ptb", fc % 2], tk["Amod"], tk["modv"]], w=[tk["hT", g]])
                    ev += 1

        if "hT" in DEBUG:
            dbg2 = dout("dbg_mod", [128, 48, 2])
            P.q_sp.dma_start(out=dbg2[:, :, :], in_=modv[:], r=[tk["modv"]], w=[tk["dbgout"]])
            dbg3 = dout("dbg_A", [128, KC, 2])
            P.q_sp.dma_start(out=dbg3[:, :, :], in_=Amod[:], r=[tk["Amod"]], w=[tk["dbgout"]])

            dbg = dout("dbg_hT", [128, KC, T], BF16)
            P.q_sp.dma_start(out=dbg[:, :, :], in_=hT[:], r=[tk["hT", g] for g in range(NG)], w=[tk["dbgout"]])

        P.barrier()
        st0.close()
        E.hT, E.idf, E.idb, E.pb, E.ptb = hT, idf, idb, pb, ptb
        wout_loaded = [False]

        def load_wout():
            wo_ = hT[:, :, 0:D]
            wov = w_out.rearrange("(c p) n -> p c n", p=128)
            for q4 in range(4):
                P.q_pool.dma_start(out=wo_[:, q4 * 4:(q4 + 1) * 4, :], in_=wov[:, q4 * 4:(q4 + 1) * 4, :],
                                   r=[], w=[tk["hT", g] for g in range(NG)])
            wout_loaded[0] = True
        E.load_wout = load_wout
        if stage >= 3:
            phase_mlstm(nc, P, tk, E, heads=DEBUG.get("heads", (0, 1, 2, 3)))
        if stage >= 4:
            phase_rwkv(nc, P, tk, E, pairs=DEBUG.get("pairs", range(8)))
        with contextlib.ExitStack() as st4:
            onesf = st4.enter_context(nc.sbuf_tensor("onesf", [128, 128], F32))
            P.dve.memset(onesf[:], 1.0, w=[tk["onesf"]])
            gate_bc = st4.enter_context(nc.sbuf_tensor("gate_bc", [128, 2, D], F32))
            fg_bc = st4.enter_context(nc.sbuf_tensor("fg_bc", [128, D], F32))
            P.q_act.dma_start(out=fg_bc[:], in_=final_g.partition_broadcast(128), w=[tk["fg"]])
            dg = [st4.enter_context(nc.sbuf_tensor(f"dg{i}", [128, 128], F32)) for i in range(2)]
            k = 0
            for j in range(2):
                for q4 in range(4):
                    bank = pb[q4 % 2]
                    for c4 in range(4):
                        ch = q4 * 4 + c4
                        d_ = dg[k % 2]
                        P.dve.tensor_scalar_mul(out=d_[:], in0=idf[:], scalar1=modv[:, 32 + ch, j:j + 1],
                                                r=[tk["idf"], tk["modv"]], w=[tk["dg", k % 2]])
                        P.pe.matmul(bank[:, c4 * 128:(c4 + 1) * 128], lhsT=onesf[:], rhs=d_[:], start=True, stop=True,
                                    r=[tk["dg", k % 2], tk["onesf"]], w=[tk["pb", q4 % 2]])
                        k += 1
                    P.act.copy(out=gate_bc[:, j, q4 * 512:(q4 + 1) * 512], in_=bank[:, :],
                               r=[tk["pb", q4 % 2]], w=[tk["gate_bc"]])
            wo = hT[:, :, 0:D]
            hT_all = [tk["hT", g] for g in range(NG)]
            if not wout_loaded[0]:
                E.load_wout()
            cg = [st4.enter_context(nc.sbuf_tensor(f"cg{i}", [128, KC, 512], BF16)) for i in range(2)]
            xt = [st4.enter_context(nc.sbuf_tensor(f"x4_{i}", [128, D], F32)) for i in range(2)]
            xo = st4.enter_context(nc.sbuf_tensor("xo", [128, D], F32))
            yo = [st4.enter_context(nc.sbuf_tensor(f"yo{i}", [128, D], F32)) for i in range(2)]
            junk4 = st4.enter_context(nc.sbuf_tensor("junk4", [128, D], BF16))
            ss4 = st4.enter_context(nc.sbuf_tensor("ss4", [128, NT], F32))
            for g in range(NG):
                j = 0 if g < 4 else 1
                cb = cg[g % 2]
                if stage >= 2:
                    P.q_sp.dma_start(out=cb[:], in_=catT[g], r=[tk["catT", g]], w=[tk["cg", g % 2]])
                for t4 in range(4):
                    tt = g * 4 + t4
                    xb_ = xt[tt % 2]
                    P.q_act.dma_start(out=xb_[:], in_=x[tt * 128:(tt + 1) * 128, :], w=[tk["x4", tt % 2]])
                    if stage >= 2:
                        for db in range(4):
                            bank = pb[2 + db]
                            for cc in range(KC):
                                P.pe.matmul(bank[:, :], lhsT=cb[:, cc, t4 * 128:(t4 + 1) * 128],
                                            rhs=wo[:, cc, db * 512:(db + 1) * 512], start=(cc == 0), stop=(cc == KC - 1),
                                            r=[tk["cg", g % 2]] + hT_all, w=[tk["pb", 2 + db]])
                            P.dve.tensor_tensor(out=xo[:, db * 512:(db + 1) * 512], in0=bank[:, :],
                                                in1=gate_bc[:, j, db * 512:(db + 1) * 512], op=ALU.mult,
                                                r=[tk["pb", 2 + db], tk["gate_bc"]], w=[tk["xo"]])
                        P.dve.tensor_tensor(out=xo[:], in0=xo[:], in1=xb_[:], op=ALU.add,
                                            r=[tk["xo"], tk["x4", tt % 2]], w=[tk["xo"]])
                        src = xo
                        srct = tk["xo"]
                    else:
                        src = xb_
                        srct = tk["x4", tt % 2]
                    P.act.activation(out=junk4[:], in_=src[:], func=AF.Square, accum_out=ss4[:, tt:tt + 1],
                                     r=[srct], w=[tk["junk4"], tk["ss4", tt]])
                    P.dve.tensor_scalar(out=ss4[:, tt:tt + 1], in0=ss4[:, tt:tt + 1], scalar1=1.0 / D, scalar2=EPS,
                                        op0=ALU.mult, op1=ALU.add, r=[tk["ss4", tt]], w=[tk["ss4", tt]])
                    P.act.sqrt(out=ss4[:, tt:tt + 1], in_=ss4[:, tt:tt + 1], r=[tk["ss4", tt]], w=[tk["ss4", tt]])
                    P.dve.reciprocal(out=ss4[:, tt:tt + 1], in_=ss4[:, tt:tt + 1], r=[tk["ss4", tt]], w=[tk["ss4", tt]])
                    yb = yo[tt % 2]
                    P.dve.scalar_tensor_tensor(out=yb[:], in0=src[:], scalar=ss4[:, tt:tt + 1], in1=fg_bc[:],
                                               op0=ALU.mult, op1=ALU.mult, r=[srct, tk["ss4", tt], tk["fg"]],
                                               w=[tk["yo", tt % 2]])
                    P.q_sp.dma_start(out=y[tt * 128:(tt + 1) * 128, :], in_=yb[:], r=[tk["yo", tt % 2]], w=[tk["y"]])

        finals = [tk["y"], tk["out_om"], tk["out_oC"], tk["out_oS"]] + [tk["catT", g] for g in range(NG)]
        if "dbgout" in tk.d:
            finals.append(tk["dbgout"])
        P.emit(nc, st, final_waits=finals)
    return nc, P


def silu_layout(v):
    return np.ascontiguousarray(v.reshape(16, 128).T)


def consts():
    sel = np.zeros((36, 8, 128), np.float32)
    for ri, q in enumerate(ROWS):
        sel[q, ri, :] = 1.0
    si, ti = np.meshgrid(np.arange(128), np.arange(128), indexing="ij")
    maskbig = np.zeros((128, 2, 128), np.float32)
    maskbig[:, 0, :] = np.where(si <= ti, 0.0, 1e30)
    maskbig[:, 1, :] = np.where(si >= ti, 0.0, 1e30)
    bones = np.zeros((128, 128), np.float32)
    bones[:64, :64] = 1.0
    bones[64:, 64:] = 1.0
    r64, c64 = np.meshgrid(np.arange(64), np.arange(64), indexing="ij")
    mk = np.zeros((128, 2, 5, 64), np.float32)
    for half in range(2):
        hs = slice(64 * half, 64 * half + 64)
        mk[hs, 0, 0] = r64 < c64
        mk[hs, 0, 1] = r64 <= c64
        mk[hs, 0, 2] = r64 < c64
        mk[hs, 0, 3] = r64 <= c64
        mk[hs, 0, 4] = c64 < r64
        mk[hs, 1, 0] = r64 > c64
        mk[hs, 1, 1] = r64 >= c64
        mk[hs, 1, 2] = r64 > c64
        mk[hs, 1, 3] = r64 >= c64
        mk[hs, 1, 4] = c64 > r64
    rm = np.ones((128, 513), np.float32)
    rm[:, 0::64] = 0.0
    gmask = np.zeros((128, 4), np.float32)
    gmask[np.arange(128), np.arange(128) % 4] = 1.0
    idp = np.zeros((128, 8, 64), np.float32)
    idp[np.arange(128), :, np.arange(128) % 64] = 1.0
    return {"ident": np.eye(128, dtype=np.float32), "sel": sel, "maskbig": maskbig, "bones": bones, "mk": mk, "rm": rm,
            "gmask": gmask, "idp": idp}


def fm(v, nblk):
    return np.ascontiguousarray(v.reshape(nblk, 128).T)


def make_in_maps(inp, cores=range(8)):
    f = lambda a: np.asarray(a, dtype=np.float32)
    x_prompt, x_sample = f(inp["x_prompt"]), f(inp["x_sample"])
    c, c_ctx = f(inp["c"]), f(inp["c_ctx"])
    cst = consts()
    shared = {
        "w_ada": f(inp["w_ada"])[0],
        "b_ada": fm(f(inp["b_ada"])[0], 48),
        "norm_g": fm(f(inp["norm_g"])[0], 16),
        "final_g": f(inp["final_g"]).reshape(1, D),
        "w_in": f(inp["w_in"])[0],
        "w_out": f(inp["w_out"])[0],
        "m_gate_b": f(inp["m_gate_b"])[0].reshape(1, 16),
        "m_conv_w": np.ascontiguousarray(f(inp["m_conv_w"])[0].reshape(3, 16, 128).transpose(2, 1, 0)),
        "m_conv_b": fm(f(inp["m_conv_b"])[0], 16),
        "m_ln_g": f(inp["m_ln_g"])[0].reshape(1, 1024),
        "r_mu": fm(f(inp["r_mu"])[0], 26),
        "r_w0": np.ascontiguousarray(f(inp["r_w0"])[0].reshape(2, 8, 128).transpose(2, 0, 1)),
        "r_a0": np.ascontiguousarray(f(inp["r_a0"])[0].reshape(2, 8, 128).transpose(2, 0, 1)),
        "r_k_k": fm(f(inp["r_k_k"])[0], 8),
        "r_k_a": fm(f(inp["r_k_a"])[0], 8),
        "r_r_k": fm(f(inp["r_r_k"])[0].reshape(-1), 8),
        "r_ln_g": fm(f(inp["r_ln_g"])[0], 8),
        "r_ln_b": fm(f(inp["r_ln_b"])[0], 8),
        "r_w2": f(inp["r_w2"])[0],
        "r_a2": f(inp["r_a2"])[0],
    }
    shared.update(cst)
    sm = f(inp["state_mlstm_m"])
    in_maps = []
    for b in cores:
        xc = np.concatenate([x_sample[b], x_prompt[2 * b], x_prompt[2 * b + 1]], axis=0)
        cvv = np.stack([fm(c[b], 16), fm(c_ctx, 16)], axis=-1)
        m0 = np.zeros((36, 1), np.float32)
        m0[0:4, 0] = sm[b, 0, 0]
        m0[32:36, 0] = sm[b, 0, 1]
        d = dict(shared)
        d.update({
            "x": np.ascontiguousarray(xc),
            "cv": np.ascontiguousarray(cvv),
            "m0": m0,
            "mC0": np.ascontiguousarray(f(inp["state_mlstm_C"])[b, 0]),
            "mn0": np.ascontiguousarray(f(inp["state_mlstm_n"])[b, 0]),
            "rS0": np.ascontiguousarray(f(inp["state_rwkv_S"])[b, 0]),
        })
        in_maps.append(d)
    return in_maps


def kernel(**inp):
    nc, P = build()
    in_maps = make_in_maps(inp)
    res = run_bass_kernel_spmd(nc, in_maps, core_ids=list(range(8)))
    R = res.results
    y_prompt = np.zeros((16, 256, D), np.float32)
    y_sample = np.zeros((8, 2048, D), np.float32)
    nC = np.zeros((16, 1, 2, 4, 256, 256), np.float32)
    nn = np.zeros((16, 1, 2, 4, 256), np.float32)
    nm = np.zeros((16, 1, 2, 4), np.float32)
    nS = np.zeros((16, 1, 2, 16, 64, 64), np.float32)
    for b in range(8):
        yy = R[b]["y"]
        y_sample[b] = yy[0:2048]
        y_prompt[2 * b] = yy[2048:2304]
        y_prompt[2 * b + 1] = yy[2304:2560]
        for pi in range(2):
            nC[2 * b + pi, 0] = R[b]["oC"][pi]
            nn[2 * b + pi, 0] = R[b]["on"][pi]
            nS[2 * b + pi, 0] = R[b]["oS"][pi]
            om = R[b]["om"]
            nm[2 * b + pi, 0, 0] = om[0:4, pi * 256 + 255]
            nm[2 * b + pi, 0, 1] = om[32:36, pi * 256]
    return y_prompt, y_sample, nC, nn, nm, nS
```

```python
import contextlib
import numpy as np
import concourse.bass as bass
import concourse.mybir as mybir
from concourse.bass_utils import run_bass_kernel_spmd

F32 = mybir.dt.float32
BF16 = mybir.dt.bfloat16
AF = mybir.ActivationFunctionType
ALU = mybir.AluOpType
AX = mybir.AxisListType

N_DMA_SEMS = 8


class Tok:
    __slots__ = ("name", "w", "r")

    def __init__(self, name=""):
        self.name = name
        self.w = None
        self.r = {}


class Eng:
    def __init__(self, name, kind, issuer=None):
        self.name = name
        self.kind = kind
        self.issuer = issuer
        self.ops = []
        self.count = 0
        self.ninst = 0
        self.waited = {}
        self.sems = None


class EW:
    def __init__(self, prog, name):
        self._p = prog
        self._n = name

    def __getattr__(self, meth):
        def rec(*args, r=(), w=(), **kw):
            import sys as _s
            ln = _s._getframe(1).f_lineno

            def fn(e):
                inst = getattr(e, meth)(*args, **kw)
                if DEBUG.get("trace_inst"):
                    try:
                        DEBUG.setdefault("inst_lines", {})[inst.ins.name] = (meth, ln)
                    except Exception:
                        pass
                return inst
            fn._ln = ln
            self._p.op(self._n, fn, reads=r, writes=w, dur=est_dur(self._n, meth, args, kw))
        return rec


def _fsize(ap):
    try:
        n = 1
        for d in ap.shape[1:]:
            n *= int(d)
        return n
    except Exception:
        return 256


def est_dur(engname, meth, args, kw):
    out = kw.get("out", args[0] if args else None)
    if engname == "pe":
        if meth == "transpose":
            return 0.12
        rhs = kw.get("rhs")
        n = _fsize(rhs) if rhs is not None else 128
        return DEBUG.get("pe_fix", 0.005) + max(n, 64) / 2300.0
    if engname.startswith("q_"):
        n = _fsize(out) if out is not None else 1024
        try:
            npart = int(out.shape[0])
        except Exception:
            npart = 128
        return 2.0 + npart * n * 2.0 / 150e3
    n = _fsize(out) if out is not None else 256
    if engname == "pool":
        return 0.3 + n / 500.0
    return DEBUG.get("ev_fix", 0.22) + n / 960.0


class Prog:
    def __init__(self):
        self.engs = {}
        for n in ("pe", "dve", "act", "pool", "sp"):
            self.engs[n] = Eng(n, "c")
        for n, iss in (("q_sp", "sp"), ("q_act", "act"), ("q_pool", "pool")):
            self.engs[n] = Eng(n, "d", iss)
        self.n_ops = 0
        self.sched = bool(DEBUG.get("sched", True))
        self.seg = []
        self.pending = {n: [] for n in ("pe", "dve", "act", "pool", "sp")}
        self.pe = EW(self, "pe")
        self.dve = EW(self, "dve")
        self.act = EW(self, "act")
        self.pool = EW(self, "pool")
        self.q_sp = EW(self, "q_sp")
        self.q_act = EW(self, "q_act")
        self.q_pool = EW(self, "q_pool")

    def _deps(self, eng, reads, writes):
        deps = {}

        def add(e, s, same_ok, ni=None):
            if e is eng and eng.kind == "c":
                if not same_ok:
                    return
                if ni is not None and eng.ninst - ni >= 3:
                    return
            k = (e.name, (s - 1) % N_DMA_SEMS if e.kind == "d" else 0)
            if deps.get(k, 0) < s:
                deps[k] = s

        for t in reads:
            if t.w is not None:
                add(t.w[0], t.w[1], True, t.w[2])
        for t in writes:
            if t.w is not None:
                add(t.w[0], t.w[1], False)
            for e, s in t.r.items():
                add(e, s, False)
        return deps

    def op(self, engname, fn, reads=(), writes=(), dur=0.5):
        if self.sched:
            self.seg.append((engname, fn, tuple(reads), tuple(writes), dur))
            return
        self._op(engname, fn, reads, writes)

    def flush(self):
        seg = self.seg
        self.seg = []
        n = len(seg)
        if n == 0:
            return
        HOP = DEBUG.get("hop_big", DEBUG.get("hop", 1.0)) if n > 40000 else DEBUG.get("hop", 1.0)
        preds = [None] * n
        lastw = {}
        readers = {}
        for i, (en, fn, rd, wr, du) in enumerate(seg):
            ps_ = set()
            for t in rd:
                j = lastw.get(id(t))
                if j is not None:
                    ps_.add(j)
            for t in wr:
                j = lastw.get(id(t))
                if j is not None:
                    ps_.add(j)
                for j in readers.get(id(t), ()):
                    ps_.add(j)
            ps_.discard(i)
            preds[i] = ps_
            for t in wr:
                lastw[id(t)] = i
                readers[id(t)] = []
            for t in rd:
                readers.setdefault(id(t), []).append(i)
        succs = [[] for _ in range(n)]
        for i in range(n):
            for j in preds[i]:
                succs[j].append(i)
        issuer_ = {en: (e.issuer if e.kind == "d" else en) for en, e in self.engs.items()}
        mark = [False] * n
        last_on = {}
        for i, (en, fn, rd, wr, du) in enumerate(seg):
            if self.engs[en].kind == "d":
                mark[i] = True
                continue
            last_on[en] = i
            wset = set(id(t) for t in wr)
            for j in succs[i]:
                enj = seg[j][0]
                if self.engs[enj].kind == "d" or enj != en:
                    mark[i] = True
                    break
                if any(id(t) in wset for t in seg[j][2]):
                    mark[i] = True
                    break
        for en, i in last_on.items():
            mark[i] = True
        prio = [0.0] * n
        for i in range(n - 1, -1, -1):
            m = 0.0
            for k in succs[i]:
                if prio[k] > m:
                    m = prio[k]
            prio[i] = seg[i][4] + 0.25 + m
        import heapq
        issuer = {}
        for en, e in self.engs.items():
            issuer[en] = e.issuer if e.kind == "d" else en
        npred = [len(p) for p in preds]
        ready_t = [0.0] * n
        ready = {k: [] for k in ("pe", "dve", "act", "pool", "sp")}
        free_t = {k: 0.0 for k in ready}
        for i in range(n):
            if npred[i] == 0:
                heapq.heappush(ready[issuer[seg[i][0]]], (-prio[i], i))
        order = []
        done = 0
        WINDOW = 3000
        while done < n:
            best = None
            for k, hp_ in ready.items():
                if not hp_:
                    continue
                cand = None
                tmpl = []
                cnt = 0
                while hp_ and cnt < DEBUG.get("cand", 8):
                    pr, i = heapq.heappop(hp_)
                    tmpl.append((pr, i))
                    cnt += 1
                    st_ = max(free_t[k], ready_t[i])
                    if cand is None or st_ < cand[0] - 1e-9:
                        cand = (st_, pr, i)
                for it in tmpl:
                    heapq.heappush(hp_, it)
                if cand is not None and (best is None or cand[0] < best[0]):
                    best = (cand[0], k, cand[2])
            st_, k, i = best
            hp_ = ready[k]
            hp_.remove((-prio[i], i))
            heapq.heapify(hp_)
            en, fn, rd, wr, du = seg[i]
            if self.engs[en].kind == "d":
                free_t[k] = st_ + 0.08
                fin = st_ + du
            else:
                free_t[k] = st_ + du
                fin = st_ + du
            order.append((st_, i))
            if DEBUG.get("why") is not None:
                fin_t = DEBUG.setdefault("_fin", {})
                fin_t[i] = fin
                idle = st_ - prev_free.get(k, 0.0) if (prev_free := DEBUG.setdefault("_pf", {})) is not None else 0.0
                if k == DEBUG["why"] and preds[i] and ready_t[i] > prev_free.get(k, 0.0) + 1e-6:
                    j = max(preds[i], key=lambda q: fin_t.get(q, 0.0))
                    key = (getattr(seg[j][1], "_ln", 0), seg[j][0], getattr(fn, "_ln", 0))
                    DEBUG.setdefault("_blame", {})[key] = DEBUG.setdefault("_blame", {}).get(key, 0.0) + (st_ - max(prev_free.get(k, 0.0), 0.0))
                prev_free[k] = free_t[k]
            done += 1
            for j in succs[i]:
                npred[j] -= 1
                rt = fin + (HOP if issuer[seg[j][0]] != k or self.engs[en].kind == "d" else DEBUG.get("same", 0.0))
                if rt > ready_t[j]:
                    ready_t[j] = rt
                if npred[j] == 0:
                    heapq.heappush(ready[issuer[seg[j][0]]], (-prio[j], j))
        order.sort()
        self.sim_time = getattr(self, "sim_time", 0.0) + max(free_t.values())
        last_emit = {}
        for st_, i in order:
            if self.engs[seg[i][0]].kind == "c":
                last_emit[seg[i][0]] = i
        for i in last_emit.values():
            mark[i] = True
        for st_, i in order:
            en, fn, rd, wr, du = seg[i]
            self._op(en, fn, rd, wr, marked=mark[i])

    def _op(self, engname, fn, reads=(), writes=(), marked=True):
        eng = self.engs[engname]
        if eng.kind == "d":
            return self._dma(eng, fn, reads, writes)
        deps = self._deps(eng, reads, writes)
        waits = []
        for k, s in deps.items():
            en = k[0]
            if en != eng.name and eng.waited.get(k, 0) >= s:
                continue
            eng.waited[k] = max(eng.waited.get(k, 0), s)
            waits.append((en, s))
        if self.pending[eng.name]:
            waits = self.pending[eng.name] + waits
            self.pending[eng.name] = []
        if marked:
            eng.count += 1
            seq = eng.count
        else:
            seq = eng.count + 1
        eng.ninst += 1
        eng.ops.append((waits, fn, None, marked))
        for t in writes:
            t.w = (eng, seq, eng.ninst)
            t.r = {}
        for t in reads:
            if t.r.get(eng, 0) < seq:
                t.r[eng] = seq
        self.n_ops += 1
        return seq

    def _dma(self, q, fn, reads, writes):
        iss = self.engs[q.issuer]
        deps = self._deps(q, reads, writes)
        waits = []
        q.count += 1
        seq = q.count
        if seq > N_DMA_SEMS:
            k = (q.name, (seq - 1) % N_DMA_SEMS)
            deps[k] = max(deps.get(k, 0), seq - N_DMA_SEMS)
        for k, s in deps.items():
            en = k[0]
            if en == iss.name:
                waits.append((en, s))
                continue
            if iss.waited.get(k, 0) >= s:
                continue
            iss.waited[k] = s
            waits.append((en, s))
        if self.pending[iss.name]:
            waits = self.pending[iss.name] + waits
            self.pending[iss.name] = []
        iss.ninst += 1
        iss.ops.append((waits, fn, (q.name, seq), True))
        for t in writes:
            t.w = (q, seq, 0)
            t.r = {}
        for t in reads:
            t.r[q] = seq
        self.n_ops += 1
        return seq

    def barrier(self):
        self.flush()
        ws = []
        for e in self.engs.values():
            if e.count == 0:
                continue
            if e.kind == "c":
                ws.append((e.name, e.count))
            else:
                for k in range(max(1, e.count - N_DMA_SEMS + 1), e.count + 1):
                    ws.append((e.name, k))
        for n in self.pending:
            self.pending[n] = [w for w in ws if w[0] != n]
            eng = self.engs[n]
            for (en, sq) in ws:
                e = self.engs[en]
                k = (en, (sq - 1) % N_DMA_SEMS if e.kind == "d" else 0)
                eng.waited[k] = max(eng.waited.get(k, 0), sq)

    def sem_ref(self, en, s):
        e = self.engs[en]
        if e.kind == "c":
            return e.sems[0], s
        slot = (s - 1) % N_DMA_SEMS
        return e.sems[slot], 16 * ((s - 1) // N_DMA_SEMS + 1)

    def emit(self, nc, st, final_waits=()):
        self.flush()
        for e in self.engs.values():
            n = 1 if e.kind == "c" else N_DMA_SEMS
            e.sems = [st.enter_context(nc.semaphore(f"s_{e.name}_{i}")) for i in range(n)]
        fin = []
        seen = set()
        for t in final_waits:
            if t.w is None:
                continue
            e, s = t.w[0], t.w[1]
            if e.kind == "d":
                for k in range(max(1, e.count - N_DMA_SEMS + 1), e.count + 1):
                    if (e.name, k) not in seen:
                        seen.add((e.name, k))
                        fin.append((e.name, k))
            elif (e.name, s) not in seen:
                seen.add((e.name, s))
                fin.append((e.name, s))
        block = st.enter_context(nc.Block())
        prog = self

        def replay(engname):
            def body(engobj):
                e = prog.engs[engname]
                own_sem = e.sems[0]
                for waits, fn, dma, marked in e.ops:
                    for en, s in waits:
                        sem, val = prog.sem_ref(en, s)
                        engobj.wait_ge(sem, val)
                    inst = fn(engobj)
                    if dma is None:
                        if marked:
                            inst.then_inc(own_sem, 1)
                    else:
                        sem, val = prog.sem_ref(dma[0], dma[1])
                        inst.then_inc(sem, 16)
                if engname == "sp":
                    for en, s in fin:
                        sem, val = prog.sem_ref(en, s)
                        engobj.wait_ge(sem, val)
            return body

        block.tensor(replay("pe"))
        block.vector(replay("dve"))
        block.scalar(replay("act"))
        block.gpsimd(replay("pool"))
        block.sync(replay("sp"))


class TK:
    def __init__(self):
        self.d = {}

    def __getitem__(self, k):
        t = self.d.get(k)
        if t is None:
            t = self.d[k] = Tok(str(k))
        return t


D = 2048
KC = 16
T = 2560
NT = 20
NG = 5
SEQS = ((0, 2048, True), (2048, 256, False), (2304, 256, False))
EPS = 1e-6
IN_COLS = 9488
C_MQ, C_MK, C_MV, C_MO, C_MZ, C_MG, C_RZ, C_RS = 0, 1024, 2048, 3072, 4096, 5120, 5136, 6160

DEBUG = {}


class NS:
    pass


NEG = -1.0e30
ROWS = (0, 1, 2, 3, 32, 33, 34, 35)


def phase_mlstm(nc, P, tk, E, heads=(0, 1, 2, 3)):
    hT, idf, idb, pb, ptb = E.hT, E.idf, E.idb, E.pb, E.ptb
    w_in_v = E.w_in.rearrange("(c p) n -> p c n", p=128)
    hT_all = [tk["hT", g] for g in range(NG)]
    with contextlib.ExitStack() as sm:
        def sb(name, shape, dt=F32):
            return sm.enter_context(nc.sbuf_tensor("m_" + name, list(shape), dt))
        rowsM = sb("rowsM", [36, T])
        colE = sb("colE", [128, NT, 8])
        colS = sb("colS", [128, NT, 8])
        colL = sb("colL", [128, NT, 8])
        colW = sb("colW", [128, NT, 8])
        decbc = sb("decbc", [128, 8, NT])
        sel = sb("sel", [36, 8, 128])
        maskbig = sb("maskbig", [128, 2, 128], BF16)
        lng_bc = sb("lng_bc", [128, 256], BF16)
        cw = sb("cw", [128, KC, 3])
        cb = sb("cb", [128, KC])
        P.q_sp.dma_start(out=sel[:], in_=E.sel_d[:, :, :], w=[tk["sel"]])
        P.q_pool.dma_start(out=maskbig[:], in_=E.maskbig_d[:, :, :], w=[tk["maskbig"]])
        P.q_sp.dma_start(out=cw[:], in_=E.m_conv_w[:, :, :], w=[tk["cw"]])
        P.q_sp.dma_start(out=cb[:], in_=E.m_conv_b[:, :], w=[tk["cb"]])

        qT = sb("qT", [128, 2, T], BF16)
        kT = sb("kT", [128, 2, T], BF16)
        big = sb("big", [128, 2 * T])
        raw = big[:, 0:T]
        acc = big[:, T:2 * T]
        woz = big[:].bitcast(BF16)[:, 0:KC * 512].rearrange("p (k n) -> p k n", n=512)
        wqk1 = sb("wqk", [128, KC, 256], BF16)
        wqk = [wqk1, wqk1]
        def qk_block(hh):
            for qi, (c0, dst) in enumerate(((C_MQ, qT), (C_MK, kT))):
                wq = wqk[qi]
                P.q_pool.dma_start(out=wq[:], in_=w_in_v[:, :, c0 + hh * 256:c0 + (hh + 1) * 256], w=[tk["wqk", 0]])
                for blk in range(2):
                    fblk = (c0 // 128) + hh * 2 + blk
                    for g in range(NG):
                        bank = pb[g % 2]
                        for kc in range(KC):
                            P.pe.matmul(bank[:, :], lhsT=wq[:, kc, blk * 128:(blk + 1) * 128], rhs=hT[:, kc, g * 512:(g + 1) * 512],
                                        start=(kc == 0), stop=(kc == KC - 1), r=hT_all + [tk["wqk", 0]], w=[tk["pb", g % 2]])
                        P.act.copy(out=raw[:, g * 512:(g + 1) * 512], in_=bank[:, :], r=[tk["pb", g % 2]], w=[tk["raw"], tk["woz"]])
                    P.dve.tensor_scalar(out=acc[:], in0=raw[:], scalar1=cw[:, fblk, 1:2], scalar2=cb[:, fblk:fblk + 1],
                                        op0=ALU.mult, op1=ALU.add, r=[tk["raw"], tk["cw"], tk["cb"]], w=[tk["acc"], tk["woz"]])
                    for (s0, ln, _) in SEQS:
                        e0 = s0 + ln
                        P.dve.scalar_tensor_tensor(out=acc[:, s0 + 1:e0], in0=raw[:, s0:e0 - 1], scalar=cw[:, fblk, 0:1],
                                                   in1=acc[:, s0 + 1:e0], op0=ALU.mult, op1=ALU.add,
                                                   r=[tk["raw"], tk["acc"]], w=[tk["acc"]])
                        P.dve.scalar_tensor_tensor(out=acc[:, s0:e0 - 1], in0=raw[:, s0 + 1:e0], scalar=cw[:, fblk, 2:3],
                                                   in1=acc[:, s0:e0 - 1], op0=ALU.mult, op1=ALU.add,
                                                   r=[tk["raw"], tk["acc"]], w=[tk["acc"]])
                    if qi == 0:
                        P.act.activation(out=dst[:, blk, :], in_=acc[:], func=AF.Silu, r=[tk["acc"]], w=[tk["qkT", qi]])
                    else:
                        P.act.activation(out=acc[:], in_=acc[:], func=AF.Silu, r=[tk["acc"]], w=[tk["acc"]])
                        P.dve.tensor_scalar_mul(out=dst[:, blk, :], in0=acc[:], scalar1=0.0625, r=[tk["acc"]], w=[tk["qkT", qi]])

        with contextlib.ExitStack() as sg:
            def sbg(name, shape, dt=F32):
                return sg.enter_context(nc.sbuf_tensor("mg_" + name, list(shape), dt))
            wg = sbg("wg", [128, KC, 16], BF16)
            gb_bc = sbg("gb_bc", [128, 16])
            g_tok = sbg("g_tok", [128, NT, 16])
            tmpf = sbg("tmpf", [128, NT, 2, 4])
            g36 = sbg("g36", [128, NT, 2, 36])
            rowsI = sbg("rowsI", [36, T])
            rowsF = sbg("rowsF", [36, T])
            rowsB = sbg("rowsB", [36, T])
            rowsT = sbg("rowsT", [36, T])
            ones36 = sbg("ones36", [36, 2048])
            m0c = sbg("m0c", [36, 1])
            Mp = sbg("Mp", [36, NT])
            Mn = sbg("Mn", [36, NT])
            nMn = sbg("nMn", [36, NT])
            dec = sbg("dec", [36, NT])
            P.q_pool.dma_start(out=wg[:], in_=w_in_v[:, :, C_MG:C_MG + 16], w=[tk["wg"]])
            P.q_act.dma_start(out=gb_bc[:], in_=E.m_gate_b.partition_broadcast(128), w=[tk["gb"]])
            P.q_sp.dma_start(out=m0c[:], in_=E.m0[:, :], w=[tk["m0c"]])
            P.dve.memset(ones36[:], 1.0, w=[tk["ones36"]])
            P.dve.memset(g36[:], 0.0, w=[tk["g36"]])
            for tt in range(NT):
                bank = pb[tt % 2]
                for kc in range(KC):
                    P.pe.matmul(bank[:, 0:16], lhsT=hT[:, kc, tt * 128:(tt + 1) * 128], rhs=wg[:, kc, :],
                                start=(kc == 0), stop=(kc == KC - 1), r=hT_all + [tk["wg"]], w=[tk["pb", tt % 2]])
                P.dve.tensor_tensor(out=g_tok[:, tt, :], in0=bank[:, 0:16], in1=gb_bc[:], op=ALU.add,
                                    r=[tk["pb", tt % 2], tk["gb"]], w=[tk["g_tok"]])
            gv = g_tok[:].rearrange("p t (d g h) -> p t d g h", d=2, g=2)
            P.act.activation(out=tmpf[:], in_=gv[:, :, :, 1, :], func=AF.Exp, scale=-1.0, r=[tk["g_tok"]], w=[tk["tmpf"]])
            P.act.activation(out=tmpf[:], in_=tmpf[:], func=AF.Ln, bias=1.0, r=[tk["tmpf"]], w=[tk["tmpf"]])
            for d in range(2):
                po = 0 if d == 0 else 32
                P.dve.tensor_copy(out=g36[:, :, 0, po:po + 4], in_=gv[:, :, d, 0, :], r=[tk["g_tok"]], w=[tk["g36"]])
                P.dve.tensor_scalar_mul(out=g36[:, :, 1, po:po + 4], in0=tmpf[:, :, d, :], scalar1=-1.0,
                                        r=[tk["tmpf"]], w=[tk["g36"]])
            for gate, rows in ((0, rowsI), (1, rowsF)):
                for g in range(NG):
                    bank = pb[2 + (g % 2)]
                    for t4 in range(4):
                        tt = g * 4 + t4
                        P.pe.matmul(bank[0:36, t4 * 128:(t4 + 1) * 128], lhsT=g36[:, tt, gate, :], rhs=idf[:, :],
                                    start=True, stop=True, r=[tk["g36"], tk["idf"]], w=[tk["pb", 2 + (g % 2)]])
                    P.act.copy(out=rows[0:36, g * 512:(g + 1) * 512], in_=bank[0:36, :],
                               r=[tk["pb", 2 + (g % 2)]], w=[tk["rows", gate]])
            tI, tF, tB, tM, tT = tk["rows", 0], tk["rows", 1], tk["rowsB"], tk["rowsM"], tk["rowsT"]
            for si, (s0, ln, _) in enumerate(SEQS):
                for d in range(2):
                    po = 0 if d == 0 else 32

                    def V(tns, ln_=ln, s0_=s0, po_=po, d_=d):
                        v = tns[po_:po_ + 4, s0_:s0_ + ln_]
                        return v[:, ::-1] if d_ == 1 else v
                    o36 = ones36[po:po + 4, 0:ln]
                    P.dve.tensor_tensor_scan(out=V(rowsB), data0=o36, data1=V(rowsF), initial=0.0,
                                             op0=ALU.mult, op1=ALU.add, r=[tF, tk["ones36"]], w=[tB])
                    P.dve.tensor_tensor(out=V(rowsI), in0=V(rowsI), in1=V(rowsB), op=ALU.subtract, r=[tI, tB], w=[tI])
                    init = m0c[po:po + 4, 0:1] if si == 0 else NEG
                    P.dve.tensor_tensor_scan(out=V(rowsM), data0=V(rowsI), data1=V(rowsI), initial=init,
                                             op0=ALU.max, op1=ALU.max, r=[tI, tk["m0c"]], w=[tM])
            Mv = rowsM[:].rearrange("p (t i) -> p t i", i=128)
            P.dve.memset(Mp[:], 0.0, w=[tk["Mp"]])
            P.dve.memset(Mn[:], 0.0, w=[tk["Mn"]])
            P.dve.tensor_copy(out=Mn[0:4, :], in_=Mv[0:4, :, 127], r=[tM], w=[tk["Mn"]])
            P.dve.tensor_copy(out=Mn[32:36, :], in_=Mv[32:36, :, 0], r=[tM], w=[tk["Mn"]])
            for si, (s0, ln, _) in enumerate(SEQS):
                ts, te = s0 // 128, (s0 + ln) // 128
                if te - ts > 1:
                    P.dve.tensor_copy(out=Mp[0:4, ts + 1:te], in_=Mn[0:4, ts:te - 1], r=[tk["Mn"]], w=[tk["Mp"]])
                    P.dve.tensor_copy(out=Mp[32:36, ts:te - 1], in_=Mn[32:36, ts + 1:te], r=[tk["Mn"]], w=[tk["Mp"]])
                for po, tpos in ((0, ts), (32, te - 1)):
                    if si == 0:
                        P.dve.tensor_copy(out=Mp[po:po + 4, tpos:tpos + 1], in_=m0c[po:po + 4, 0:1], r=[tk["m0c"]], w=[tk["Mp"]])
                    else:
                        P.dve.memset(Mp[po:po + 4, tpos:tpos + 1], NEG, w=[tk["Mp"]])
            P.dve.tensor_scalar_mul(out=nMn[:], in0=Mn[:], scalar1=-1.0, r=[tk["Mn"]], w=[tk["nMn"]])
            P.dve.tensor_tensor(out=dec[:], in0=Mp[:], in1=Mn[:], op=ALU.subtract, r=[tk["Mp"], tk["Mn"]], w=[tk["dec"]])
            P.act.activation(out=dec[:], in_=dec[:], func=AF.Exp, r=[tk["dec"]], w=[tk["dec"]])
            for ri in range(8):
                P.pe.matmul(pb[4][:, ri * NT:(ri + 1) * NT], lhsT=sel[0:36, ri, :], rhs=dec[0:36, :], start=True, stop=True,
                            r=[tk["sel"], tk["dec"]], w=[tk["pb", 4]])
            P.act.copy(out=decbc[:].rearrange("p r t -> p (r t)"), in_=pb[4][:, 0:8 * NT], r=[tk["pb", 4]], w=[tk["decbc"]])

            def to_cols(rows, rtok, col, ctok):
                for g8 in range(0, NT, 8):
                    n = min(8, NT - g8)
                    bank = pb[2 + ((g8 // 8) % 2)]
                    bt = tk["pb", 2 + ((g8 // 8) % 2)]
                    for i in range(n):
                        tt = g8 + i
                        P.pe.matmul(bank[:, i * 36:(i + 1) * 36], lhsT=rows[0:36, tt * 128:(tt + 1) * 128],
                                    rhs=idf[0:36, 0:36], start=True, stop=True, r=[rtok, tk["idf"]], w=[bt])
                    bv = bank[:, 0:n * 36].rearrange("p (t c) -> p t c", c=36)
                    P.act.copy(out=col[:, g8:g8 + n, 0:4], in_=bv[:, :, 0:4], r=[bt], w=[ctok])
                    P.act.copy(out=col[:, g8:g8 + n, 4:8], in_=bv[:, :, 32:36], r=[bt], w=[ctok])
            to_cols(rowsI, tI, colE, tk["colE"])
            for tt in range(NT):
                for po in (0, 32):
                    P.act.activation(out=rowsT[po:po + 4, tt * 128:(tt + 1) * 128], in_=rowsM[po:po + 4, tt * 128:(tt + 1) * 128],
                                     func=AF.Exp, scale=-1.0, bias=Mp[po:po + 4, tt:tt + 1], r=[tM, tk["Mp"]], w=[tT])
            to_cols(rowsT, tT, colS, tk["colS"])
            for tt in range(NT):
                for po in (0, 32):
                    P.act.activation(out=rowsT[po:po + 4, tt * 128:(tt + 1) * 128], in_=rowsI[po:po + 4, tt * 128:(tt + 1) * 128],
                                     func=AF.Exp, scale=1.0, bias=nMn[po:po + 4, tt:tt + 1], r=[tI, tk["nMn"]], w=[tT])
            to_cols(rowsT, tT, colW, tk["colW"])
            P.dve.tensor_tensor(out=rowsB[:], in0=rowsB[:], in1=rowsM[:], op=ALU.add, r=[tB, tM], w=[tB])
            P.q_sp.dma_start(out=E.om[:, :], in_=rowsB[0:36, 2048:2560], r=[tB], w=[tk["out_om"]])
            P.act.activation(out=rowsT[:], in_=rowsB[:], func=AF.Exp, scale=-1.0, r=[tB], w=[tT])
            to_cols(rowsT, tT, colL, tk["colL"])
        if len(heads):
            qk_block(heads[0])
        P.barrier()

        v_ext = sb("v_ext", [128, NT, 257], BF16)
        k_tok = sb("k_tok", [128, NT, 256], BF16)
        hf_ = sb("hf_", [128, NT, 256], BF16)
        hb_ = sb("hb_", [128, NT, 256], BF16)
        Cst2 = [sb(f"Cst{d}", [128, 2, 257]) for d in range(2)]
        Cb2 = [sb(f"Cb{d}", [128, 2, 257], BF16) for d in range(2)]
        dexp2 = [sb(f"dexp{d}", [128, 128]) for d in range(2)]
        AT2 = [sb(f"AT{d}", [128, 128], BF16) for d in range(2)]
        av2 = [sb(f"av_sb{d}", [128, 257]) for d in range(2)]
        num2 = [sb(f"num{d}", [128, 257]) for d in range(2)]
        dd2 = [sb(f"dd{d}", [128, 2]) for d in range(2)]
        wk2 = [sb(f"wk{d}", [128, 256], BF16) for d in range(2)]
        pst = [ptb[d][:].bitcast(F32) for d in range(2)]
        hsL = [sb(f"hs{i}", [128, 256]) for i in range(2)]
        soL = [sb(f"so{i}", [128, 256], BF16) for i in range(2)]
        szL = [sb(f"sz{i}", [128, 256], BF16) for i in range(2)]
        st1L = [sb(f"st1{i}", [128, 4]) for i in range(2)]
        cat_tokL = [sb(f"cat_tok{i}", [128, 256], BF16) for i in range(2)]
        catT_sbL = [sb(f"catT_sb{i}", [128, 2, 128], BF16) for i in range(2)]
        P.dve.memset(v_ext[:, :, 256:257], 1.0, w=[tk["v_ext"]])

        for hh in heads:
            P.q_pool.dma_start(out=lng_bc[:], in_=E.m_ln_g[:, hh * 256:(hh + 1) * 256].partition_broadcast(128), w=[tk["lng"]])
            if hh != heads[0]:
                qk_block(hh)
            wv_ = wqk[0]
            P.q_pool.dma_start(out=wv_[:], in_=w_in_v[:, :, C_MV + hh * 256:C_MV + (hh + 1) * 256], w=[tk["wqk", 0]])
            for tt in range(NT):
                bank = pb[tt % 2]
                for kc in range(KC):
                    P.pe.matmul(bank[:, 0:256], lhsT=hT[:, kc, tt * 128:(tt + 1) * 128], rhs=wv_[:, kc, :],
                                start=(kc == 0), stop=(kc == KC - 1), r=hT_all + [tk["wqk", 0]], w=[tk["pb", tt % 2]])
                P.act.copy(out=v_ext[:, tt, 0:256], in_=bank[:, 0:256], r=[tk["pb", tt % 2]], w=[tk["v_ext"]])
                pt = ptb[tt % 2]
                for blk in range(2):
                    P.pe.transpose(out=pt[:, blk * 128:(blk + 1) * 128], in_=kT[:, blk, tt * 128:(tt + 1) * 128], identity=idb[:],
                                   r=[tk["qkT", 1], tk["idb"]], w=[tk["ptb", tt % 2]])
                P.dve.tensor_copy(out=k_tok[:, tt, :], in_=pt[:, 0:256], r=[tk["ptb", tt % 2]], w=[tk["k_tok"]])
            for si, (s0, ln, _) in enumerate(SEQS):
                ts, te = s0 // 128, (s0 + ln) // 128
                for d in range(2):
                    tS = tk["Cst", d]
                    if si == 0:
                        P.q_sp.dma_start(out=Cst2[d][:, :, 0:256], in_=E.mC0[d, hh].rearrange("(b p) e -> p b e", p=128), w=[tS])
                        P.q_sp.dma_start(out=Cst2[d][:, :, 256:257], in_=E.mn0[d, hh].rearrange("(b p o) -> p b o", p=128, o=1),
                                         allow_slow_non_contiguous=True, w=[tS])
                    else:
                        P.dve.memset(Cst2[d][:], 0.0, w=[tS])
                    P.act.copy(out=Cb2[d][:], in_=Cst2[d][:], r=[tS], w=[tk["Cb", d]])
                nch = te - ts
                for i in range(nch):
                    for d in range(2):
                        tt = ts + i if d == 0 else te - 1 - i
                        last = (i == nch - 1)
                        ci = d * 4 + hh
                        ri = d * 4 + hh
                        tS = tk["Cst", d]
                        Cst, Cb = Cst2[d], Cb2[d]
                        dexp, AT, av_sb, num, dd, wk = dexp2[d], AT2[d], av2[d], num2[d], dd2[d], wk2[d]
                        bX, bQ, bA, bS = pb[3 * d], pb[3 * d + 1], pb[3 * d + 2], pst[d]
                        tX, tQ, tA, tSt = tk["pb", 3 * d], tk["pb", 3 * d + 1], tk["pb", 3 * d + 2], tk["ptb", d]
                        tsl = slice(tt * 128, (tt + 1) * 128)
                        for blk in range(2):
                            P.pe.matmul(bX[:, 0:128], lhsT=kT[:, blk, tsl], rhs=qT[:, blk, tsl], start=(blk == 0), stop=(blk == 1),
                                        r=[tk["qkT", 0], tk["qkT", 1]], w=[tX])
                        P.pe.matmul(bX[:, 128:256], lhsT=sel[0:36, ri, :], rhs=rowsM[0:36, tsl], start=True, stop=False,
                                    r=[tk["sel"], tk["rowsM"]], w=[tX])
                        P.pe.matmul(bX[:, 128:256], lhsT=idb[:, :], rhs=maskbig[:, d, :], start=False, stop=True,
                                    r=[tk["idb"], tk["maskbig"]], w=[tX])
                        for blk in range(2):
                            P.pe.matmul(bQ[:, 0:257], lhsT=qT[:, blk, tsl], rhs=Cb[:, blk, :], start=(blk == 0), stop=(blk == 1),
                                        r=[tk["qkT", 0], tk["Cb", d]], w=[tQ])
                        if not (si == 0 and last):
                            P.dve.tensor_scalar_mul(out=wk[:], in0=k_tok[:, tt, :], scalar1=colW[:, tt, ci:ci + 1],
                                                    r=[tk["k_tok"], tk["colW"]], w=[tk["wk", d]])
                            for blk in range(2):
                                P.pe.matmul(bS[:, 0:257], lhsT=wk[:, blk * 128:(blk + 1) * 128], rhs=v_ext[:, tt, :], start=True, stop=True,
                                            r=[tk["wk", d], tk["v_ext"]], w=[tSt])
                                P.dve.scalar_tensor_tensor(out=Cst[:, blk, :], in0=Cst[:, blk, :], scalar=decbc[:, ri, tt:tt + 1], in1=bS[:, 0:257],
                                                           op0=ALU.mult, op1=ALU.add, r=[tS, tk["decbc"], tSt], w=[tS])
                            P.act.copy(out=Cb[:], in_=Cst[:], r=[tS], w=[tk["Cb", d]])
                        P.act.activation(out=dexp[:], in_=bX[:, 128:256], func=AF.Exp, scale=-1.0, bias=colE[:, tt, ci:ci + 1],
                                         r=[tX, tk["colE"]], w=[tk["dexp", d]])
                        P.dve.tensor_tensor(out=AT[:], in0=bX[:, 0:128], in1=dexp[:], op=ALU.mult,
                                            r=[tX, tk["dexp", d]], w=[tk["AT", d]])
                        P.pe.matmul(bA[:, 0:257], lhsT=AT[:], rhs=v_ext[:, tt, :], start=True, stop=True,
                                    r=[tk["AT", d], tk["v_ext"]], w=[tA])
                        P.act.copy(out=av_sb[:], in_=bA[:, 0:257], r=[tA], w=[tk["av_sb", d]])
                        P.dve.scalar_tensor_tensor(out=num[:], in0=bQ[:, 0:257], scalar=colS[:, tt, ci:ci + 1], in1=av_sb[:],
                                                   op0=ALU.mult, op1=ALU.add, r=[tQ, tk["colS"], tk["av_sb", d]], w=[tk["num", d]])
                        P.dve.scalar_tensor_tensor(out=dd[:, 0:1], in0=num[:, 256:257], scalar=-1.0, in1=num[:, 256:257],
                                                   op0=ALU.mult, op1=ALU.max, r=[tk["num", d]], w=[tk["dd", d]])
                        P.dve.tensor_tensor(out=dd[:, 0:1], in0=dd[:, 0:1], in1=colL[:, tt, ci:ci + 1], op=ALU.max,
                                            r=[tk["dd", d], tk["colL"]], w=[tk["dd", d]])
                        P.dve.reciprocal(out=dd[:, 1:2], in_=dd[:, 0:1], r=[tk["dd", d]], w=[tk["dd", d]])
                        if d == 0:
                            P.act.activation(out=hf_[:, tt, :], in_=num[:, 0:256], func=AF.Copy, scale=dd[:, 1:2],
                                             r=[tk["num", d], tk["dd", d]], w=[tk["hf", tt]])
                        else:
                            P.act.activation(out=hb_[:, tt, :], in_=num[:, 0:256], func=AF.Copy, scale=dd[:, 1:2],
                                             r=[tk["num", d], tk["dd", d]], w=[tk["hb", tt]])
                for d in range(2):
                    if si > 0:
                        pi = si - 1
                        tS = tk["Cst", d]
                        P.q_sp.dma_start(out=E.oC[pi, d, hh].rearrange("(b p) e -> p b e", p=128), in_=Cst2[d][:, :, 0:256], r=[tS], w=[tk["out_oC"]])
                        P.q_sp.dma_start(out=E.on[pi, d, hh].rearrange("(b p o) -> p b o", p=128, o=1), in_=Cst2[d][:, :, 256:257],
                                         allow_slow_non_contiguous=True, r=[tS], w=[tk["out_oC"]])
            P.q_pool.dma_start(out=woz[:, :, 0:256], in_=w_in_v[:, :, C_MO + hh * 256:C_MO + (hh + 1) * 256], w=[tk["woz"], tk["raw"], tk["acc"]])
            P.q_pool.dma_start(out=woz[:, :, 256:512], in_=w_in_v[:, :, C_MZ + hh * 256:C_MZ + (hh + 1) * 256], w=[tk["woz"], tk["raw"], tk["acc"]])
            for tt in range(NT):
                tsl = slice(tt * 128, (tt + 1) * 128)
                fi = tt % 2
                hs, so, sz, st1, cat_tok, catT_sb = hsL[fi], soL[fi], szL[fi], st1L[fi], cat_tokL[fi], catT_sbL[fi]
                pbf, tpbf = pb[4 + fi], tk["pb", 4 + fi]
                for kc in range(KC):
                    P.pe.matmul(pbf[:, :], lhsT=hT[:, kc, tsl], rhs=woz[:, kc, :], start=(kc == 0), stop=(kc == KC - 1),
                                r=hT_all + [tk["woz"]], w=[tpbf])
                P.act.activation(out=so[:], in_=pbf[:, 0:256], func=AF.Sigmoid, r=[tpbf], w=[tk["so", fi]])
                P.act.activation(out=sz[:], in_=pbf[:, 256:512], func=AF.Silu, r=[tpbf], w=[tk["sz", fi]])
                P.dve.tensor_tensor(out=hs[:], in0=hf_[:, tt, :], in1=hb_[:, tt, :], op=ALU.add, r=[tk["hf", tt], tk["hb", tt]], w=[tk["hs", fi]])
                P.dve.tensor_tensor(out=hs[:], in0=hs[:], in1=so[:], op=ALU.mult, r=[tk["hs", fi], tk["so", fi]], w=[tk["hs", fi]])
                P.dve.tensor_reduce(out=st1[:, 0:1], in_=hs[:], axis=AX.X, op=ALU.add, r=[tk["hs", fi]], w=[tk["st1", fi]])
                P.dve.tensor_scalar_mul(out=st1[:, 1:2], in0=st1[:, 0:1], scalar1=-1.0 / 256, r=[tk["st1", fi]], w=[tk["st1", fi]])
                P.dve.tensor_scalar_add(out=hs[:], in0=hs[:], scalar1=st1[:, 1:2], r=[tk["hs", fi], tk["st1", fi]], w=[tk["hs", fi]])
                P.act.activation(out=so[:], in_=hs[:], func=AF.Square, accum_out=st1[:, 2:3], r=[tk["hs", fi]], w=[tk["so", fi], tk["st1", fi]])
                P.dve.tensor_scalar(out=st1[:, 2:3], in0=st1[:, 2:3], scalar1=1.0 / 256, scalar2=EPS, op0=ALU.mult, op1=ALU.add,
                                    r=[tk["st1", fi]], w=[tk["st1", fi]])
                P.act.sqrt(out=st1[:, 2:3], in_=st1[:, 2:3], r=[tk["st1", fi]], w=[tk["st1", fi]])
                P.dve.reciprocal(out=st1[:, 3:4], in_=st1[:, 2:3], r=[tk["st1", fi]], w=[tk["st1", fi]])
                P.dve.scalar_tensor_tensor(out=hs[:], in0=hs[:], scalar=st1[:, 3:4], in1=lng_bc[:, :],
                                           op0=ALU.mult, op1=ALU.mult, r=[tk["hs", fi], tk["st1", fi], tk["lng"]], w=[tk["hs", fi]])
                P.dve.tensor_tensor(out=cat_tok[:], in0=hs[:], in1=sz[:], op=ALU.mult, r=[tk["hs", fi], tk["sz", fi]], w=[tk["cat_tok", fi]])
                pt = ptb[tt % 2]
                for blk in range(2):
                    P.pe.transpose(out=pt[:, blk * 128:(blk + 1) * 128], in_=cat_tok[:, blk * 128:(blk + 1) * 128], identity=idb[:],
                                   r=[tk["cat_tok", fi], tk["idb"]], w=[tk["ptb", tt % 2]])
                P.act.copy(out=catT_sb[:].rearrange("p b t -> p (b t)"), in_=pt[:, 0:256], r=[tk["ptb", tt % 2]], w=[tk["catT_sb", fi]])
                g, t4 = tt // 4, tt % 4
                P.q_act.dma_start(out=E.catT[g, :, hh * 2:hh * 2 + 2, t4 * 128:(t4 + 1) * 128], in_=catT_sb[:],
                                  r=[tk["catT_sb", fi]], w=[tk["catT", g]])
    P.barrier()


KAP = 0.6065306597126334
LNX_EPS = 64e-5


def phase_rwkv(nc, P, tk, E, pairs=range(8)):
    hT, idf, idb, pb, ptb = E.hT, E.idf, E.idb, E.pb, E.ptb
    w_in_v = E.w_in.rearrange("(c p) n -> p c n", p=128)
    hT_all = [tk["hT", g] for g in range(NG)]
    with contextlib.ExitStack() as sr:
        def sb(name, shape, dt=F32):
            return sr.enter_context(nc.sbuf_tensor("rk_" + name, list(shape), dt))
        bonesf = sb("bonesf", [128, 128])
        bonesb = sb("bonesb", [128, 128], BF16)
        mkf = sb("mkf", [128, 2, 5, 64], BF16)
        rm = sb("rm", [128, 513])
        gmask = sb("gmask", [128, 4])
        idpf = sb("idpf", [128, 8, 64], BF16)
        mu_sb = sb("mu", [128, 26])
        w0_sb = sb("w0", [128, 2, 8])
        a0_sb = sb("a0", [128, 2, 8])
        kk_sb = sb("kkp", [128, 8])
        ka_sb = sb("kap", [128, 8])
        rk_sb = sb("rkp", [128, 8])
        lg_sb = sb("lgp", [128, 8])
        lb_sb = sb("lbp", [128, 8])
        w2hh = sb("w2hh", [128, 1, 2, 128], BF16)
        a2hh = sb("a2hh", [128, 1, 2, 128], BF16)
        coef = sb("coef", [128, 8])
        coef2 = sb("coef2", [128, 8])
        for dst, src, nm in ((bonesf, E.bones_d, "bonesf"), (rm, E.rm_d, "rm"), (gmask, E.gmask_d, "gmask"),
                             (mu_sb, E.r_mu, "mu"), (w0_sb, E.r_w0, "w0"), (a0_sb, E.r_a0, "a0"),
                             (kk_sb, E.r_k_k, "kkp"), (ka_sb, E.r_k_a, "kap"), (rk_sb, E.r_r_k, "rkp"), (lg_sb, E.r_ln_g, "lgp"),
                             (lb_sb, E.r_ln_b, "lbp")):
            P.q_sp.dma_start(out=dst[:], in_=src, w=[tk[nm]])
        P.dve.tensor_copy(out=bonesb[:], in_=bonesf[:], r=[tk["bonesf"]], w=[tk["bonesb"]])
        P.q_pool.dma_start(out=idpf[:], in_=E.idp_d, w=[tk["idpf"]])
        P.q_pool.dma_start(out=mkf[:], in_=E.mk_d, w=[tk["mkf"]])
        P.dve.memset(w2hh[:], 0.0, w=[tk["w2h", 0]])
        P.dve.memset(a2hh[:], 0.0, w=[tk["a2h", 0]])

        rwdT = sb("rwdT", [128, T], BF16)
        radT = sb("radT", [128, T], BF16)
        big = sb("big", [128, 2 * T])
        raw = big[:, 0:T]
        acc = big[:, T:2 * T]
        wr = sb("wr", [128, 2, KC, 128], BF16)
        wr_n = [0]

        def project_cm(col0, widx, evac):
            widx = wr_n[0] % 2
            wr_n[0] += 1
            P.q_pool.dma_start(out=wr[:, widx, :, :], in_=w_in_v[:, :, col0:col0 + 128], w=[tk["wr", widx]])
            for g in range(NG):
                bank = pb[g % 2]
                for kc in range(KC):
                    P.pe.matmul(bank[:, :], lhsT=wr[:, widx, kc, :], rhs=hT[:, kc, g * 512:(g + 1) * 512],
                                start=(kc == 0), stop=(kc == KC - 1), r=hT_all + [tk["wr", widx]], w=[tk["pb", g % 2]])
                evac(g, bank, tk["pb", g % 2])

        RA = [(raw, acc, 0), (raw, acc, 0)]
        cur_ra = [RA[0]]
        coefs = [coef, coef2]

        def evac_raw(g, bank, bt):
            raw_, acc_, pp = cur_ra[0]
            P.act.copy(out=raw_[:, g * 512:(g + 1) * 512], in_=bank[:, :], r=[bt], w=[tk["raw", pp, g]])
            P.act.activation(out=acc_[:, g * 512:(g + 1) * 512], in_=bank[:, :], func=AF.Copy, scale=coefs[pp][:, 0:1],
                             r=[bt, tk["coef", pp]], w=[tk["acc", pp, g]])

        def prep_coef(mublk):
            m = mu_sb[:, mublk:mublk + 1]
            pp = cur_ra[0][2]
            coef = coefs[pp]
            tc_ = tk["coef", pp]
            P.dve.tensor_scalar(out=coef[:, 0:1], in0=m, scalar1=-1.0, scalar2=1.0, op0=ALU.mult, op1=ALU.add, r=[tk["mu"]], w=[tc_])
            P.dve.tensor_scalar_mul(out=coef[:, 1:5], in0=gmask[:], scalar1=m, r=[tk["gmask"], tk["mu"]], w=[tc_])
            P.dve.tensor_tensor(out=coef[:, 5:6], in0=coef[:, 1:2], in1=coef[:, 3:4], op=ALU.add, r=[tc_], w=[tc_])
            P.dve.tensor_tensor(out=coef[:, 6:7], in0=coef[:, 2:3], in1=coef[:, 4:5], op=ALU.add, r=[tc_], w=[tc_])

        def mix(mublk):
            raw, acc, pp = cur_ra[0]
            coef = coefs[pp]
            tc_ = tk["coef", pp]
            traw = [tk["raw", pp, g] for g in range(NG)]
            tacc = [tk["acc", pp, g] for g in range(NG)]
            rv_ = raw[:, 0:2048].rearrange("p (r c) -> p r c", c=64)
            av_ = acc[:, 0:2048].rearrange("p (r c) -> p r c", c=64)

            def stt(o, i, cidx, tr, ta):
                P.dve.scalar_tensor_tensor(out=o, in0=i, scalar=coef[:, cidx:cidx + 1], in1=o, op0=ALU.mult, op1=ALU.add,
                                           r=tr + ta + [tc_], w=ta)
            stt(av_[:, :, 1:64], rv_[:, :, 0:63], 1, traw[0:4], tacc[0:4])
            stt(av_[:, :, 0:63], rv_[:, :, 1:64], 2, traw[0:4], tacc[0:4])
            stt(av_[:, 1:32, :], rv_[:, 0:31, :], 3, traw[0:4], tacc[0:4])
            stt(av_[:, 0:31, :], rv_[:, 1:32, :], 4, traw[0:4], tacc[0:4])
            rp_ = raw[:, 2048:2560].rearrange("p (s c) -> p s c", c=256)
            ap_ = acc[:, 2048:2560].rearrange("p (s c) -> p s c", c=256)
            stt(ap_[:, :, 1:256], rp_[:, :, 0:255], 5, traw[4:5], tacc[4:5])
            stt(ap_[:, :, 0:255], rp_[:, :, 1:256], 6, traw[4:5], tacc[4:5])

        accall = [tk["acc", 0, g] for g in range(NG)]
        prep_coef(24)
        project_cm(C_RS + 3072, 0, evac_raw)
        mix(24)
        P.act.activation(out=rwdT[:], in_=acc[:], func=AF.Tanh, r=accall, w=[tk["rwdT"]])
        prep_coef(25)
        project_cm(C_RS + 3200, 0, evac_raw)
        mix(25)
        P.act.copy(out=radT[:], in_=acc[:], r=accall, w=[tk["radT"]])

        stgA = sb("stgA", [128, 4, T], BF16)
        stg = [stgA[:, j, :] for j in range(4)]
        raw2 = stgA[:, 0:2, :].rearrange("p a t -> p (a t)").bitcast(F32)
        acc2 = stgA[:, 2:4, :].rearrange("p a t -> p (a t)").bitcast(F32)
        RA[0] = (raw, acc, 0)
        RA[1] = (raw2, acc2, 1)
        nblk = [0]
        for hp in pairs:
            for j, c0 in enumerate((C_RS, C_RS + 1024, C_RS + 2048)):
                cur_ra[0] = RA[nblk[0] % 2]
                nblk[0] += 1
                prep_coef(j * 8 + hp)
                project_cm(c0 + hp * 128, j, evac_raw)
                mix(j * 8 + hp)
                pp = cur_ra[0][2]
                P.q_pool.dma_start(out=E.rscr[hp, j], in_=cur_ra[0][1], r=[tk["acc", pp, g] for g in range(NG)], w=[tk["rscr", hp]])
            cur_ra[0] = RA[nblk[0] % 2]
            nblk[0] += 1
            pp = cur_ra[0][2]

            def evac_sz(g, bank, bt, pp=pp, accb=cur_ra[0][1]):
                P.act.activation(out=accb[:, g * 512:(g + 1) * 512], in_=bank[:, :], func=AF.Silu, r=[bt], w=[tk["acc", pp, g]])
            project_cm(C_RZ + hp * 128, 3, evac_sz)
            P.q_pool.dma_start(out=E.rscr[hp, 3], in_=cur_ra[0][1], r=[tk["acc", pp, g] for g in range(NG)], w=[tk["rscr", hp]])
        P.barrier()

        hTf = hT[:].rearrange("p k t -> p (k t)")
        carve_off = [0]

        def carve(shape, dt):
            n = 1
            for d_ in shape[1:]:
                n *= d_
            ne = n if dt == BF16 else 2 * n
            v = hTf[:, carve_off[0]:carve_off[0] + ne]
            carve_off[0] += ne
            if dt != BF16:
                v = v.bitcast(F32)
            if len(shape) == 2:
                return v
            names = "abcd"[:len(shape) - 1]
            kws = {names[i]: shape[1 + i] for i in range(1, len(shape) - 1)}
            return v.rearrange("p (" + " ".join(names) + ") -> p " + " ".join(names), **kws)

        class BS:
            pass

        def mkset(z):
            B = BS()
            if z == 0:
                al_ = lambda name, shape, dt=F32: sb(name, shape, dt)[:]
                B.tmp = [big[:, i * 512:(i + 1) * 512] for i in range(10)]
            else:
                al_ = lambda name, shape, dt=F32: carve(shape, dt)
                bg = carve([128, 2 * T], F32)
                B.tmp = [bg[:, i * 512:(i + 1) * 512] for i in range(10)]
            B.H = []
            for par in range(2):
                h = BS()
                if par == 1 and z == 0:
                    wrf = wr[:, 0, :, :].rearrange("p k n -> p (k n)")
                    h.AR = wrf[:, 0:1024].rearrange("p (a b) -> p a b", b=512)
                    h.BK = wrf[:, 1024:2048].rearrange("p (a b) -> p a b", b=512)
                else:
                    h.AR = al_(f"AR{par}", [128, 2, 512], BF16)
                    h.BK = al_(f"BK{par}", [128, 2, 512], BF16)
                stk = (lambda name, shape, dt=F32: sb(name + "z1", shape, dt)[:]) if (par == 1 and z == 1) else al_
                h.tokm = stk(f"tokm{par}", [128, 8, 4, 64], BF16)
                h.MS = al_(f"MS{par}", [128, 8, 5, 64], BF16)
                h.Qa = stk(f"Qa{par}", [128, 8, 64], BF16)
                h.Qb = stk(f"Qb{par}", [128, 8, 64], BF16)
                h.egL = al_(f"egL{par}", [128, 8])
                B.H.append(h)
            B.PW = al_("PW", [128, 2, 8, 2, 64], BF16)
            B.W1b = al_("W1b", [128, 8, 64], BF16)
            B.W2b = al_("W2b", [128, 8, 64], BF16)
            B.IPhiT = al_("IPhiT", [128, 8, 64])
            B.PsiE = al_("PsiE", [128, 8, 64])
            B.RG = al_("RG", [128, 8, 64], BF16)
            B.MV = B.RG
            B.Tst2 = al_("Tst2", [128, 64])
            B.cur = [0]
            B.Tbs = al_("Tbs", [128, 9, 64], BF16)
            B.Tst = al_("Tst", [128, 64])
            B.S0p = al_("S0p", [128, 64])
            B.So = al_("So", [128, 64])
            B.sqb = al_("sqb", [128, 512], BF16)
            return B
        sets = [mkset(0), mkset(1)]
        inp2 = [stg, [carve([128, T], BF16) for _ in range(4)]]
        yT = sb("yT", [128, T])
        bonT = sb("bonT", [128, T], BF16)

        def group_stage(hp, z, g, groups, B, par, rT, kT, vT, tin, w2h, a2h):
            h = B.H[par]
            AR, BK, tokm, MS, PW, Qa, Qb, MV, W1b, W2b, IPhiT, PsiE, RG, egL, Tbs, Tst, Tst2, S0p, So, sqb = (
                h.AR, h.BK, h.tokm, h.MS, B.PW, h.Qa, h.Qb, B.MV, B.W1b, B.W2b, B.IPhiT, B.PsiE, B.RG, h.egL, B.Tbs, B.Tst, B.Tst2, B.S0p, B.So, B.sqb)
            B_cur = B.cur
            sw, al, cs, eg, ieg, egp, kkf, kkn, t1, t2 = B.tmp

            class TKZ:
                def __getitem__(self_, k):
                    if k in ("AR", "BK", "egL") or (isinstance(k, tuple) and k[0] in ("tokm", "MS", "Q")):
                        return tk[("z", z, par, k)]
                    if k == "MV":
                        k = "RG"
                    if k == "w2h":
                        return w2h[1]
                    if k == "a2h":
                        return a2h[1]
                    if k in ("rm", "rwdT", "radT", "w0", "a0", "kkp", "kap", "rkp", "bonesb", "bonesf", "mkf", "idp", "idpf", "idb", "idf",
                             "yT", "bonT", "out_oS") or (isinstance(k, tuple) and k[0] in ("pb", "ptb", "yT", "bonT")):
                        return tk[k]
                    if isinstance(k, tuple) and k[0] == "rkv":
                        return tin
                    return tk[("z", z, k)]
            tkz = TKZ()
            gs = slice(g * 512, (g + 1) * 512)
            _group_body(hp, z, g, groups, gs, tkz, AR, BK, tokm, MS, PW, Qa, Qb, MV, W1b, W2b, IPhiT, PsiE, RG, egL, Tbs, Tst, Tst2, B_cur, S0p, So, sqb,
                        sw, al, cs, eg, ieg, egp, kkf, kkn, t1, t2, rT, kT, vT, w2h[0], a2h[0])

        def _group_body(hp, z, g, groups, gs, tk, AR, BK, tokm, MS, PW, Qa, Qb, MV, W1b, W2b, IPhiT, PsiE, RG, egL, Tbs, Tst, Tst2, B_cur, S0p, So, sqb,
                        sw, al, cs, eg, ieg, egp, kkf, kkn, t1, t2, rT, kT, vT, w2h, a2h):
            bA, bB, bC, bT = pb[3 * z], pb[3 * z + 1], pb[3 * z + 2], ptb[z]
            tA, tB, tC, tT_ = tk["pb", 3 * z], tk["pb", 3 * z + 1], tk["pb", 3 * z + 2], tk["ptb", z]
            P.pe.matmul(bB[:, :], lhsT=w2h[:, z, :], rhs=rwdT[:, gs], start=True, stop=True,
                        r=[tk["w2h"], tk["rwdT"]], w=[tB])
            P.act.activation(out=sw, in_=bB[:, :], func=AF.Sigmoid, bias=w0_sb[:, z, hp:hp + 1], r=[tB, tk["w0"]], w=[tk["sw"]])
            P.pe.matmul(bC[:, :], lhsT=a2h[:, z, :], rhs=radT[:, gs], start=True, stop=True,
                        r=[tk["a2h"], tk["radT"]], w=[tC])
            P.act.activation(out=al, in_=bC[:, :], func=AF.Sigmoid, bias=a0_sb[:, z, hp:hp + 1], r=[tC, tk["a0"]], w=[tk["al"]])
            if z == 0:
                P.dve.tensor_tensor_scan(out=cs, data0=rm[:, 0:512], data1=sw, initial=0.0, op0=ALU.mult, op1=ALU.add,
                                         r=[tk["rm"], tk["sw"]], w=[tk["cs"]])
            else:
                P.dve.tensor_tensor_scan(out=cs[:, ::-1], data0=rm[:, 1:513][:, ::-1], data1=sw[:, ::-1], initial=0.0, op0=ALU.mult, op1=ALU.add,
                                         r=[tk["rm"], tk["sw"]], w=[tk["cs"]])
            P.act.activation(out=eg, in_=cs, func=AF.Exp, scale=-KAP, r=[tk["cs"]], w=[tk["eg"]])
            P.act.activation(out=ieg, in_=cs, func=AF.Exp, scale=KAP, r=[tk["cs"]], w=[tk["ieg"]])
            P.dve.tensor_tensor(out=t1, in0=cs, in1=sw, op=ALU.subtract, r=[tk["cs"], tk["sw"]], w=[tk["t1"]])
            P.act.activation(out=egp, in_=t1, func=AF.Exp, scale=-KAP, r=[tk["t1"]], w=[tk["egp"]])
            egv = eg.rearrange("p (c i) -> p c i", i=64)
            P.dve.tensor_copy(out=egL[:], in_=egv[:, :, 63 if z == 0 else 0], r=[tk["eg"]], w=[tk["egL"]])
            P.dve.tensor_scalar_mul(out=kkf, in0=kT[:, gs], scalar1=kk_sb[:, hp:hp + 1], r=[tk["rkv", 1], tk["kkp"]], w=[tk["kkf"]])
            P.dve.tensor_tensor(out=sqb[:], in0=kkf, in1=kkf, op=ALU.mult, r=[tk["kkf"]], w=[tk["sqb"]])
            P.pe.matmul(bB[:, :], lhsT=bonesb[:], rhs=sqb[:], start=True, stop=True, r=[tk["bonesb"], tk["sqb"]], w=[tB])
            P.dve.tensor_scalar_max(out=t1, in0=bB[:, :], scalar1=1e-24, r=[tB], w=[tk["t1"]])
            P.act.activation(out=t1, in_=t1, func=AF.Ln, r=[tk["t1"]], w=[tk["t1"]])
            P.act.activation(out=t1, in_=t1, func=AF.Exp, scale=-0.5, r=[tk["t1"]], w=[tk["t1"]])
            P.dve.tensor_tensor(out=kkn, in0=kkf, in1=t1, op=ALU.mult, r=[tk["kkf"], tk["t1"]], w=[tk["kkn"]])
            P.dve.tensor_scalar(out=t2, in0=al, scalar1=-1.0, scalar2=ka_sb[:, hp:hp + 1], op0=ALU.add, op1=ALU.mult,
                                r=[tk["al"], tk["kap"]], w=[tk["t2"]])
            P.dve.scalar_tensor_tensor(out=t2, in0=t2, scalar=1.0, in1=kT[:, gs], op0=ALU.add, op1=ALU.mult,
                                       r=[tk["t2"], tk["rkv", 1]], w=[tk["t2"]])
            P.dve.scalar_tensor_tensor(out=AR[:, 0, :], in0=kkn, scalar=-1.0, in1=egp, op0=ALU.mult, op1=ALU.mult,
                                       r=[tk["kkn"], tk["egp"]], w=[tk["AR"]])
            P.dve.tensor_tensor(out=AR[:, 1, :], in0=rT[:, gs], in1=eg, op=ALU.mult, r=[tk["rkv", 0], tk["eg"]], w=[tk["AR"]])
            P.dve.tensor_tensor(out=t1, in0=kkn, in1=al, op=ALU.mult, r=[tk["kkn"], tk["al"]], w=[tk["t1"]])
            P.dve.tensor_tensor(out=BK[:, 0, :], in0=t1, in1=ieg, op=ALU.mult, r=[tk["t1"], tk["ieg"]], w=[tk["BK"]])
            P.dve.tensor_tensor(out=BK[:, 1, :], in0=t2, in1=ieg, op=ALU.mult, r=[tk["t2"], tk["ieg"]], w=[tk["BK"]])
            P.dve.scalar_tensor_tensor(out=sqb[:], in0=t2, scalar=rk_sb[:, hp:hp + 1], in1=rT[:, gs], op0=ALU.mult, op1=ALU.mult,
                                       r=[tk["t2"], tk["rkp"], tk["rkv", 0]], w=[tk["sqb"]])
            P.pe.matmul(bC[:, :], lhsT=bonesb[:], rhs=sqb[:], start=True, stop=True, r=[tk["bonesb"], tk["sqb"]], w=[tC])
            P.dve.tensor_tensor(out=bonT[:, gs], in0=bonT[:, gs], in1=bC[:, :], op=ALU.add, r=[tk["bonT", g], tC], w=[tk["bonT", g]])

            tp = (None, (64, 64))

            def hs_(e):
                return slice(64 * e, 64 * e + 64)
            srcs = (AR[:, 0, :], BK[:, 0, :], BK[:, 1, :], vT[:, gs])
            srct = (tk["AR"], tk["BK"], tk["BK"], tk["rkv", 2])
            bTf = bT[:].bitcast(F32)
            for half in range(2):
                pt = bT
                for c4 in range(4):
                    c = half * 4 + c4
                    for wi in range(4):
                        for e in range(2):
                            kw = {} if e == 0 else {"tile_position": (64, 64)}
                            P.pe.transpose(out=pt[hs_(e), (c4 * 4 + wi) * 64:(c4 * 4 + wi + 1) * 64],
                                           in_=srcs[wi][hs_(e), c * 64:(c + 1) * 64], identity=idb[hs_(e), hs_(e)],
                                           r=[srct[wi], tk["idb"]], w=[tT_], **kw)
                P.act.copy(out=tokm[:, half * 4:(half + 1) * 4, :, :].rearrange("p c w i -> p (c w i)"), in_=pt[:, :],
                           r=[tT_], w=[tk["tokm", half]])
            for c in range(8):
                bank = (bC, bB)[c % 2]
                bt = (tC, tB)[c % 2]
                csl = slice(c * 64, (c + 1) * 64)
                for e in range(2):
                    kw = {} if e == 0 else {"tile_position": (64, 64)}
                    P.pe.matmul(bank[hs_(e), 0:128], lhsT=BK[hs_(e), 0, csl], rhs=AR[hs_(e), :, csl], start=True, stop=True,
                                r=[tk["AR"], tk["BK"]], w=[bt], **kw)
                    P.pe.matmul(bank[hs_(e), 128:256], lhsT=BK[hs_(e), 1, csl], rhs=AR[hs_(e), :, csl], start=True, stop=True,
                                r=[tk["AR"], tk["BK"]], w=[bt], **kw)
                    P.pe.matmul(bank[hs_(e), 256:320], lhsT=AR[hs_(e), 0, csl], rhs=BK[hs_(e), 0, csl], start=True, stop=True,
                                r=[tk["AR"], tk["BK"]], w=[bt], **kw)
                P.dve.tensor_tensor(out=MS[:, c, :, :].rearrange("p m i -> p (m i)"), in0=bank[:, 0:320],
                                    in1=mkf[:, z, :, :].rearrange("p m i -> p (m i)"), op=ALU.mult,
                                    r=[bt, tk["mkf"]], w=[tk["MS", c]])
            Qs = (Qa, Qb)
            for half in range(2):
                hsl = slice(half * 4, (half + 1) * 4)
                P.dve.tensor_tensor(out=Qa[:, hsl, :], in0=MS[:, hsl, 0, :], in1=idpf[:, hsl, :], op=ALU.add,
                                    r=[tk["MS", c] for c in range(half * 4, half * 4 + 4)] + [tk["idpf"]], w=[tk["Q", 0, half]])
            for lvl in range(5):
                for half in range(2):
                    hsl = slice(half * 4, (half + 1) * 4)
                    bank = (bB, bC)[half]
                    bt = (tB, tC)[half]
                    for c4 in range(4):
                        c = half * 4 + c4
                        if lvl == 0:
                            Z, ZT = MS[:, c, 4, :], MS[:, c, 0, :]
                        else:
                            Z, ZT = PW[:, (lvl - 1) % 2, c, 0, :], PW[:, (lvl - 1) % 2, c, 1, :]
                        for e in range(2):
                            kw = {} if e == 0 else {"tile_position": (64, 64)}
                            rt = [tk["MS", c]] if lvl == 0 else [tk["PW", (lvl - 1) % 2, half]]
                            P.pe.matmul(bank[hs_(e), (c4 * 2) * 64:(c4 * 2 + 1) * 64], lhsT=ZT[hs_(e), :], rhs=Z[hs_(e), :],
                                        start=True, stop=True, r=rt, w=[bt], **kw)
                            if lvl < 4:
                                P.pe.matmul(bank[hs_(e), (c4 * 2 + 1) * 64:(c4 * 2 + 2) * 64], lhsT=Z[hs_(e), :], rhs=ZT[hs_(e), :],
                                            start=True, stop=True, r=rt, w=[bt], **kw)
                    P.act.copy(out=PW[:, lvl % 2, hsl, :, :].rearrange("p c w i -> p (c w i)"), in_=bank[:, :],
                               r=[bt], w=[tk["PW", lvl % 2, half]])
                    qi, qo = Qs[lvl % 2], Qs[(lvl + 1) % 2]
                    qbank, qbt = bTf[:, half * 256:(half + 1) * 256], tT_
                    for c4 in range(4):
                        c = half * 4 + c4
                        for e in range(2):
                            kw = {} if e == 0 else {"tile_position": (64, 64)}
                            P.pe.matmul(qbank[hs_(e), c4 * 64:(c4 + 1) * 64], lhsT=PW[hs_(e), lvl % 2, c, 0, :], rhs=qi[hs_(e), c, :],
                                        start=True, stop=True, r=[tk["PW", lvl % 2, half], tk["Q", lvl % 2, half]], w=[qbt], **kw)
                    P.dve.tensor_tensor(out=qo[:, hsl, :].rearrange("p c i -> p (c i)"), in0=qi[:, hsl, :].rearrange("p c i -> p (c i)"),
                                        in1=qbank[:, 0:256], op=ALU.add, r=[tk["Q", lvl % 2, half], qbt], w=[tk["Q", (lvl + 1) % 2, half]])
            Q = Qs[1]
            for c in range(8):
                for e in range(2):
                    kw = {} if e == 0 else {"tile_position": (64, 64)}
                    P.pe.matmul(bA[hs_(e), c * 64:(c + 1) * 64], lhsT=MS[hs_(e), c, 2, :], rhs=tokm[hs_(e), c, 3, :],
                                start=True, stop=True, r=[tk["MS", c], tk["tokm", c // 4]], w=[tA], **kw)
            P.act.copy(out=MV[:].rearrange("p c i -> p (c i)"), in_=bA[:, :], r=[tA], w=[tk["MV"]])

            def tail_mm(fn, evac):
                for c in range(8):
                    for e in range(2):
                        kw = {} if e == 0 else {"tile_position": (64, 64)}
                        fn(c, e, slice(c * 64, (c + 1) * 64), kw)
                evac()
            tail_mm(lambda c, e, csl, kw: P.pe.matmul(bA[hs_(e), csl], lhsT=Q[hs_(e), c, :], rhs=tokm[hs_(e), c, 0, :], start=True, stop=True,
                                                      r=[tk["tokm", c // 4], tk["Q", 1, c // 4]], w=[tA], **kw),
                    lambda: P.act.copy(out=W1b[:].rearrange("p c i -> p (c i)"), in_=bA[:, :], r=[tA], w=[tk["W1b"]]))
            tail_mm(lambda c, e, csl, kw: P.pe.matmul(bA[hs_(e), csl], lhsT=Q[hs_(e), c, :], rhs=MV[hs_(e), c, :], start=True, stop=True,
                                                      r=[tk["Q", 1, c // 4], tk["MV"]], w=[tA], **kw),
                    lambda: P.act.copy(out=W2b[:].rearrange("p c i -> p (c i)"), in_=bA[:, :], r=[tA], w=[tk["W2b"]]))
            tail_mm(lambda c, e, csl, kw: P.pe.matmul(bA[hs_(e), csl], lhsT=W1b[hs_(e), c, :], rhs=tokm[hs_(e), c, 1, :], start=True, stop=True,
                                                      r=[tk["W1b"], tk["tokm", c // 4]], w=[tA], **kw),
                    lambda: P.dve.tensor_tensor(out=IPhiT[:].rearrange("p c i -> p (c i)"), in0=bA[:, :], in1=idpf[:].rearrange("p c i -> p (c i)"),
                                                op=ALU.add, r=[tA, tk["idpf"]], w=[tk["IPhiT"]]))

            def psi_mm(c, e, csl, kw):
                P.pe.matmul(bA[hs_(e), csl], lhsT=tokm[hs_(e), c, 2, :], rhs=tokm[hs_(e), c, 3, :], start=True, stop=False,
                            r=[tk["tokm", c // 4]], w=[tA], **kw)
                P.pe.matmul(bA[hs_(e), csl], lhsT=tokm[hs_(e), c, 1, :], rhs=W2b[hs_(e), c, :], start=False, stop=True,
                            r=[tk["tokm", c // 4], tk["W2b"]], w=[tA], **kw)

            def psi_ev():
                for c in range(8):
                    P.act.activation(out=PsiE[:, c, :], in_=bA[:, c * 64:(c + 1) * 64], func=AF.Copy, scale=egL[:, c:c + 1],
                                     r=[tA, tk["egL"]], w=[tk["PsiE"]])
            tail_mm(psi_mm, psi_ev)
            tail_mm(lambda c, e, csl, kw: P.pe.matmul(bA[hs_(e), csl], lhsT=W1b[hs_(e), c, :], rhs=MS[hs_(e), c, 1, :], start=True, stop=True,
                                                      r=[tk["W1b"], tk["MS", c]], w=[tA], **kw),
                    lambda: P.dve.tensor_tensor(out=RG[:].rearrange("p c i -> p (c i)"), in0=bA[:, :], in1=AR[:, 1, :], op=ALU.add,
                                                r=[tA, tk["AR"]], w=[tk["RG"]]))
            if g < 4:
                seqs_c = [list(range(8)) if z == 0 else list(range(7, -1, -1))]
                first_of_seq = (g == groups[0])
            else:
                seqs_c = [[0, 1, 2, 3], [4, 5, 6, 7]] if z == 0 else [[3, 2, 1, 0], [7, 6, 5, 4]]
                first_of_seq = True
            TT = (Tst, Tst2)
            for si_, corder in enumerate(seqs_c):
                if first_of_seq:
                    cur = 0
                    if g < 4:
                        P.q_sp.dma_start(out=S0p[:], in_=E.rS0[z, 2 * hp:2 * hp + 2].rearrange("e v k -> (e v) k"), w=[tk["S0p"]])
                        for e in range(2):
                            kw = {} if e == 0 else {"tile_position": (64, 64)}
                            P.pe.matmul(bA[hs_(e), 0:64], lhsT=S0p[hs_(e), :], rhs=idf[hs_(e), hs_(e)], start=True, stop=True,
                                        r=[tk["S0p"], tk["idf"]], w=[tA], **kw)
                        P.dve.tensor_copy(out=TT[0][:], in_=bA[:, 0:64], r=[tA], w=[tk["Tst", 0]])
                    else:
                        P.dve.memset(TT[0][:], 0.0, w=[tk["Tst", 0]])
                else:
                    cur = B_cur[0]
                for c in corder:
                    Tc, Tn = TT[cur], TT[1 - cur]
                    P.act.copy(out=Tbs[:, c, :], in_=Tc[:], r=[tk["Tst", cur]], w=[tk["Tbs", c]])
                    for e in range(2):
                        kw = {} if e == 0 else {"tile_position": (64, 64)}
                        P.pe.matmul(bA[hs_(e), 0:64], lhsT=IPhiT[hs_(e), c, :], rhs=Tc[hs_(e), :], start=True, stop=True,
                                    r=[tk["IPhiT"], tk["Tst", cur]], w=[tA], **kw)
                    P.dve.scalar_tensor_tensor(out=Tn[:], in0=bA[:, 0:64], scalar=egL[:, c:c + 1], in1=PsiE[:, c, :], op0=ALU.mult, op1=ALU.add,
                                               r=[tA, tk["egL"], tk["PsiE"]], w=[tk["Tst", 1 - cur]])
                    cur = 1 - cur
                B_cur[0] = cur
                if g == 4:
                    for e in range(2):
                        kw = {} if e == 0 else {"tile_position": (64, 64)}
                        P.pe.matmul(bA[hs_(e), 64:128], lhsT=TT[cur][hs_(e), :], rhs=idf[hs_(e), hs_(e)], start=True, stop=True,
                                    r=[tk["Tst", cur], tk["idf"]], w=[tA], **kw)
                    P.act.copy(out=So[:], in_=bA[:, 64:128], r=[tA], w=[tk["So"]])
                    P.q_sp.dma_start(out=E.oS[si_, z, 2 * hp:2 * hp + 2].rearrange("e v k -> (e v) k"), in_=So[:], r=[tk["So"]], w=[tk["out_oS"]])
            for c in range(8):
                csl = slice(c * 64, (c + 1) * 64)
                for e in range(2):
                    kw = {} if e == 0 else {"tile_position": (64, 64)}
                    P.pe.matmul(bA[hs_(e), csl], lhsT=Tbs[hs_(e), c, :], rhs=RG[hs_(e), c, :], start=True, stop=False,
                                r=[tk["Tbs", c], tk["RG"]], w=[tA], **kw)
                    P.pe.matmul(bA[hs_(e), csl], lhsT=W2b[hs_(e), c, :], rhs=MS[hs_(e), c, 1, :], start=False, stop=False,
                                r=[tk["W2b"], tk["MS", c]], w=[tA], **kw)
                    P.pe.matmul(bA[hs_(e), csl], lhsT=tokm[hs_(e), c, 3, :], rhs=MS[hs_(e), c, 3, :], start=False, stop=True,
                                r=[tk["tokm", c // 4], tk["MS", c]], w=[tA], **kw)
            P.dve.tensor_tensor(out=yT[:, gs], in0=yT[:, gs], in1=bA[:, :], op=ALU.add, r=[tk["yT", g], tA], w=[tk["yT", g]])

        for ih, hp in enumerate(pairs):
            ipar = (len(pairs) - 1 - ih) % 2
            rT, kT, vT, szT = inp2[ipar]
            tin = tk["inp2", ipar]
            for j in range(4):
                P.q_act.dma_start(out=inp2[ipar][j][:], in_=E.rscr[hp, j], r=[tk["rscr", hp]], w=[tin])
            for z in range(2):
                P.q_pool.dma_start(out=w2hh[z * 64:(z + 1) * 64, 0, z, :], in_=E.r_w2[z][:, hp * 128:(hp + 1) * 128], w=[tk["w2h", 0]])
                P.q_pool.dma_start(out=a2hh[z * 64:(z + 1) * 64, 0, z, :], in_=E.r_a2[z][:, hp * 128:(hp + 1) * 128], w=[tk["a2h", 0]])
            for g in range(NG):
                P.dve.memset(yT[:, g * 512:(g + 1) * 512], 0.0, w=[tk["yT", g]])
                P.dve.memset(bonT[:, g * 512:(g + 1) * 512], 0.0, w=[tk["bonT", g]])
            for z in range(2):
                groups = (0, 1, 2, 3, 4) if z == 0 else (3, 2, 1, 0, 4)
                for gi, g in enumerate(groups):
                    group_stage(hp, z, g, groups, sets[z], gi % 2, rT, kT, vT, tin, (w2hh[:, 0], tk["w2h", 0]), (a2hh[:, 0], tk["a2h", 0]))
            t1, t2 = sets[0].tmp[8], sets[0].tmp[9]
            sqb = sets[0].sqb
            if ih == len(pairs) - 1 and len(pairs) == 8:
                P.barrier()
                E.load_wout()
            for g in range(NG):
                gs = slice(g * 512, (g + 1) * 512)
                P.pe.matmul(pb[0][:, :], lhsT=bonesf[:], rhs=yT[:, gs], start=True, stop=True, r=[tk["bonesf"], tk["yT", g]], w=[tk["pb", 0]])
                P.dve.scalar_tensor_tensor(out=t1, in0=pb[0][:, :], scalar=-1.0 / 64, in1=yT[:, gs], op0=ALU.mult, op1=ALU.add,
                                           r=[tk["pb", 0], tk["yT", g]], w=[tk["z", 0, "t1"]])
                P.act.activation(out=t2, in_=t1, func=AF.Square, r=[tk["z", 0, "t1"]], w=[tk["z", 0, "t2"]])
                P.pe.matmul(pb[1][:, :], lhsT=bonesf[:], rhs=t2, start=True, stop=True, r=[tk["bonesf"], tk["z", 0, "t2"]], w=[tk["pb", 1]])
                P.dve.tensor_scalar(out=t2, in0=pb[1][:, :], scalar1=1.0 / 64, scalar2=LNX_EPS, op0=ALU.mult, op1=ALU.add,
                                    r=[tk["pb", 1]], w=[tk["z", 0, "t2"]])
                P.act.activation(out=t2, in_=t2, func=AF.Ln, r=[tk["z", 0, "t2"]], w=[tk["z", 0, "t2"]])
                P.act.activation(out=t2, in_=t2, func=AF.Exp, scale=-0.5, r=[tk["z", 0, "t2"]], w=[tk["z", 0, "t2"]])
                P.dve.tensor_tensor(out=t1, in0=t1, in1=t2, op=ALU.mult, r=[tk["z", 0, "t1"], tk["z", 0, "t2"]], w=[tk["z", 0, "t1"]])
                P.dve.tensor_scalar(out=t1, in0=t1, scalar1=lg_sb[:, hp:hp + 1], scalar2=lb_sb[:, hp:hp + 1], op0=ALU.mult, op1=ALU.add,
                                    r=[tk["z", 0, "t1"], tk["lgp"], tk["lbp"]], w=[tk["z", 0, "t1"]])
                P.dve.tensor_tensor(out=t2, in0=bonT[:, gs], in1=vT[:, gs], op=ALU.mult, r=[tk["bonT", g], tin], w=[tk["z", 0, "t2"]])
                P.dve.tensor_tensor(out=t1, in0=t1, in1=t2, op=ALU.add, r=[tk["z", 0, "t1"], tk["z", 0, "t2"]], w=[tk["z", 0, "t1"]])
                P.dve.tensor_tensor(out=sqb[:], in0=t1, in1=szT[:, gs], op=ALU.mult, r=[tk["z", 0, "t1"], tin], w=[tk["z", 0, "sqb"]])
                P.q_act.dma_start(out=E.catT[g, :, 8 + hp, :], in_=sqb[:], r=[tk["z", 0, "sqb"]], w=[tk["catT", g]])
    P.barrier()


def build(stage=99):
    nc = bass.Bass("TRN2", target_bir_lowering=False)
    P = Prog()
    tk = TK()

    def din(name, shape, dt=F32):
        return nc.dram_tensor(name, list(shape), dt, kind="ExternalInput").ap()

    def dout(name, shape, dt=F32):
        return nc.dram_tensor(name, list(shape), dt, kind="ExternalOutput").ap()

    x = din("x", [T, D])
    cv = din("cv", [128, KC, 2])
    w_ada = din("w_ada", [D, 3 * D])
    b_ada = din("b_ada", [128, 48])
    norm_g = din("norm_g", [128, KC])
    final_g = din("final_g", [1, D])
    w_in = din("w_in", [D, IN_COLS])
    w_out = din("w_out", [D, D])
    ident = din("ident", [128, 128])
    E = NS()
    E.w_in = w_in
    E.m_gate_b = din("m_gate_b", [1, 16])
    E.m_conv_w = din("m_conv_w", [128, KC, 3])
    E.m_conv_b = din("m_conv_b", [128, KC])
    E.m_ln_g = din("m_ln_g", [1, 1024])
    E.m0 = din("m0", [36, 1])
    E.mC0 = din("mC0", [2, 4, 256, 256])
    E.mn0 = din("mn0", [2, 4, 256])
    E.sel_d = din("sel", [36, 8, 128])
    E.maskbig_d = din("maskbig", [128, 2, 128])
    E.bones_d = din("bones", [128, 128])
    E.mk_d = din("mk", [128, 2, 5, 64])
    E.rm_d = din("rm", [128, 513])
    E.gmask_d = din("gmask", [128, 4])
    E.idp_d = din("idp", [128, 8, 64])
    E.r_mu = din("r_mu", [128, 26])
    E.r_w0 = din("r_w0", [128, 2, 8])
    E.r_a0 = din("r_a0", [128, 2, 8])
    E.r_k_k = din("r_k_k", [128, 8])
    E.r_k_a = din("r_k_a", [128, 8])
    E.r_r_k = din("r_r_k", [128, 8])
    E.r_ln_g = din("r_ln_g", [128, 8])
    E.r_ln_b = din("r_ln_b", [128, 8])
    E.r_w2 = din("r_w2", [2, 64, 1024])
    E.r_a2 = din("r_a2", [2, 64, 1024])
    E.rS0 = din("rS0", [2, 16, 64, 64])
    y = dout("y", [T, D])
    E.oS = dout("oS", [2, 2, 16, 64, 64])
    E.oC = dout("oC", [2, 2, 4, 256, 256])
    E.on = dout("on", [2, 2, 4, 256])
    E.om = dout("om", [36, 512])
    if "catT" in DEBUG:
        catT = dout("catT", [NG, 128, KC, 512], BF16)
    else:
        catT = nc.dram_tensor("catT", [NG, 128, KC, 512], BF16).ap()
    E.catT = catT
    E.rscr = nc.dram_tensor("rscr", [8, 4, 128, T], BF16).ap()

    st = contextlib.ExitStack()
    with st:
        def sb(name, shape, dt=F32):
            return st.enter_context(nc.sbuf_tensor(name, list(shape), dt))

        def ps(name, shape, dt=F32):
            return st.enter_context(nc.psum_tensor(name, list(shape), dt))

        hT = sb("hT", [128, KC, T], BF16)
        idf = sb("idf", [128, 128])
        idb = sb("idb", [128, 128], BF16)
        modv = sb("modv", [128, 48, 2])
        Amod = sb("Amod", [128, KC, 2])
        ng_sb = sb("ng_sb", [128, KC])
        bada_sb = sb("bada_sb", [128, 48])
        cv_sb = sb("cv_sb", [128, KC, 2])
        sT = sb("sT", [128, KC, 2], BF16)
        pb = [ps(f"pb{i}", [128, 512]) for i in range(6)]
        ptb = [ps(f"ptb{i}", [128, 1024], BF16) for i in range(2)]

        P.q_sp.dma_start(out=idf[:], in_=ident[:, :], w=[tk["idf"]])
        P.dve.tensor_copy(out=idb[:], in_=idf[:], r=[tk["idf"]], w=[tk["idb"]])
        P.q_sp.dma_start(out=cv_sb[:], in_=cv[:, :, :], w=[tk["cv"]])
        P.q_sp.dma_start(out=ng_sb[:], in_=norm_g[:, :], w=[tk["ng"]])
        P.q_sp.dma_start(out=bada_sb[:], in_=b_ada[:, :], w=[tk["bada"]])
        P.act.activation(out=sT[:], in_=cv_sb[:], func=AF.Silu, r=[tk["cv"]], w=[tk["sT"]])

        st0 = contextlib.ExitStack()
        if True:
            wab = [st0.enter_context(nc.sbuf_tensor(f"wab{i}", [128, 3 * D], BF16)) for i in range(2)]
            wv = w_ada.rearrange("(c p) n -> p c n", p=128)
            pmod = pb[0][:, 0:96].rearrange("p (f j) -> p f j", j=2)
            for kc in range(KC):
                b = wab[kc % 2]
                P.q_pool.dma_start(out=b[:], in_=wv[:, kc, :], w=[tk["wab", kc % 2]])
                for fb in range(48):
                    P.pe.matmul(pmod[:, fb, :], lhsT=b[:, fb * 128:(fb + 1) * 128], rhs=sT[:, kc, :],
                                start=(kc == 0 and fb == 0), stop=(kc == KC - 1 and fb == 47), skip_group_check=True,
                                r=[tk["wab", kc % 2], tk["sT"]], w=[tk["pb", 0]])
            for j in range(2):
                P.dve.tensor_tensor(out=modv[:, :, j], in0=pmod[:, :, j], in1=bada_sb[:], op=ALU.add,
                                    r=[tk["pb", 0], tk["bada"]], w=[tk["modv"]])
            for j in range(2):
                P.dve.scalar_tensor_tensor(out=Amod[:, :, j], in0=modv[:, 16:32, j], scalar=1.0, in1=ng_sb[:],
                                           op0=ALU.add, op1=ALU.mult, r=[tk["modv"], tk["ng"]], w=[tk["Amod"]])
        st1 = st0
        if True:
            xt = [st1.enter_context(nc.sbuf_tensor(f"xt{i}", [128, D], F32)) for i in range(4)]
            xn = [st1.enter_context(nc.sbuf_tensor(f"xn{i}", [128, D], BF16)) for i in range(12)]
            junk = st1.enter_context(nc.sbuf_tensor("junk", [128, D], BF16))
            ssq = st1.enter_context(nc.sbuf_tensor("ssq", [128, NT], F32))
            rstd = st1.enter_context(nc.sbuf_tensor("rstd", [128, NT], F32))
            ev = 0
            for g in range(NG):
                j = 0 if g < 4 else 1
                for t4 in range(4):
                    tt = g * 4 + t4
                    xb_ = xt[tt % 4]
                    (P.q_sp if tt % 2 == 0 else P.q_act).dma_start(out=xb_[:], in_=x[tt * 128:(tt + 1) * 128, :],
                                                                   w=[tk["xt", tt % 4]])
                    P.act.activation(out=junk[:], in_=xb_[:], func=AF.Square, accum_out=ssq[:, tt:tt + 1],
                                     r=[tk["xt", tt % 4]], w=[tk["junk"], tk["ssq", tt]])
                    P.dve.tensor_scalar(out=rstd[:, tt:tt + 1], in0=ssq[:, tt:tt + 1], scalar1=1.0 / D, scalar2=EPS,
                                        op0=ALU.mult, op1=ALU.add, r=[tk["ssq", tt]], w=[tk["rstd", tt]])
                    P.act.sqrt(out=rstd[:, tt:tt + 1], in_=rstd[:, tt:tt + 1], r=[tk["rstd", tt]], w=[tk["rstd", tt]])
                    P.dve.reciprocal(out=rstd[:, tt:tt + 1], in_=rstd[:, tt:tt + 1], r=[tk["rstd", tt]], w=[tk["rstd", tt]])
                    P.act.activation(out=xn[tt % 12][:], in_=xb_[:], func=AF.Copy, scale=rstd[:, tt:tt + 1],
                                     r=[tk["xt", tt % 4], tk["rstd", tt]], w=[tk["xn", tt % 12]])
                for fc in range(KC):
                    pt = ptb[fc % 2]
                    for t4 in range(4):
                        P.pe.transpose(out=pt[:, t4 * 128:(t4 + 1) * 128], in_=xn[(g * 4 + t4) % 12][:, fc * 128:(fc + 1) * 128],
                                       identity=idb[:], r=[tk["xn", (g * 4 + t4) % 12], tk["idb"]], w=[tk["ptb", fc % 2]])
                    dst = hT[:, fc, g * 512:(g + 1) * 512]
                    if ev % 2 == 0:
                        P.act.activation(out=dst, in_=pt[:, 0:512], func=AF.Identity, scale=Amod[:, fc, j:j + 1],
                                         bias=modv[:, fc, j:j + 1], r=[tk["ptb", fc % 2], tk["Amod"], tk["modv"]],
                                         w=[tk["hT", g]])
                    else:
                        P.dve.tensor_scalar(out=dst, in0=pt[:, 0:512], scalar1=Amod[:, fc, j:j + 1],
                                            scalar2=modv[:, fc, j:j + 1], op0=ALU.mult, op1=ALU.add,
                                            r=[tk["ptb", fc % 2], tk["Amod"], tk["modv"]], w=[tk["hT", g]])
                    ev += 1

        if "hT" in DEBUG:
            dbg2 = dout("dbg_mod", [128, 48, 2])
            P.q_sp.dma_start(out=dbg2[:, :, :], in_=modv[:], r=[tk["modv"]], w=[tk["dbgout"]])
            dbg3 = dout("dbg_A", [128, KC, 2])
            P.q_sp.dma_start(out=dbg3[:, :, :], in_=Amod[:], r=[tk["Amod"]], w=[tk["dbgout"]])

            dbg = dout("dbg_hT", [128, KC, T], BF16)
            P.q_sp.dma_start(out=dbg[:, :, :], in_=hT[:], r=[tk["hT", g] for g in range(NG)], w=[tk["dbgout"]])

        P.barrier()
        st0.close()
        E.hT, E.idf, E.idb, E.pb, E.ptb = hT, idf, idb, pb, ptb
        wout_loaded = [False]

        def load_wout():
            wo_ = hT[:, :, 0:D]
            wov = w_out.rearrange("(c p) n -> p c n", p=128)
            for q4 in range(4):
                P.q_pool.dma_start(out=wo_[:, q4 * 4:(q4 + 1) * 4, :], in_=wov[:, q4 * 4:(q4 + 1) * 4, :],
                                   r=[], w=[tk["hT", g] for g in range(NG)])
            wout_loaded[0] = True
        E.load_wout = load_wout
        if stage >= 3:
            phase_mlstm(nc, P, tk, E, heads=DEBUG.get("heads", (0, 1, 2, 3)))
        if stage >= 4:
            phase_rwkv(nc, P, tk, E, pairs=DEBUG.get("pairs", range(8)))
        with contextlib.ExitStack() as st4:
            onesf = st4.enter_context(nc.sbuf_tensor("onesf", [128, 128], F32))
            P.dve.memset(onesf[:], 1.0, w=[tk["onesf"]])
            gate_bc = st4.enter_context(nc.sbuf_tensor("gate_bc", [128, 2, D], F32))
            fg_bc = st4.enter_context(nc.sbuf_tensor("fg_bc", [128, D], F32))
            P.q_act.dma_start(out=fg_bc[:], in_=final_g.partition_broadcast(128), w=[tk["fg"]])
            dg = [st4.enter_context(nc.sbuf_tensor(f"dg{i}", [128, 128], F32)) for i in range(2)]
            k = 0
            for j in range(2):
                for q4 in range(4):
                    bank = pb[q4 % 2]
                    for c4 in range(4):
                        ch = q4 * 4 + c4
                        d_ = dg[k % 2]
                        P.dve.tensor_scalar_mul(out=d_[:], in0=idf[:], scalar1=modv[:, 32 + ch, j:j + 1],
                                                r=[tk["idf"], tk["modv"]], w=[tk["dg", k % 2]])
                        P.pe.matmul(bank[:, c4 * 128:(c4 + 1) * 128], lhsT=onesf[:], rhs=d_[:], start=True, stop=True,
                                    r=[tk["dg", k % 2], tk["onesf"]], w=[tk["pb", q4 % 2]])
                        k += 1
                    P.act.copy(out=gate_bc[:, j, q4 * 512:(q4 + 1) * 512], in_=bank[:, :],
                               r=[tk["pb", q4 % 2]], w=[tk["gate_bc"]])
            wo = hT[:, :, 0:D]
            hT_all = [tk["hT", g] for g in range(NG)]
            if not wout_loaded[0]:
                E.load_wout()
            cg = [st4.enter_context(nc.sbuf_tensor(f"cg{i}", [128, KC, 512], BF16)) for i in range(2)]
            xt = [st4.enter_context(nc.sbuf_tensor(f"x4_{i}", [128, D], F32)) for i in range(2)]
            xo = st4.enter_context(nc.sbuf_tensor("xo", [128, D], F32))
            yo = [st4.enter_context(nc.sbuf_tensor(f"yo{i}", [128, D], F32)) for i in range(2)]
            junk4 = st4.enter_context(nc.sbuf_tensor("junk4", [128, D], BF16))
            ss4 = st4.enter_context(nc.sbuf_tensor("ss4", [128, NT], F32))
            for g in range(NG):
                j = 0 if g < 4 else 1
                cb = cg[g % 2]
                if stage >= 2:
                    P.q_sp.dma_start(out=cb[:], in_=catT[g], r=[tk["catT", g]], w=[tk["cg", g % 2]])
                for t4 in range(4):
                    tt = g * 4 + t4
                    xb_ = xt[tt % 2]
                    P.q_act.dma_start(out=xb_[:], in_=x[tt * 128:(tt + 1) * 128, :], w=[tk["x4", tt % 2]])
                    if stage >= 2:
                        for db in range(4):
                            bank = pb[2 + db]
                            for cc in range(KC):
                                P.pe.matmul(bank[:, :], lhsT=cb[:, cc, t4 * 128:(t4 + 1) * 128],
                                            rhs=wo[:, cc, db * 512:(db + 1) * 512], start=(cc == 0), stop=(cc == KC - 1),
                                            r=[tk["cg", g % 2]] + hT_all, w=[tk["pb", 2 + db]])
                            P.dve.tensor_tensor(out=xo[:, db * 512:(db + 1) * 512], in0=bank[:, :],
                                                in1=gate_bc[:, j, db * 512:(db + 1) * 512], op=ALU.mult,
                                                r=[tk["pb", 2 + db], tk["gate_bc"]], w=[tk["xo"]])
                        P.dve.tensor_tensor(out=xo[:], in0=xo[:], in1=xb_[:], op=ALU.add,
                                            r=[tk["xo"], tk["x4", tt % 2]], w=[tk["xo"]])
                        src = xo
                        srct = tk["xo"]
                    else:
                        src = xb_
                        srct = tk["x4", tt % 2]
                    P.act.activation(out=junk4[:], in_=src[:], func=AF.Square, accum_out=ss4[:, tt:tt + 1],
                                     r=[srct], w=[tk["junk4"], tk["ss4", tt]])
                    P.dve.tensor_scalar(out=ss4[:, tt:tt + 1], in0=ss4[:, tt:tt + 1], scalar1=1.0 / D, scalar2=EPS,
                                        op0=ALU.mult, op1=ALU.add, r=[tk["ss4", tt]], w=[tk["ss4", tt]])
                    P.act.sqrt(out=ss4[:, tt:tt + 1], in_=ss4[:, tt:tt + 1], r=[tk["ss4", tt]], w=[tk["ss4", tt]])
                    P.dve.reciprocal(out=ss4[:, tt:tt + 1], in_=ss4[:, tt:tt + 1], r=[tk["ss4", tt]], w=[tk["ss4", tt]])
                    yb = yo[tt % 2]
                    P.dve.scalar_tensor_tensor(out=yb[:], in0=src[:], scalar=ss4[:, tt:tt + 1], in1=fg_bc[:],
                                               op0=ALU.mult, op1=ALU.mult, r=[srct, tk["ss4", tt], tk["fg"]],
                                               w=[tk["yo", tt % 2]])
                    P.q_sp.dma_start(out=y[tt * 128:(tt + 1) * 128, :], in_=yb[:], r=[tk["yo", tt % 2]], w=[tk["y"]])

        finals = [tk["y"], tk["out_om"], tk["out_oC"], tk["out_oS"]] + [tk["catT", g] for g in range(NG)]
        if "dbgout" in tk.d:
            finals.append(tk["dbgout"])
        P.emit(nc, st, final_waits=finals)
    return nc, P


def silu_layout(v):
    return np.ascontiguousarray(v.reshape(16, 128).T)


def consts():
    sel = np.zeros((36, 8, 128), np.float32)
    for ri, q in enumerate(ROWS):
        sel[q, ri, :] = 1.0
    si, ti = np.meshgrid(np.arange(128), np.arange(128), indexing="ij")
    maskbig = np.zeros((128, 2, 128), np.float32)
    maskbig[:, 0, :] = np.where(si <= ti, 0.0, 1e30)
    maskbig[:, 1, :] = np.where(si >= ti, 0.0, 1e30)
    bones = np.zeros((128, 128), np.float32)
    bones[:64, :64] = 1.0
    bones[64:, 64:] = 1.0
    r64, c64 = np.meshgrid(np.arange(64), np.arange(64), indexing="ij")
    mk = np.zeros((128, 2, 5, 64), np.float32)
    for half in range(2):
        hs = slice(64 * half, 64 * half + 64)
        mk[hs, 0, 0] = r64 < c64
        mk[hs, 0, 1] = r64 <= c64
        mk[hs, 0, 2] = r64 < c64
        mk[hs, 0, 3] = r64 <= c64
        mk[hs, 0, 4] = c64 < r64
        mk[hs, 1, 0] = r64 > c64
        mk[hs, 1, 1] = r64 >= c64
        mk[hs, 1, 2] = r64 > c64
        mk[hs, 1, 3] = r64 >= c64
        mk[hs, 1, 4] = c64 > r64
    rm = np.ones((128, 513), np.float32)
    rm[:, 0::64] = 0.0
    gmask = np.zeros((128, 4), np.float32)
    gmask[np.arange(128), np.arange(128) % 4] = 1.0
    idp = np.zeros((128, 8, 64), np.float32)
    idp[np.arange(128), :, np.arange(128) % 64] = 1.0
    return {"ident": np.eye(128, dtype=np.float32), "sel": sel, "maskbig": maskbig, "bones": bones, "mk": mk, "rm": rm,
            "gmask": gmask, "idp": idp}


def fm(v, nblk):
    return np.ascontiguousarray(v.reshape(nblk, 128).T)


def make_in_maps(inp, cores=range(8)):
    f = lambda a: np.asarray(a, dtype=np.float32)
    x_prompt, x_sample = f(inp["x_prompt"]), f(inp["x_sample"])
    c, c_ctx = f(inp["c"]), f(inp["c_ctx"])
    cst = consts()
    shared = {
        "w_ada": f(inp["w_ada"])[0],
        "b_ada": fm(f(inp["b_ada"])[0], 48),
        "norm_g": fm(f(inp["norm_g"])[0], 16),
        "final_g": f(inp["final_g"]).reshape(1, D),
        "w_in": f(inp["w_in"])[0],
        "w_out": f(inp["w_out"])[0],
        "m_gate_b": f(inp["m_gate_b"])[0].reshape(1, 16),
        "m_conv_w": np.ascontiguousarray(f(inp["m_conv_w"])[0].reshape(3, 16, 128).transpose(2, 1, 0)),
        "m_conv_b": fm(f(inp["m_conv_b"])[0], 16),
        "m_ln_g": f(inp["m_ln_g"])[0].reshape(1, 1024),
        "r_mu": fm(f(inp["r_mu"])[0], 26),
        "r_w0": np.ascontiguousarray(f(inp["r_w0"])[0].reshape(2, 8, 128).transpose(2, 0, 1)),
        "r_a0": np.ascontiguousarray(f(inp["r_a0"])[0].reshape(2, 8, 128).transpose(2, 0, 1)),
        "r_k_k": fm(f(inp["r_k_k"])[0], 8),
        "r_k_a": fm(f(inp["r_k_a"])[0], 8),
        "r_r_k": fm(f(inp["r_r_k"])[0].reshape(-1), 8),
        "r_ln_g": fm(f(inp["r_ln_g"])[0], 8),
        "r_ln_b": fm(f(inp["r_ln_b"])[0], 8),
        "r_w2": f(inp["r_w2"])[0],
        "r_a2": f(inp["r_a2"])[0],
    }
    shared.update(cst)
    sm = f(inp["state_mlstm_m"])
    in_maps = []
    for b in cores:
        xc = np.concatenate([x_sample[b], x_prompt[2 * b], x_prompt[2 * b + 1]], axis=0)
        cvv = np.stack([fm(c[b], 16), fm(c_ctx, 16)], axis=-1)
        m0 = np.zeros((36, 1), np.float32)
        m0[0:4, 0] = sm[b, 0, 0]
        m0[32:36, 0] = sm[b, 0, 1]
        d = dict(shared)
        d.update({
            "x": np.ascontiguousarray(xc),
            "cv": np.ascontiguousarray(cvv),
            "m0": m0,
            "mC0": np.ascontiguousarray(f(inp["state_mlstm_C"])[b, 0]),
            "mn0": np.ascontiguousarray(f(inp["state_mlstm_n"])[b, 0]),
            "rS0": np.ascontiguousarray(f(inp["state_rwkv_S"])[b, 0]),
        })
        in_maps.append(d)
    return in_maps


def kernel(**inp):
    nc, P = build()
    in_maps = make_in_maps(inp)
    res = run_bass_kernel_spmd(nc, in_maps, core_ids=list(range(8)))
    R = res.results
    y_prompt = np.zeros((16, 256, D), np.float32)
    y_sample = np.zeros((8, 2048, D), np.float32)
    nC = np.zeros((16, 1, 2, 4, 256, 256), np.float32)
    nn = np.zeros((16, 1, 2, 4, 256), np.float32)
    nm = np.zeros((16, 1, 2, 4), np.float32)
    nS = np.zeros((16, 1, 2, 16, 64, 64), np.float32)
    for b in range(8):
        yy = R[b]["y"]
        y_sample[b] = yy[0:2048]
        y_prompt[2 * b] = yy[2048:2304]
        y_prompt[2 * b + 1] = yy[2304:2560]
        for pi in range(2):
            nC[2 * b + pi, 0] = R[b]["oC"][pi]
            nn[2 * b + pi, 0] = R[b]["on"][pi]
            nS[2 * b + pi, 0] = R[b]["oS"][pi]
            om = R[b]["om"]
            nm[2 * b + pi, 0, 0] = om[0:4, pi * 256 + 255]
            nm[2 * b + pi, 0, 1] = om[32:36, pi * 256]
    return y_prompt, y_sample, nC, nn, nm, nS
```
